# Optimizing a Trainium2 kernel written in Bass

```python
import math
import jax, jax.numpy as jnp
from jax import lax
import numpy as np

D_MODEL = 2048
BATCH = 1
SEQ = 16384
DEPTH = 1
DEC_BATCH = 8
DEC_SEQ = 2048
PAST_LEN = 128

HEAD_DIM = 128
N_ATTN_HEADS = 8
ATTN_WIDTH = N_ATTN_HEADS * HEAD_DIM
LRU_WIDTH = 1024
N_LRU_BLOCKS = 8
LRU_BLOCK = LRU_WIDTH // N_LRU_BLOCKS
MIX_WIDTH = ATTN_WIDTH + LRU_WIDTH
IN_WIDTH = 3 * ATTN_WIDTH + 2 * LRU_WIDTH
D_FF = 3 * D_MODEL
DILATED_BRANCHES = ((128, 1), (512, 4), (2048, 16))
ROPE_THETA = 500000.0
ROPE_DIM = HEAD_DIM // 4
LRU_C = 8.0
LRU_CONV_WIDTH = 4
FF_CONV_WIDTH = 3
NORM_EPS = 1e-6
NEG_INF = -1e30

kernel_name = "hymba_dilated_attn_rglru_convffn_encoder"


def rmsnorm(x, g):
    xf = x.astype(jnp.float32)
    y = xf * lax.rsqrt(jnp.mean(xf * xf, axis=-1, keepdims=True) + NORM_EPS)
    return (y * g.astype(jnp.float32)).astype(x.dtype)


def partial_rope(t, pos):
    half = ROPE_DIM // 2
    inv = ROPE_THETA ** (-jnp.arange(half, dtype=jnp.float32) / half)
    ang = pos.astype(jnp.float32)[:, None] * inv[None, :]
    cos = jnp.cos(ang)[None, :, None, :]
    sin = jnp.sin(ang)[None, :, None, :]
    tf = t.astype(jnp.float32)
    t1 = tf[..., :half]
    t2 = tf[..., half:ROPE_DIM]
    out = jnp.concatenate([t1 * cos - t2 * sin, t1 * sin + t2 * cos, tf[..., ROPE_DIM:]], axis=-1)
    return out.astype(t.dtype)


def centred_dwconv(x, w, b, left):
    K = w.shape[0]
    S = x.shape[1]
    xp = jnp.pad(x, ((0, 0), (left, K - 1 - left), (0, 0)))
    return sum((xp[:, j:j + S] * w[j] for j in range(K)), b)


def block_neighbours(t, axis):
    n = t.shape[axis]
    zero = jnp.zeros_like(lax.slice_in_dim(t, 0, 1, axis=axis))
    prev = jnp.concatenate([zero, lax.slice_in_dim(t, 0, n - 1, axis=axis)], axis=axis)
    nxt = jnp.concatenate([lax.slice_in_dim(t, 1, n, axis=axis), zero], axis=axis)
    return jnp.concatenate([prev, t, nxt], axis=axis + 1)


def dilated_branch(q, k, v, window, dil):
    B, S, H, Dh = q.shape
    nk = window // 2 // dil
    unit = dil * nk
    S_pad = -(-S // unit) * unit
    L = S_pad // dil
    nb = L // nk
    pad = ((0, 0), (0, S_pad - S), (0, 0), (0, 0))

    def to_sub(t):
        t = jnp.pad(t, pad).reshape(B, L, dil, H, Dh).transpose(0, 2, 1, 3, 4)
        return t.reshape(B, dil, nb, nk, H, Dh)

    qs = to_sub(q)
    kw = block_neighbours(to_sub(k), 2)
    vw = block_neighbours(to_sub(v), 2)
    valid = (jnp.arange(S_pad) < S).reshape(L, dil).T.reshape(dil, nb, nk)
    kvalid = block_neighbours(valid, 1)
    rel = jnp.arange(3 * nk)[None, :] - nk - jnp.arange(nk)[:, None]
    mask = (jnp.abs(rel) <= nk)[None, None] & kvalid[:, :, None, :]

    s = jnp.einsum('brnqhd,brnkhd->brnhqk', qs, kw, preferred_element_type=jnp.float32) * (Dh ** -0.5)
    s = jnp.where(mask[None, :, :, None], s, NEG_INF)
    m = jnp.max(s, axis=-1)
    p = jnp.exp(s - m[..., None])
    den = jnp.sum(p, axis=-1)
    num = jnp.einsum('brnhqk,brnkhd->brnqhd', p, vw.astype(jnp.float32))

    def from_sub(t):
        X = t.shape[-1]
        t = t.reshape(B, dil, L, H, X).transpose(0, 2, 1, 3, 4).reshape(B, S_pad, H, X)
        return t[:, :S]

    m = from_sub(jnp.swapaxes(m, -1, -2)[..., None])
    den = from_sub(jnp.swapaxes(den, -1, -2)[..., None])
    return from_sub(num), den, m


def dilated_attention(q, k, v):
    outs = [dilated_branch(q, k, v, w, d) for (w, d) in DILATED_BRANCHES]
    nums = jnp.stack([o[0] for o in outs])
    dens = jnp.stack([o[1] for o in outs])
    ms = jnp.stack([o[2] for o in outs])
    scale = jnp.exp(ms - jnp.max(ms, axis=0, keepdims=True))
    return jnp.sum(scale * nums, axis=0) / jnp.sum(scale * dens, axis=0)


def rglru_mixer(xr, gate, conv_w, conv_b, w_a, b_a, w_x, b_x, lam):
    B, S, _ = xr.shape
    f32 = jnp.float32
    u = centred_dwconv(xr, conv_w, conv_b, LRU_CONV_WIDTH // 2).astype(f32)
    ub = u.reshape(B, S, N_LRU_BLOCKS, LRU_BLOCK)
    r = jax.nn.sigmoid(jnp.einsum('bsgi,zgij->zbsgj', ub, w_a.astype(f32)).reshape(2, B, S, LRU_WIDTH)
                       + b_a.astype(f32)[:, None, None, :])
    i = jax.nn.sigmoid(jnp.einsum('bsgi,zgij->zbsgj', ub, w_x.astype(f32)).reshape(2, B, S, LRU_WIDTH)
                       + b_x.astype(f32)[:, None, None, :])
    log_a = -LRU_C * jax.nn.softplus(-lam.astype(f32))[:, None, None, :] * r
    a = jnp.exp(log_a)
    inp = jnp.sqrt(-jnp.expm1(2.0 * log_a)) * (i * u[None])
    a = jnp.stack([a[0], jnp.flip(a[1], axis=1)])
    inp = jnp.stack([inp[0], jnp.flip(inp[1], axis=1)])

    def combine(e1, e2):
        a1, b1 = e1
        a2, b2 = e2
        return a1 * a2, a2 * b1 + b2

    _, h = lax.associative_scan(combine, (a, inp), axis=2)
    h = h[0] + jnp.flip(h[1], axis=1)
    return (h * jax.nn.gelu(gate.astype(f32))).astype(xr.dtype)


def conv_mlp(h, w_up, conv_w, conv_b, w_down):
    u = h @ w_up
    u = centred_dwconv(u, conv_w, conv_b, FF_CONV_WIDTH // 2)
    g, val = jnp.split(u, 2, axis=-1)
    return (jax.nn.gelu(g) * val) @ w_down


def encode(x, g_mix, w_in, w_out, g_attn_out, g_lru_out, conv_rg_w, conv_rg_b, rg_w_a, rg_b_a,
           rg_w_x, rg_b_x, rg_lam, g_mlp, w_up, conv_ff_w, conv_ff_b, w_down, g_final):
    B, S, _ = x.shape
    pos = jnp.arange(S)
    for l in range(DEPTH):
        h = rmsnorm(x, g_mix[l])
        z = h @ w_in[l]
        q, k, v, xr, gate = jnp.split(
            z, [ATTN_WIDTH, 2 * ATTN_WIDTH, 3 * ATTN_WIDTH, 3 * ATTN_WIDTH + LRU_WIDTH], axis=-1)
        q = partial_rope(q.reshape(B, S, N_ATTN_HEADS, HEAD_DIM), pos)
        k = partial_rope(k.reshape(B, S, N_ATTN_HEADS, HEAD_DIM), pos)
        v = v.reshape(B, S, N_ATTN_HEADS, HEAD_DIM)
        attn = dilated_attention(q, k, v).reshape(B, S, ATTN_WIDTH).astype(x.dtype)
        lru = rglru_mixer(xr, gate, conv_rg_w[l], conv_rg_b[l], rg_w_a[l], rg_b_a[l],
                          rg_w_x[l], rg_b_x[l], rg_lam[l])
        mixed = jnp.concatenate([rmsnorm(attn, g_attn_out[l]), rmsnorm(lru, g_lru_out[l])], axis=-1)
        x = x + mixed @ w_out[l]
        x = x + conv_mlp(rmsnorm(x, g_mlp[l]), w_up[l], conv_ff_w[l], conv_ff_b[l], w_down[l])
    return rmsnorm(x, g_final)


def setup_inputs(seed: int = 0) -> dict:
    key = jax.random.key(seed)
    ks = jax.random.split(key, 20)
    f32 = jnp.float32

    def nrm(k, shape, scale):
        return jax.random.normal(k, shape, f32) * scale

    u = jax.random.uniform(ks[12], (DEPTH, 2, LRU_WIDTH), f32, minval=0.9, maxval=0.999)
    a_base = u ** (1.0 / LRU_C)
    rg_lam = jnp.log(a_base) - jnp.log1p(-a_base)
    return {
        "x_prompt": nrm(ks[0], (BATCH, SEQ, D_MODEL), 1.0),
        "x_sample": nrm(ks[1], (DEC_BATCH, DEC_SEQ, D_MODEL), 1.0),
        "g_mix": 1.0 + nrm(ks[2], (DEPTH, D_MODEL), 0.01),
        "w_in": nrm(ks[3], (DEPTH, D_MODEL, IN_WIDTH), D_MODEL ** -0.5),
        "w_out": nrm(ks[4], (DEPTH, MIX_WIDTH, D_MODEL), MIX_WIDTH ** -0.5),
        "g_attn_out": 1.0 + nrm(ks[5], (DEPTH, ATTN_WIDTH), 0.01),
        "g_lru_out": 1.0 + nrm(ks[6], (DEPTH, LRU_WIDTH), 0.01),
        "conv_rg_w": nrm(ks[7], (DEPTH, LRU_CONV_WIDTH, LRU_WIDTH), LRU_CONV_WIDTH ** -0.5),
        "conv_rg_b": nrm(ks[8], (DEPTH, LRU_WIDTH), 0.01),
        "rg_w_a": nrm(ks[9], (DEPTH, 2, N_LRU_BLOCKS, LRU_BLOCK, LRU_BLOCK), LRU_BLOCK ** -0.5),
        "rg_b_a": nrm(ks[10], (DEPTH, 2, LRU_WIDTH), 0.01),
        "rg_w_x": nrm(ks[11], (DEPTH, 2, N_LRU_BLOCKS, LRU_BLOCK, LRU_BLOCK), LRU_BLOCK ** -0.5),
        "rg_b_x": nrm(ks[13], (DEPTH, 2, LRU_WIDTH), 0.01),
        "rg_lam": rg_lam,
        "g_mlp": 1.0 + nrm(ks[14], (DEPTH, D_MODEL), 0.01),
        "w_up": nrm(ks[15], (DEPTH, D_MODEL, 2 * D_FF), D_MODEL ** -0.5),
        "conv_ff_w": nrm(ks[16], (DEPTH, FF_CONV_WIDTH, 2 * D_FF), FF_CONV_WIDTH ** -0.5),
        "conv_ff_b": nrm(ks[17], (DEPTH, 2 * D_FF), 0.01),
        "w_down": nrm(ks[18], (DEPTH, D_FF, D_MODEL), D_FF ** -0.5),
        "g_final": 1.0 + nrm(ks[19], (D_MODEL,), 0.01),
    }


def reference(x_prompt, x_sample, g_mix, w_in, w_out, g_attn_out, g_lru_out, conv_rg_w, conv_rg_b,
              rg_w_a, rg_b_a, rg_w_x, rg_b_x, rg_lam, g_mlp, w_up, conv_ff_w, conv_ff_b, w_down, g_final):
    y_prompt = encode(x_prompt, g_mix, w_in, w_out, g_attn_out, g_lru_out, conv_rg_w, conv_rg_b,
                      rg_w_a, rg_b_a, rg_w_x, rg_b_x, rg_lam, g_mlp, w_up, conv_ff_w, conv_ff_b,
                      w_down, g_final)
    y_sample = encode(x_sample, g_mix, w_in, w_out, g_attn_out, g_lru_out, conv_rg_w, conv_rg_b,
                      rg_w_a, rg_b_a, rg_w_x, rg_b_x, rg_lam, g_mlp, w_up, conv_ff_w, conv_ff_b,
                      w_down, g_final)
    return (y_prompt, y_sample)
```

```python
import contextlib
import math
import numpy as np
import concourse.bass as bass
import concourse.mybir as mybir
from concourse.bass_utils import run_bass_kernel_spmd

F32 = mybir.dt.float32
BF16 = mybir.dt.bfloat16
AF = mybir.ActivationFunctionType
ALU = mybir.AluOpType
AX = mybir.AxisListType

D = 2048
NCORE = 8
SEQ_P = 16384
CH = 2048
EPS = 1e-6
NR = 16
NRQ = {"sp": 16, "pool": 4}


class Buf:
    __slots__ = ("w", "r", "excl")

    def __init__(self, excl=False):
        self.w = None
        self.r = {}
        self.excl = excl


class Sched:
    def __init__(self, nc, es):
        self.nc = nc
        self.E = {"pe": nc.tensor, "act": nc.scalar, "dve": nc.vector, "pool": nc.gpsimd, "sp": nc.sync}
        self.cs = {k: es.enter_context(nc.semaphore("c_" + k)) for k in ("pe", "act", "dve", "pool")}
        self.cc = {k: 0 for k in self.cs}
        self.ring = {q: [es.enter_context(nc.semaphore("d_%s%d" % (q, i))) for i in range(NRQ[q])] for q in ("sp", "pool")}
        self.rt = {q: [0] * NRQ[q] for q in self.ring}
        self.rk = {q: 0 for q in self.ring}
        self.waited = {k: {} for k in self.E}
        self.pr = []
        self.pw = []

    def _wait(self, e, sem, val):
        k = id(sem)
        if self.waited[e].get(k, 0) < val:
            self.E[e].wait_ge(sem, val)
            self.waited[e][k] = val

    def I(self, e, fn, r=(), w=(), dma=False, inc=True):
        deps = []
        for b in r:
            if b.w is not None:
                deps.append(b.w)
            if b.excl:
                deps.extend(b.r.values())
        for b in w:
            if b.w is not None:
                deps.append(b.w)
            deps.extend(b.r.values())
        for sem, val in deps:
            if e == "pe" and sem is self.cs["pe"]:
                continue
            self._wait(e, sem, val)
        eng = self.E[e]
        if dma:
            i = self.rk[e] % NRQ[e]
            self.rk[e] += 1
            sem = self.ring[e][i]
            if self.rt[e][i] > 0:
                self._wait(e, sem, self.rt[e][i])
            inst = fn(eng)
            self.rt[e][i] += 16
            inst.then_inc(sem, 16)
            t = (sem, self.rt[e][i])
        else:
            inst = fn(eng)
            if not inc:
                self.pr += list(r)
                self.pw += list(w)
                return
            self.cc[e] += 1
            inst.then_inc(self.cs[e], 1)
            t = (self.cs[e], self.cc[e])
            if e == "pe":
                r = list(r) + self.pr
                w = list(w) + self.pw
                self.pr = []
                self.pw = []
        for b in r:
            b.r[id(t[0])] = t
        for b in w:
            b.w = t
            b.r = {}

    def barrier(self):
        for e in self.E:
            for k in self.cs:
                if k != e and self.cc[k] > 0:
                    self._wait(e, self.cs[k], self.cc[k])
            for q in self.ring:
                for i in range(NRQ[q]):
                    if self.rt[q][i] > 0:
                        self._wait(e, self.ring[q][i], self.rt[q][i])

    def finish(self):
        for q in self.ring:
            for i in range(NRQ[q]):
                if self.rt[q][i] > 0:
                    self._wait("sp", self.ring[q][i], self.rt[q][i])
        for k in self.cs:
            if self.cc[k] > 0:
                self._wait("sp", self.cs[k], self.cc[k])


class T:
    def __init__(self, ap, excl=False):
        self.t = ap
        self.b = Buf(excl)


DILS = (1, 4, 16)
DEBUG = {"stop": 99, "scratch_out": False, "outs": (), "maxsteps": 999, "sub": 9}


class _Stop(Exception):
    pass


def build(prompt_on=True, prepass_on=True):
    nc = bass.Bass("TRN2", target_bir_lowering=False)
    es = contextlib.ExitStack()
    S = Sched(nc, es)
    I = S.I

    def din(name, shape, dt=F32):
        return nc.dram_tensor(name, list(shape), dt, kind="ExternalInput").ap()

    def dscr(name, shape, dt):
        return nc.dram_tensor(name, list(shape), dt, kind=("ExternalOutput" if name in DEBUG["outs"] else "Internal")).ap()

    def stage(n):
        if DEBUG["stop"] <= n:
            raise _Stop()

    xs_d = din("xs", [CH, D])
    xp_d = din("xp", [4352, D])
    xf_d = din("xf", [SEQ_P if prepass_on and prompt_on else 128, D])
    w_in_d = din("w_in", [D, 5120])
    w_out_d = din("w_out", [D, D])
    w_up_d = din("w_up", [D, 12288])
    w_dn_d = din("w_down", [6144, D])
    g_mix_d = din("g_mix", [128, 16])
    g_mlp_d = din("g_mlp", [128, 16])
    g_mo_d = din("g_mo", [128, 16])
    crw_d = din("crw", [128, 8, 4])
    crb_d = din("crb", [128, 8])
    wa_d = din("rg_w_a", [2, 8, 128, 128])
    wx_d = din("rg_w_x", [2, 8, 128, 128])
    ba_d = din("rg_b_a", [128, 2, 8])
    bx_d = din("rg_b_x", [128, 2, 8])
    lam_d = din("rg_lam", [128, 2, 8])
    cfw_d = din("cfw", [128, 96, 3])
    cfb_d = din("cfb", [128, 96])
    gfin_d = din("gfin", [128, D])
    cos_s_d = din("cos_s", [32, CH])
    sin_s_d = din("sin_s", [32, CH])
    cos_p_d = din("cos_p", [32, 4352])
    sin_p_d = din("sin_p", [32, 4352])
    pm_d = din("pm", [128, 128])
    mask_d = din("bmask", [128, 512])
    ident_d = din("ident", [128, 128])
    kb_d = din("kbias", [128, 3, 16, 34])
    sel_d = din("sel", [128, 2, 8])
    vm_d = din("vmask", [128, 2])
    tm_d = din("tmask", [128, 256])
    ys_d = nc.dram_tensor("ys", [CH, D], F32, kind="ExternalOutput").ap()
    yp_d = nc.dram_tensor("yp", [CH, D], F32, kind="ExternalOutput").ap()

    WIN = dscr("WIN", [40, 128, 16, 128], BF16)
    WUP = dscr("WUP", [96, 128, 16, 128], BF16)
    WOUT = dscr("WOUT", [16, 128, D], BF16)
    WDN = dscr("WDN", [48, 128, D], BF16)
    XRF = dscr("XRF", [8, 128, SEQ_P], F32)

    def sb(name, shape, dt=F32):
        return T(es.enter_context(nc.sbuf_tensor("sb_" + name, list(shape), dt)))

    PS = [T(es.enter_context(nc.psum_tensor("ps%d" % i, [128, 512], F32)), True) for i in range(6)]
    PB = [T(es.enter_context(nc.psum_tensor("pb%d" % i, [128, 1024], BF16)), True) for i in range(2)]

    def load(dst, src, q="sp"):
        I(q, lambda e: e.dma_start(out=dst.t[:], in_=src), w=[dst.b], dma=True)

    gmix = sb("gmix", [128, 16]); load(gmix, g_mix_d[:, :])
    gmlp = sb("gmlp", [128, 16]); load(gmlp, g_mlp_d[:, :])
    gmo = sb("gmo", [128, 16]); load(gmo, g_mo_d[:, :])
    crw = sb("crw", [128, 8, 4]); load(crw, crw_d[:, :, :])
    crb = sb("crb", [128, 8]); load(crb, crb_d[:, :])
    ba = sb("ba", [128, 2, 8]); load(ba, ba_d[:, :, :])
    bx = sb("bx", [128, 2, 8]); load(bx, bx_d[:, :, :])
    lam = sb("lam", [128, 2, 8]); load(lam, lam_d[:, :, :])
    cfw = sb("cfw", [128, 96, 3]); load(cfw, cfw_d[:, :, :])
    cfb = sb("cfb", [128, 96]); load(cfb, cfb_d[:, :])
    gfin = sb("gfin", [128, D]); load(gfin, gfin_d[:, :])
    kbias = sb("kbias", [128, 3, 16, 34]); load(kbias, kb_d[:, :, :, :])
    sel = sb("sel", [128, 2, 8]); load(sel, sel_d[:, :, :])
    vmask = sb("vmask", [128, 2]); load(vmask, vm_d[:, :])
    tmask = sb("tmask", [128, 256]); load(tmask, tm_d[:, :])
    tmpc = sb("tmpc", [128, 512])
    bmask = sb("bmaskb", [128, 512], BF16)
    load(tmpc, mask_d[:, :])
    I("dve", lambda e: e.tensor_copy(out=bmask.t[:], in_=tmpc.t[:]), r=[tmpc.b], w=[bmask.b])
    ident = sb("identb", [128, 128], BF16)
    tmpi = sb("tmpi", [128, 128])
    load(tmpi, ident_d[:, :])
    I("dve", lambda e: e.tensor_copy(out=ident.t[:], in_=tmpi.t[:]), r=[tmpi.b], w=[ident.b])
    pm = sb("pmb", [128, 128], BF16)
    tmpp = sb("tmpp", [128, 128])
    load(tmpp, pm_d[:, :])
    I("dve", lambda e: e.tensor_copy(out=pm.t[:], in_=tmpp.t[:]), r=[tmpp.b], w=[pm.b])
    ones = sb("onesb", [128, 128], BF16)
    I("dve", lambda e: e.memset(ones.t[:], 1.0), w=[ones.b])
    cl = sb("cl", [128, 2, 8]); cl2 = sb("cl2", [128, 2, 8])
    I("act", lambda e: e.activation(out=cl.t[:], in_=lam.t[:], func=AF.Exp, scale=-1.0), r=[lam.b], w=[cl.b])
    I("act", lambda e: e.activation(out=cl.t[:], in_=cl.t[:], func=AF.Ln, bias=1.0), r=[cl.b], w=[cl.b])
    I("dve", lambda e: e.tensor_scalar(out=cl2.t[:], in0=cl.t[:], scalar1=-16.0, scalar2=None, op0=ALU.mult), r=[cl.b], w=[cl2.b])
    I("dve", lambda e: e.tensor_scalar(out=cl.t[:], in0=cl.t[:], scalar1=-8.0, scalar2=None, op0=ALU.mult), r=[cl.b], w=[cl.b])
    hba = sb("hba", [128, 2, 8]); hbx = sb("hbx", [128, 2, 8]); hcl = sb("hcl", [128, 2, 8])
    I("dve", lambda e: e.tensor_scalar(out=hba.t[:], in0=ba.t[:], scalar1=0.5, scalar2=None, op0=ALU.mult), r=[ba.b], w=[hba.b])
    I("dve", lambda e: e.tensor_scalar(out=hbx.t[:], in0=bx.t[:], scalar1=0.5, scalar2=None, op0=ALU.mult), r=[bx.b], w=[hbx.b])
    I("dve", lambda e: e.tensor_scalar(out=hcl.t[:], in0=cl.t[:], scalar1=0.5, scalar2=None, op0=ALU.mult), r=[cl.b], w=[hcl.b])
    wg = sb("wg", [128, 32, 128], BF16)
    esg = contextlib.ExitStack()
    wgt = T(esg.enter_context(nc.sbuf_tensor("wgt", [128, 16, 128], F32)))
    for gi, src in enumerate((wa_d, wx_d)):
        I("sp", lambda e, src=src: e.dma_start(out=wgt.t[:], in_=src.rearrange("z g i j -> i (z g) j")), w=[wgt.b], dma=True)
        I("dve", lambda e, gi=gi: e.tensor_copy(out=wg.t[:, gi * 16:(gi + 1) * 16, :], in_=wgt.t[:]), r=[wgt.b], w=[wg.b])

    S.barrier()
    esg.close()

    st4 = [sb("st4_%d" % i, [128, 4]) for i in range(3)]
    HF = sb("HF", [128, 8])
    HB = sb("HB", [128, 8])
    hbds = {"s": sb("hbd_s", [128, 16, 8], BF16), "p": sb("hbd_p", [128, 16, 8], BF16)}
    esp = contextlib.ExitStack()
    wst = [T(esp.enter_context(nc.sbuf_tensor("wst%d" % i, [128, 2048], F32))) for i in range(4)]
    wsb = [T(esp.enter_context(nc.sbuf_tensor("wsb%d" % i, [128, 2048], BF16))) for i in range(4)]
    cnt = [0]

    def prep(Wd, K, N, g, dst, tiled, WB):
        for kc in range(K // 128):
            for cb in range(0, N, 2048):
                cw = min(2048, N - cb)
                k = cnt[0] % 4
                cnt[0] += 1
                a, b = wst[k], wsb[k]
                I("sp", lambda e: e.dma_start(out=a.t[:, :cw], in_=Wd[kc * 128:(kc + 1) * 128, cb:cb + cw]), w=[a.b], dma=True)
                if g is not None:
                    I("dve", lambda e: e.tensor_scalar(out=b.t[:, :cw], in0=a.t[:, :cw], scalar1=g.t[:, kc:kc + 1], scalar2=None, op0=ALU.mult), r=[a.b, g.b], w=[b.b])
                else:
                    I("dve", lambda e: e.tensor_copy(out=b.t[:, :cw], in_=a.t[:, :cw]), r=[a.b], w=[b.b])
                if tiled:
                    j0 = cb // 128
                    I("pool", lambda e: e.dma_start(out=dst[j0:j0 + cw // 128, :, kc, :].rearrange("j p c -> p j c"),
                                                    in_=b.t[:, :cw].rearrange("p (j c) -> p j c", c=128)), r=[b.b], w=[WB], dma=True)
                else:
                    I("pool", lambda e: e.dma_start(out=dst[kc, :, cb:cb + cw], in_=b.t[:, :cw]), r=[b.b], w=[WB], dma=True)
                yield None

    WBi, WBo, WBu, WBd = Buf(), Buf(), Buf(), Buf()
    import itertools
    for _ in prep(w_in_d, D, 5120, gmix, WIN, True, WBi):
        pass
    prep_rest = itertools.chain(prep(w_out_d, D, D, gmo, WOUT, False, WBo), prep(w_up_d, D, 12288, gmlp, WUP, True, WBu),
                                prep(w_dn_d, 6144, D, None, WDN, False, WBd))

    A = {}
    uid = [0]

    def alloc_front(esx, full=True, hw=512):
        uid[0] += 1
        p = "f%d_" % uid[0]

        def sbx(name, shape, dt=F32):
            return T(esx.enter_context(nc.sbuf_tensor(p + name, list(shape), dt)))
        A["hsb"] = [sbx("hs%d" % i, [128, D], BF16) for i in range(3 if full else 2)]
        A["junk"] = sbx("junk", [128, D], BF16)
        A["hT"] = [sbx("hT%d" % i, [128, 16, hw], BF16) for i in range(2)]
        if full:
            A["xtb"] = [sbx("xt%d" % i, [128, D]) for i in range(3)]
            A["wch"] = [sbx("wch%d" % i, [128, 16, 128], BF16) for i in range(4)]
            A["zb"] = [sbx("zb%d" % i, [128, 512], BF16) for i in range(3)]
            A["zf"] = [sbx("zf%d" % i, [128, 512]) for i in range(3)]
            A["r1"] = sbx("r1", [32, 512]); A["r2"] = sbx("r2", [32, 512])
            A["cst"] = [sbx("cst%d" % i, [32, 512]) for i in range(2)]
            A["snt"] = [sbx("snt%d" % i, [32, 512]) for i in range(2)]
    wk = [0]
    tk = [0]
    DVE_EVAC = [False]

    def front(xsrc_rows, hTt, col0):
        k = tk[0] % 3
        tk[0] += 1
        xt, hs, s4 = A["xtb"][k], A["hsb"][k], st4[k]
        junk = A["junk"]
        I("sp", lambda e: e.dma_start(out=xt.t[:], in_=xsrc_rows), w=[xt.b], dma=True)
        I("act", lambda e: e.activation(out=junk.t[:], in_=xt.t[:], func=AF.Square, accum_out=s4.t[:, 0:1]), r=[xt.b], w=[junk.b, s4.b])
        I("dve", lambda e: e.tensor_scalar(out=s4.t[:, 1:2], in0=s4.t[:, 0:1], scalar1=1.0 / D, scalar2=EPS, op0=ALU.mult, op1=ALU.add), r=[s4.b], w=[s4.b])
        I("act", lambda e: e.activation(out=s4.t[:, 2:3], in_=s4.t[:, 1:2], func=AF.Sqrt), r=[s4.b], w=[s4.b])
        I("dve", lambda e: e.reciprocal(out=s4.t[:, 3:4], in_=s4.t[:, 2:3]), r=[s4.b], w=[s4.b])
        I("act", lambda e: e.activation(out=hs.t[:], in_=xt.t[:], func=AF.Identity, scale=s4.t[:, 3:4]), r=[xt.b, s4.b], w=[hs.b])
        transpose16(hs, hTt, col0)

    def transpose16(hs, hTt, col0):
        for half in range(2):
            pb = PB[half]
            for kk in range(8):
                kc = half * 8 + kk
                I("pe", lambda e: e.transpose(pb.t[:, kk * 128:(kk + 1) * 128], hs.t[:, kc * 128:(kc + 1) * 128], ident.t[:]),
                  r=[hs.b, ident.b], w=[pb.b], inc=(kk == 7))
            eng = "dve" if (half == 0 or DVE_EVAC[0]) else "act"
            src = pb.t[:].rearrange("p (k c) -> p k c", c=128)
            dst = hTt.t[:, half * 8:(half + 1) * 8, col0:col0 + 128]
            if eng == "dve":
                I("dve", lambda e: e.tensor_copy(out=dst, in_=src), r=[pb.b], w=[hTt.b])
            else:
                I("act", lambda e: e.activation(out=dst, in_=src, func=AF.Identity), r=[pb.b], w=[hTt.b])

    def proj_chunk(Wt, j, hTt, W, ps, WB):
        wt = A["wch"][wk[0] % 4]
        wk[0] += 1
        I("sp", lambda e: e.dma_start(out=wt.t[:], in_=Wt[j]), r=[WB], w=[wt.b], dma=True)
        for kc in range(16):
            I("pe", lambda e: e.matmul(ps.t[:, :W], lhsT=wt.t[:, kc, :], rhs=hTt.t[:, kc, :W], start=(kc == 0), stop=(kc == 15)),
              r=[wt.b, hTt.b], w=[ps.b], inc=(kc == 15))

    zk = [0]

    def pipelined_steps(nsteps, front_fn, proj_fn):
        for f in front_fn(0):
            f()
        for st in range(nsteps):
            nxt = front_fn(st + 1) if st + 1 < nsteps else []
            pj = proj_fn(st)
            per = max(1, len(pj) // max(1, len(nxt)))
            for idx, p in enumerate(pj):
                if nxt and idx % per == 0:
                    nxt.pop(0)()
                p()
            for f in nxt:
                f()

    def run_seq(tag, x_d, R, Q0, TQ, M0, cos_d, sin_d, y_d, is_prompt, HF, HB):
        QT = dscr(tag + "QT", [8, 128, TQ], BF16)
        KT = dscr(tag + "KT", [8, 128, R], BF16)
        VT = dscr(tag + "VT", [8, 128, R], BF16)
        XRT = dscr(tag + "XRT", [8, 128, TQ], F32)
        GT = dscr(tag + "GT", [8, 128, TQ], F32)
        X1 = dscr(tag + "X1", [TQ, D], F32)
        X2 = dscr(tag + "X2", [CH, D], F32)
        H2 = dscr(tag + "H2", [128, 16, TQ], BF16)
        SQT, SKT, SVT, SXR, SGT, SX1, SX2, SH2 = (Buf() for _ in range(8))
        hbd = hbds[tag]
        I("dve", lambda e: e.memset(hbd.t[:], 0.0), w=[hbd.b])
        bnd = {}
        for gi_ in range(4):
            for side_, tok_ in ((0, M0 + 512 * gi_ - 1), (1, M0 + 512 * gi_ + 512)):
                if 0 <= tok_ < TQ:
                    bnd[tok_] = 2 * gi_ + side_

        stage(2)
        esA = contextlib.ExitStack()
        alloc_front(esA)
        zb, zf, r1, r2, cst, snt = A["zb"], A["zf"], A["r1"], A["r2"], A["cst"], A["snt"]
        nsteps = min((R + 511) // 512, DEBUG["maxsteps"])

        def front_fn(st):
            s0 = st * 512
            W = min(512, R - s0)
            hTt = A["hT"][st % 2]
            return [(lambda ti=ti: front(x_d[s0 + ti * 128:s0 + (ti + 1) * 128, :], hTt, ti * 128)) for ti in range(W // 128)]

        def proj_fn(st):
            s0 = st * 512
            W = min(512, R - s0)
            hTt = A["hT"][st % 2]
            inq = (s0 >= Q0) and (s0 < Q0 + TQ)
            wq = min(W, Q0 + TQ - s0) if inq else 0
            ct, sn = cst[st % 2], snt[st % 2]
            chunks = list(range(8, 24)) + (list(range(0, 8)) + list(range(24, 40)) if inq else [])

            def body(j, first):
                if first:
                    I("sp", lambda e: e.dma_start(out=ct.t[:, :W], in_=cos_d[:, s0:s0 + W]), w=[ct.b], dma=True)
                    I("sp", lambda e: e.dma_start(out=sn.t[:, :W], in_=sin_d[:, s0:s0 + W]), w=[sn.b], dma=True)
                ps = PS[zk[0] % 2]
                k3 = zk[0] % 3
                zk[0] += 1
                proj_chunk(WIN, j, hTt, W, ps, WBi)
                if tag == "s":
                    next(prep_rest, None)
                if j < 16:
                    z = zb[k3]
                    I("act", lambda e: e.activation(out=z.t[:, :W], in_=ps.t[:, :W], func=AF.Identity), r=[ps.b], w=[z.b])
                    rp = PS[2]
                    I("pe", lambda e: e.matmul(rp.t[:, :W], lhsT=pm.t[:, :], rhs=z.t[:, :W], start=True, stop=True), r=[pm.b, z.b], w=[rp.b])
                    I("dve", lambda e: e.tensor_tensor(out=r1.t[:, :W], in0=ps.t[0:32, :W], in1=ct.t[:, :W], op=ALU.mult), r=[ps.b, ct.b], w=[r1.b])
                    I("dve", lambda e: e.tensor_tensor(out=r2.t[:, :W], in0=rp.t[0:32, :W], in1=sn.t[:, :W], op=ALU.mult), r=[rp.b, sn.b], w=[r2.b])
                    I("dve", lambda e: e.tensor_tensor(out=z.t[0:32, :W], in0=r1.t[:, :W], in1=r2.t[:, :W], op=ALU.add), r=[r1.b, r2.b], w=[z.b])
                    if j < 8:
                        I("pool", lambda e: e.dma_start(out=QT[j, :, s0 - Q0:s0 - Q0 + wq], in_=z.t[:, :wq]), r=[z.b], w=[SQT], dma=True)
                    else:
                        I("pool", lambda e: e.dma_start(out=KT[j - 8, :, s0:s0 + W], in_=z.t[:, :W]), r=[z.b], w=[SKT], dma=True)
                elif j < 24:
                    z = zb[k3]
                    I("act", lambda e: e.activation(out=z.t[:, :W], in_=ps.t[:, :W], func=AF.Identity), r=[ps.b], w=[z.b])
                    I("pool", lambda e: e.dma_start(out=VT[j - 16, :, s0:s0 + W], in_=z.t[:, :W]), r=[z.b], w=[SVT], dma=True)
                else:
                    z = zf[k3]
                    I("act", lambda e: e.activation(out=z.t[:, :W], in_=ps.t[:, :W], func=AF.Identity), r=[ps.b], w=[z.b])
                    if j < 32:
                        I("pool", lambda e: e.dma_start(out=XRT[j - 24, :, s0 - Q0:s0 - Q0 + wq], in_=z.t[:, :wq]), r=[z.b], w=[SXR], dma=True)
                    else:
                        I("pool", lambda e: e.dma_start(out=GT[j - 32, :, s0 - Q0:s0 - Q0 + wq], in_=z.t[:, :wq]), r=[z.b], w=[SGT], dma=True)
            return [(lambda j=j, f=(i == 0): body(j, f)) for i, j in enumerate(chunks)]

        pipelined_steps(nsteps, front_fn, proj_fn)

        if tag == "s":
            for _ in prep_rest:
                pass
        S.barrier()
        esA.close()
        if tag == "s":
            esp.close()
        esq = contextlib.ExitStack()
        mixT = T(esq.enter_context(nc.sbuf_tensor("t_" + tag + "mixT", [128, 16, TQ], BF16)))

        stage(3)
        with contextlib.ExitStack() as es2:
            def sb2(name, shape, dt=F32):
                return T(es2.enter_context(nc.sbuf_tensor("t_" + tag + name, list(shape), dt)))
            RP = R + (1792 if is_prompt else 0)
            kT = [sb2("kT%d" % i, [128, RP], BF16) for i in range(2)]
            vT = [sb2("vT%d" % i, [128, RP], BF16) for i in range(1)]
            if RP > R:
                for t_ in kT + vT:
                    I("dve", lambda e: e.memset(t_.t[:, R:RP], 0.0), w=[t_.b])
            qT = [sb2("qT%d" % i, [128, TQ], BF16) for i in range(2)]
            nblk = [((R // d) + 127) // 128 for d in DILS]
            vc = [sb2("vc%d" % b, [128, DILS[b] * nblk[b], 128], BF16) for b in range(3)]
            pt = [sb2("pt%d" % i, [128, 256], BF16) for i in range(4)]
            rd = [sb2("rd%d" % i, [128, 512]) for i in range(2)]
            pk = 0
            sc = 1.0 / math.sqrt(128.0)
            for h in range(8):
                k_, v_, q_ = kT[h % 2], vT[0], qT[h % 2]
                I("sp", lambda e: e.dma_start(out=k_.t[:, :R], in_=KT[h]), r=[SKT], w=[k_.b], dma=True)
                I("sp", lambda e: e.dma_start(out=v_.t[:, :R], in_=VT[h]), r=[SVT], w=[v_.b], dma=True)
                I("sp", lambda e: e.dma_start(out=q_.t[:], in_=QT[h]), r=[SQT], w=[q_.b], dma=True)
                for b, dil in enumerate(DILS):
                    L = R // dil
                    items = [(r, kb) for r in range(dil) for kb in range(nblk[b])]
                    for i0 in range(0, len(items), 8):
                        grp = items[i0:i0 + 8]
                        pb = PB[(i0 // 8) % 2]
                        for n_, (r, kb) in enumerate(grp):
                            k0 = kb * 128
                            nk = 128
                            a0 = r + dil * k0
                            I("pe", lambda e: e.transpose(pb.t[:nk, n_ * 128:(n_ + 1) * 128], v_.t[:, a0:a0 + dil * (nk - 1) + 1:dil], ident.t[:]),
                              r=[v_.b, ident.b], w=[pb.b], inc=(n_ == len(grp) - 1))
                        for n_, (r, kb) in enumerate(grp):
                            nk = 128
                            idx = r * nblk[b] + kb
                            eng = "act" if (i0 // 8) % 2 else "dve"
                            if eng == "dve":
                                I("dve", lambda e: e.tensor_copy(out=vc[b].t[:nk, idx, :], in_=pb.t[:nk, n_ * 128:(n_ + 1) * 128]), r=[pb.b], w=[vc[b].b])
                            else:
                                I("act", lambda e: e.activation(out=vc[b].t[:nk, idx, :], in_=pb.t[:nk, n_ * 128:(n_ + 1) * 128], func=AF.Identity), r=[pb.b], w=[vc[b].b])
                nbank = (TQ + 511) // 512
                for qb in range(nbank):
                    b0 = qb * 512
                    Wq = min(512, TQ - b0)
                    num, den = PS[2 + (qb % 2) * 2], PS[3 + (qb % 2) * 2]
                    blocks = []
                    for b, dil in enumerate(DILS):
                        L = R // dil
                        for r in range(dil):
                            i0 = (r - (Q0 + b0)) % dil
                            nq = len(range(i0, Wq, dil))
                            if nq <= 0:
                                continue
                            lq0 = (Q0 + b0 + i0 - r) // dil
                            lo = max(0, lq0 - 64)
                            hi = min(L - 1, lq0 + nq - 1 + 64)
                            for kb in range(lo // 128, hi // 128 + 1):
                                k0 = kb * 128
                                nk = 128
                                qa = max(lq0, k0 - 64)
                                qe = min(lq0 + nq, k0 + min(128, L - k0) + 64)
                                n = qe - qa
                                if n <= 0:
                                    continue
                                blocks.append((b, dil, r, kb, k0, nk, qa, n))

                    def emit_S(i):
                        b, dil, r, kb, k0, nk, qa, n = blocks[i]
                        sp_ = PS[(pk + i) % 2]
                        p_ = pt[(pk + i) % 4]
                        ka = r + dil * k0
                        qc = r + dil * qa - Q0
                        I("pe", lambda e: e.matmul(sp_.t[:nk, :n], lhsT=k_.t[:, ka:ka + dil * (nk - 1) + 1:dil],
                                                   rhs=q_.t[:, qc:qc + dil * (n - 1) + 1:dil], start=True, stop=True),
                          r=[k_.b, q_.b], w=[sp_.b])
                        if is_prompt:
                            I("act", lambda e: e.activation(out=p_.t[:nk, :n], in_=sp_.t[:nk, :n], func=AF.Exp, scale=sc,
                                                            bias=kbias.t[:nk, b, r, kb:kb + 1]), r=[sp_.b, kbias.b], w=[p_.b])
                        else:
                            I("act", lambda e: e.activation(out=p_.t[:nk, :n], in_=sp_.t[:nk, :n], func=AF.Exp, scale=sc), r=[sp_.b], w=[p_.b])
                        off = qa - k0 + 64
                        meng = "dve" if i % 3 else "pool"
                        I(meng, lambda e: e.tensor_tensor(out=p_.t[:nk, :n], in0=p_.t[:nk, :n], in1=bmask.t[:nk, off:off + n], op=ALU.mult),
                          r=[p_.b, bmask.b], w=[p_.b])

                    def emit_PV(i):
                        b, dil, r, kb, k0, nk, qa, n = blocks[i]
                        p_ = pt[(pk + i) % 4]
                        c0 = r + dil * qa - Q0 - b0
                        idx = r * nblk[b] + kb
                        I("pe", lambda e: e.matmul(num.t[:, c0:c0 + dil * (n - 1) + 1:dil], lhsT=vc[b].t[:nk, idx, :], rhs=p_.t[:nk, :n],
                                                   start=(i == 0), stop=False, skip_group_check=True), r=[vc[b].b, p_.b], w=[num.b], inc=False)
                        I("pe", lambda e: e.matmul(den.t[:, c0:c0 + dil * (n - 1) + 1:dil], lhsT=ones.t[:nk, :], rhs=p_.t[:nk, :n],
                                                   start=(i == 0), stop=False, skip_group_check=True), r=[ones.b, p_.b], w=[den.b])

                    emit_S(0)
                    for i in range(len(blocks)):
                        if i + 1 < len(blocks):
                            emit_S(i + 1)
                        emit_PV(i)
                    pk += len(blocks)
                    rd_ = rd[qb % 2]
                    I("dve", lambda e: e.tensor_scalar(out=rd_.t[:, :Wq], in0=den.t[:, :Wq], scalar1=1e-30, scalar2=None, op0=ALU.add), r=[den.b], w=[rd_.b])
                    I("dve", lambda e: e.reciprocal(out=rd_.t[:, :Wq], in_=rd_.t[:, :Wq]), r=[rd_.b], w=[rd_.b])
                    I("dve", lambda e: e.tensor_tensor(out=mixT.t[:, h, b0:b0 + Wq], in0=num.t[:, :Wq], in1=rd_.t[:, :Wq], op=ALU.mult),
                      r=[num.b, rd_.b], w=[mixT.b])
            S.barrier()

        stage(4)
        with contextlib.ExitStack() as es2:
            def sb2(name, shape, dt=F32):
                return T(es2.enter_context(nc.sbuf_tensor("t_" + tag + name, list(shape), dt)))
            xrp = sb2("xrp", [128, TQ + 3])
            gt = sb2("gt", [128, TQ])
            u = sb2("u", [128, TQ])
            ub = sb2("ub", [128, TQ], BF16)
            rr = sb2("rr", [128, TQ])
            ii = sb2("ii", [128, TQ])
            aa = sb2("aa", [128, TQ])
            hh = [sb2("hh%d" % i, [128, TQ]) for i in range(2)]
            I("dve", lambda e: e.memset(xrp.t[:, 0:2], 0.0), w=[xrp.b])
            I("dve", lambda e: e.memset(xrp.t[:, TQ + 2:TQ + 3], 0.0), w=[xrp.b])
            for g in range(8):
                I("sp", lambda e: e.dma_start(out=xrp.t[:, 2:TQ + 2], in_=XRT[g]), r=[SXR], w=[xrp.b], dma=True)
                I("sp", lambda e: e.dma_start(out=gt.t[:], in_=GT[g]), r=[SGT], w=[gt.b], dma=True)
                lru_conv(xrp, u, ub, g, TQ)
                for z in range(2):
                    lru_gates(u, ub, rr, ii, aa, z, g, TQ)
                    if is_prompt:
                        I("pool", lambda e: e.tensor_tensor(out=ii.t[:, 0:128], in0=ii.t[:, 0:128], in1=tmask.t[:, 0:128], op=ALU.mult), r=[ii.b, tmask.b], w=[ii.b])
                        I("pool", lambda e: e.tensor_tensor(out=ii.t[:, TQ - 128:TQ], in0=ii.t[:, TQ - 128:TQ], in1=tmask.t[:, 128:256], op=ALU.mult), r=[ii.b, tmask.b], w=[ii.b])
                    if z == 0:
                        lo, hi = (2, TQ) if is_prompt else (0, TQ)
                        init = HF.t[:, g:g + 1] if is_prompt else 0.0
                        if is_prompt:
                            I("dve", lambda e: e.memset(hh[0].t[:, 0:2], 0.0), w=[hh[0].b])
                        I("dve", lambda e: e.tensor_tensor_scan(out=hh[0].t[:, lo:hi], data0=aa.t[:, lo:hi], data1=ii.t[:, lo:hi], initial=init,
                                                                op0=ALU.mult, op1=ALU.add), r=[aa.b, ii.b] + ([HF.b] if is_prompt else []), w=[hh[0].b])
                    else:
                        lo, hi = (0, TQ - 2) if is_prompt else (0, TQ)
                        init = HB.t[:, g:g + 1] if is_prompt else 0.0
                        if is_prompt:
                            I("dve", lambda e: e.memset(hh[1].t[:, TQ - 2:TQ], 0.0), w=[hh[1].b])
                        I("dve", lambda e: e.tensor_tensor_scan(out=hh[1].t[:, lo:hi][:, ::-1], data0=aa.t[:, lo:hi][:, ::-1], data1=ii.t[:, lo:hi][:, ::-1],
                                                                initial=init, op0=ALU.mult, op1=ALU.add), r=[aa.b, ii.b] + ([HB.b] if is_prompt else []), w=[hh[1].b])
                I("pool", lambda e: e.tensor_tensor(out=hh[0].t[:], in0=hh[0].t[:], in1=hh[1].t[:], op=ALU.add), r=[hh[0].b, hh[1].b], w=[hh[0].b])
                I("act", lambda e: e.activation(out=rr.t[:], in_=gt.t[:], func=AF.Gelu), r=[gt.b], w=[rr.b])
                I("dve", lambda e: e.tensor_tensor(out=mixT.t[:, 8 + g, :TQ], in0=hh[0].t[:], in1=rr.t[:], op=ALU.mult), r=[hh[0].b, rr.b], w=[mixT.b])
            S.barrier()

        stage(5)
        with contextlib.ExitStack() as es2:
            def sb2(name, shape, dt=F32):
                return T(es2.enter_context(nc.sbuf_tensor("t_" + tag + name, list(shape), dt)))
            alloc_front(es2, full=False, hw=128)
            hsb, junk, hT = A["hsb"], A["junk"], A["hT"]
            sq = [sb2("sq%d" % i, [128, 256], BF16) for i in range(2)]
            rs = [sb2("rs%d" % i, [128, 256]) for i in range(2)]
            mm = sb2("mm", [128, 16, 256], BF16)
            wo = [sb2("wo%d" % i, [128, 16, 512], BF16) for i in range(2)]
            x1b = [sb2("x1b%d" % i, [128, D]) for i in range(2)]
            nbank = (TQ + 255) // 256
            xk = 0
            for tb in range(nbank):
                b0 = tb * 256
                W = min(256, TQ - b0)
                for part in range(2):
                    ssp = PS[part]
                    for c in range(8):
                        s_ = sq[c % 2]
                        I("pool", lambda e: e.tensor_tensor(out=s_.t[:, :W], in0=mixT.t[:, part * 8 + c, b0:b0 + W], in1=mixT.t[:, part * 8 + c, b0:b0 + W], op=ALU.mult),
                          r=[mixT.b], w=[s_.b])
                        I("pe", lambda e: e.matmul(ssp.t[:, :W], lhsT=ones.t[:, :], rhs=s_.t[:, :W], start=(c == 0), stop=(c == 7)), r=[ones.b, s_.b], w=[ssp.b])
                    r_ = rs[part]
                    I("dve", lambda e: e.tensor_scalar(out=r_.t[:, :W], in0=ssp.t[:, :W], scalar1=1.0 / 1024, scalar2=EPS, op0=ALU.mult, op1=ALU.add), r=[ssp.b], w=[r_.b])
                    I("act", lambda e: e.activation(out=r_.t[:, :W], in_=r_.t[:, :W], func=AF.Sqrt), r=[r_.b], w=[r_.b])
                    I("dve", lambda e: e.reciprocal(out=r_.t[:, :W], in_=r_.t[:, :W]), r=[r_.b], w=[r_.b])
                    for c in range(8):
                        I("dve", lambda e: e.tensor_tensor(out=mm.t[:, part * 8 + c, :W], in0=mixT.t[:, part * 8 + c, b0:b0 + W], in1=r_.t[:, :W], op=ALU.mult),
                          r=[mixT.b, r_.b], w=[mm.b])
                nt = W // 128
                for cg in range(4):
                    w_ = wo[cg % 2]
                    I("sp", lambda e: e.dma_start(out=w_.t[:], in_=WOUT[:, :, cg * 512:(cg + 1) * 512].rearrange("k p c -> p k c")), r=[WBo], w=[w_.b], dma=True)
                    for ti in range(nt):
                        ps = PS[2 + (ti % 2)]
                        if cg == 0:
                            tok = Q0 + b0 + ti * 128
                            I("sp", lambda e: e.dma_start(out=x1b[ti].t[:], in_=x_d[tok:tok + 128, :]), w=[x1b[ti].b], dma=True)
                        for kc in range(16):
                            I("pe", lambda e: e.matmul(ps.t[:, :], lhsT=mm.t[:, kc, ti * 128:(ti + 1) * 128], rhs=w_.t[:, kc, :], start=(kc == 0), stop=(kc == 15)),
                              r=[mm.b, w_.b], w=[ps.b], inc=(kc == 15))
                        I("dve", lambda e: e.tensor_tensor(out=x1b[ti].t[:, cg * 512:(cg + 1) * 512], in0=ps.t[:, :], in1=x1b[ti].t[:, cg * 512:(cg + 1) * 512], op=ALU.add),
                          r=[ps.b, x1b[ti].b], w=[x1b[ti].b])
                for ti in range(nt):
                    tl = b0 + ti * 128
                    x1 = x1b[ti]
                    I("pool", lambda e: e.dma_start(out=X1[tl:tl + 128, :], in_=x1.t[:]), r=[x1.b], w=[SX1], dma=True)
                    k = xk % 2
                    xk += 1
                    hs, s4 = hsb[k], st4[k]
                    I("act", lambda e: e.activation(out=junk.t[:], in_=x1.t[:], func=AF.Square, accum_out=s4.t[:, 0:1]), r=[x1.b], w=[junk.b, s4.b])
                    I("dve", lambda e: e.tensor_scalar(out=s4.t[:, 1:2], in0=s4.t[:, 0:1], scalar1=1.0 / D, scalar2=EPS, op0=ALU.mult, op1=ALU.add), r=[s4.b], w=[s4.b])
                    I("act", lambda e: e.activation(out=s4.t[:, 2:3], in_=s4.t[:, 1:2], func=AF.Sqrt), r=[s4.b], w=[s4.b])
                    I("dve", lambda e: e.reciprocal(out=s4.t[:, 3:4], in_=s4.t[:, 2:3]), r=[s4.b], w=[s4.b])
                    I("act", lambda e: e.activation(out=hs.t[:], in_=x1.t[:], func=AF.Identity, scale=s4.t[:, 3:4]), r=[x1.b, s4.b], w=[hs.b])
                    hTt = hT[k]
                    transpose16(hs, hTt, 0)
                    I("pool", lambda e: e.dma_start(out=H2[:, :, tl:tl + 128], in_=hTt.t[:, :, 0:128]), r=[hTt.b], w=[SH2], dma=True)
                    for tok_, idx_ in bnd.items():
                        if tl <= tok_ < tl + 128:
                            cc_ = tok_ - tl
                            edge_ = is_prompt and idx_ in (0, 7)
                            if edge_:
                                I("dve", lambda e: e.tensor_scalar(out=hbd.t[:, :, idx_:idx_ + 1], in0=hTt.t[:, :, cc_:cc_ + 1], scalar1=vmask.t[:, (0 if idx_ == 0 else 1):(1 if idx_ == 0 else 2)], scalar2=None, op0=ALU.mult),
                                  r=[hTt.b, vmask.b], w=[hbd.b])
                            else:
                                I("dve", lambda e: e.tensor_copy(out=hbd.t[:, :, idx_:idx_ + 1], in_=hTt.t[:, :, cc_:cc_ + 1]), r=[hTt.b], w=[hbd.b])
            S.barrier()
        esq.close()

        stage(6)
        with contextlib.ExitStack() as es2:
            def sb2(name, shape, dt=F32):
                return T(es2.enter_context(nc.sbuf_tensor("t_" + tag + name, list(shape), dt)))
            h2 = [sb2("h2_%d" % i, [128, 16, 514], BF16) for i in range(2)]
            actT = sb2("actT", [128, 48, 512], BF16)
            wd = [sb2("wd%d" % i, [128, 12, 512], BF16) for i in range(2)]
            A["wch"] = [sb2("wch%d" % i, [128, 16, 128], BF16) for i in range(4)]
            wch = A["wch"]
            U = [sb2("U%d" % i, [128, 514]) for i in range(2)]
            cv = [sb2("cv%d" % i, [128, 512]) for i in range(2)]
            gl = sb2("gl", [128, 512])
            x1p = [sb2("x1p%d" % i, [128, 512]) for i in range(2)]
            x2p = [sb2("x2p%d" % i, [128, 512]) for i in range(2)]
            UB = sb2("UB", [128, 96, 8])
            ssq = sb2("ssq", [128, 4, 4])
            s3 = sb2("s3", [128, 4])
            x2r = [sb2("x2r%d" % i, [128, D]) for i in range(1)]
            yt = [sb2("yt%d" % i, [128, D]) for i in range(1)]
            uk = 0
            dk = 0
            for gi in range(4):
                t0 = M0 + gi * 512
                h2t = h2[gi % 2]
                lo = t0 - 1
                hi = t0 + 513
                left_ok = lo >= 0
                right_ok = hi <= TQ
                c_lo = 0 if left_ok else 1
                c_hi = 514 if right_ok else 513
                I("sp", lambda e: e.dma_start(out=h2t.t[:, :, c_lo:c_hi], in_=H2[:, :, lo + c_lo:lo + c_hi]), r=[SH2], w=[h2t.b], dma=True)
                for jj in range(48):
                    for half in range(2):
                        j = jj + 48 * half
                        ps = PS[uk % 2]
                        pbd = PS[2]
                        U_ = U[uk % 2]
                        c_ = cv[half]
                        uk += 1
                        wt = wch[wk[0] % 4]
                        wk[0] += 1
                        I("sp", lambda e: e.dma_start(out=wt.t[:], in_=WUP[j]), r=[WBu], w=[wt.b], dma=True)
                        for kc in range(16):
                            I("pe", lambda e: e.matmul(ps.t[:, :], lhsT=wt.t[:, kc, :], rhs=h2t.t[:, kc, 1:513], start=(kc == 0), stop=(kc == 15)),
                              r=[wt.b, h2t.b], w=[ps.b], inc=(kc == 15))
                        I("act", lambda e: e.activation(out=U_.t[:, 1:513], in_=ps.t[:, :], func=AF.Identity), r=[ps.b], w=[U_.b])
                        if gi == 0:
                            for kc in range(16):
                                I("pe", lambda e: e.matmul(pbd.t[:, 0:8], lhsT=wt.t[:, kc, :], rhs=hbd.t[:, kc, :], start=(kc == 0), stop=(kc == 15)),
                                  r=[wt.b, hbd.b], w=[pbd.b], inc=(kc == 15))
                            I("dve", lambda e: e.tensor_copy(out=UB.t[:, j, :], in_=pbd.t[:, 0:8]), r=[pbd.b], w=[UB.b])
                        I("pool", lambda e: e.tensor_copy(out=U_.t[:, 0:1], in_=UB.t[:, j, 2 * gi:2 * gi + 1]), r=[UB.b], w=[U_.b])
                        I("pool", lambda e: e.tensor_copy(out=U_.t[:, 513:514], in_=UB.t[:, j, 2 * gi + 1:2 * gi + 2]), r=[UB.b], w=[U_.b])
                        I("dve", lambda e: e.tensor_scalar(out=c_.t[:], in0=U_.t[:, 0:512], scalar1=cfw.t[:, j, 0:1], scalar2=cfb.t[:, j:j + 1], op0=ALU.mult, op1=ALU.add),
                          r=[U_.b, cfw.b, cfb.b], w=[c_.b])
                        I("dve", lambda e: e.scalar_tensor_tensor(out=c_.t[:], in0=U_.t[:, 1:513], scalar=cfw.t[:, j, 1:2], in1=c_.t[:], op0=ALU.mult, op1=ALU.add),
                          r=[U_.b, cfw.b, c_.b], w=[c_.b])
                        I("dve", lambda e: e.scalar_tensor_tensor(out=c_.t[:], in0=U_.t[:, 2:514], scalar=cfw.t[:, j, 2:3], in1=c_.t[:], op0=ALU.mult, op1=ALU.add),
                          r=[U_.b, cfw.b, c_.b], w=[c_.b])
                    I("act", lambda e: e.activation(out=gl.t[:], in_=cv[0].t[:], func=AF.Gelu), r=[cv[0].b], w=[gl.b])
                    I("pool", lambda e: e.tensor_tensor(out=actT.t[:, jj, :], in0=gl.t[:], in1=cv[1].t[:], op=ALU.mult), r=[gl.b, cv[1].b], w=[actT.b])
                for cg in range(4):
                    for jb in range(4):
                        w_ = wd[dk % 2]
                        dk += 1
                        I("sp", lambda e: e.dma_start(out=w_.t[:], in_=WDN[jb * 12:(jb + 1) * 12, :, cg * 512:(cg + 1) * 512].rearrange("k p c -> p k c")), r=[WBd], w=[w_.b], dma=True)
                        for ti in range(4):
                            ps = PS[2 + ti]
                            for j2 in range(12):
                                jx = jb * 12 + j2
                                I("pe", lambda e: e.matmul(ps.t[:, :], lhsT=actT.t[:, jx, ti * 128:(ti + 1) * 128], rhs=w_.t[:, j2, :], start=(jx == 0), stop=(jx == 47)),
                                  r=[actT.b, w_.b], w=[ps.b], inc=(j2 == 11))
                    for ti in range(4):
                        ps = PS[2 + ti]
                        tl = t0 + ti * 128
                        to = gi * 512 + ti * 128
                        a_, b_ = x1p[ti % 2], x2p[ti % 2]
                        I("sp", lambda e: e.dma_start(out=a_.t[:], in_=X1[tl:tl + 128, cg * 512:(cg + 1) * 512]), r=[SX1], w=[a_.b], dma=True)
                        I("dve", lambda e: e.tensor_tensor(out=b_.t[:], in0=ps.t[:, :], in1=a_.t[:], op=ALU.add), r=[ps.b, a_.b], w=[b_.b])
                        I("act", lambda e: e.activation(out=a_.t[:], in_=b_.t[:], func=AF.Square, accum_out=ssq.t[:, ti, cg:cg + 1]), r=[b_.b], w=[a_.b, ssq.b])
                        I("pool", lambda e: e.dma_start(out=X2[to:to + 128, cg * 512:(cg + 1) * 512], in_=b_.t[:]), r=[b_.b], w=[SX2], dma=True)
                for ti in range(4):
                    to = gi * 512 + ti * 128
                    xr2, y_ = x2r[0], yt[0]
                    I("dve", lambda e: e.reduce_sum(out=s3.t[:, 0:1], in_=ssq.t[:, ti, :], axis=AX.X), r=[ssq.b], w=[s3.b])
                    I("dve", lambda e: e.tensor_scalar(out=s3.t[:, 1:2], in0=s3.t[:, 0:1], scalar1=1.0 / D, scalar2=EPS, op0=ALU.mult, op1=ALU.add), r=[s3.b], w=[s3.b])
                    I("act", lambda e: e.activation(out=s3.t[:, 2:3], in_=s3.t[:, 1:2], func=AF.Sqrt), r=[s3.b], w=[s3.b])
                    I("dve", lambda e: e.reciprocal(out=s3.t[:, 3:4], in_=s3.t[:, 2:3]), r=[s3.b], w=[s3.b])
                    I("sp", lambda e: e.dma_start(out=xr2.t[:], in_=X2[to:to + 128, :]), r=[SX2], w=[xr2.b], dma=True)
                    I("dve", lambda e: e.scalar_tensor_tensor(out=y_.t[:], in0=xr2.t[:], scalar=s3.t[:, 3:4], in1=gfin.t[:], op0=ALU.mult, op1=ALU.mult),
                      r=[xr2.b, s3.b, gfin.b], w=[y_.b])
                    I("pool", lambda e: e.dma_start(out=y_d[to:to + 128, :], in_=y_.t[:]), r=[y_.b], w=[YOUT], dma=True)
            S.barrier()

    def lru_conv(xrp, u, ub, g, n, cast="pool"):
        I("dve", lambda e: e.tensor_scalar(out=u.t[:, :n], in0=xrp.t[:, 0:n], scalar1=crw.t[:, g, 0:1], scalar2=crb.t[:, g:g + 1], op0=ALU.mult, op1=ALU.add),
          r=[xrp.b, crw.b, crb.b], w=[u.b])
        for j in range(1, 4):
            I("dve", lambda e: e.scalar_tensor_tensor(out=u.t[:, :n], in0=xrp.t[:, j:j + n], scalar=crw.t[:, g, j:j + 1], in1=u.t[:, :n], op0=ALU.mult, op1=ALU.add),
              r=[xrp.b, crw.b, u.b], w=[u.b])
        if cast == "act":
            I("act", lambda e: e.activation(out=ub.t[:, :n], in_=u.t[:, :n], func=AF.Identity), r=[u.b], w=[ub.b])
        elif cast == "pool":
            I("pool", lambda e: e.tensor_copy(out=ub.t[:, :n], in_=u.t[:, :n]), r=[u.b], w=[ub.b])

    def lru_gates_a(u, ub, rr, ii, aa, z, g, n):
        k = 0
        for gi, (dst, bias) in enumerate(((rr, hba), (ii, hbx))):
            for c0 in range(0, n, 512):
                W = min(512, n - c0)
                ps = PS[k % 2]
                k += 1
                I("pe", lambda e: e.matmul(ps.t[:, :W], lhsT=wg.t[:, gi * 16 + z * 8 + g, :], rhs=ub.t[:, c0:c0 + W], start=True, stop=True), r=[wg.b, ub.b], w=[ps.b])
                I("act", lambda e: e.activation(out=dst.t[:, c0:c0 + W], in_=ps.t[:, :W], func=AF.Tanh, scale=0.5, bias=bias.t[:, z, g:g + 1]), r=[ps.b, bias.b], w=[dst.b])

    def lru_gates_b1(u, ub, rr, ii, aa, z, g, n):
        I("act", lambda e: e.activation(out=aa.t[:, :n], in_=rr.t[:, :n], func=AF.Exp, scale=hcl.t[:, z, g:g + 1], bias=hcl.t[:, z, g:g + 1]), r=[rr.b, hcl.b], w=[aa.b])
        I("act", lambda e: e.activation(out=rr.t[:, :n], in_=rr.t[:, :n], func=AF.Exp, scale=cl.t[:, z, g:g + 1], bias=cl.t[:, z, g:g + 1]), r=[rr.b, cl.b], w=[rr.b])

    def lru_gates_b2(u, ub, rr, ii, aa, z, g, n):
        I("act", lambda e: e.activation(out=rr.t[:, :n], in_=rr.t[:, :n], func=AF.Sqrt, scale=-0.25, bias=0.25), r=[rr.b], w=[rr.b])

    def lru_gates_b3(u, ub, rr, ii, aa, z, g, n, split=False):
        I("dve", lambda e: e.scalar_tensor_tensor(out=ii.t[:, :n], in0=ii.t[:, :n], scalar=1.0, in1=u.t[:, :n], op0=ALU.add, op1=ALU.mult), r=[ii.b, u.b], w=[ii.b])
        I("pool", lambda e: e.tensor_tensor(out=ii.t[:, :n], in0=ii.t[:, :n], in1=rr.t[:, :n], op=ALU.mult), r=[ii.b, rr.b], w=[ii.b])

    def lru_gates_b(u, ub, rr, ii, aa, z, g, n, split=False):
        lru_gates_b1(u, ub, rr, ii, aa, z, g, n)
        lru_gates_b2(u, ub, rr, ii, aa, z, g, n)
        lru_gates_b3(u, ub, rr, ii, aa, z, g, n)

    def lru_gates(u, ub, rr, ii, aa, z, g, n):
        lru_gates_a(u, ub, rr, ii, aa, z, g, n)
        lru_gates_b(u, ub, rr, ii, aa, z, g, n)

    YOUT = Buf()

    def prepass():
        SXF = Buf()
        esA = contextlib.ExitStack()
        alloc_front(esA)
        zf = A["zf"]
        DVE_EVAC[0] = True
        wres = T(esA.enter_context(nc.sbuf_tensor("t_wres", [128, 8, 16, 128], BF16)))
        I("sp", lambda e: e.dma_start(out=wres.t[:], in_=WIN[24:32].rearrange("j p k c -> p j k c")), r=[WBi], w=[wres.b], dma=True)
        def front_fn(st):
            s0 = st * 512
            hTt = A["hT"][st % 2]
            return [(lambda ti=ti: front(xf_d[s0 + ti * 128:s0 + (ti + 1) * 128, :], hTt, ti * 128)) for ti in range(4)]

        def proj_fn(st):
            s0 = st * 512
            hTt = A["hT"][st % 2]

            def body(g):
                ps = PS[zk[0] % 2]
                z = zf[zk[0] % 3]
                zk[0] += 1
                for kc in range(16):
                    I("pe", lambda e: e.matmul(ps.t[:, :], lhsT=wres.t[:, g, kc, :], rhs=hTt.t[:, kc, :], start=(kc == 0), stop=(kc == 15)),
                      r=[wres.b, hTt.b], w=[ps.b], inc=(kc == 15))
                I("act", lambda e: e.activation(out=z.t[:, :], in_=ps.t[:, :], func=AF.Identity), r=[ps.b], w=[z.b])
                I("pool", lambda e: e.dma_start(out=XRF[g, :, s0:s0 + 512], in_=z.t[:, :]), r=[z.b], w=[SXF], dma=True)
            return [(lambda g=g: body(g)) for g in range(8)]

        pipelined_steps(SEQ_P // 512, front_fn, proj_fn)
        DVE_EVAC[0] = False
        S.barrier()
        esA.close()
        with contextlib.ExitStack() as es2:
            def sb2(name, shape, dt=F32):
                return T(es2.enter_context(nc.sbuf_tensor("t_pp" + name, list(shape), dt)))
            n = 1024
            NSG = SEQ_P // n
            NB = 4
            UF = dscr("UF", [8, 128, SEQ_P], F32)
            SUF = Buf()
            xrp = [sb2("xrp%d" % i, [128, n + 3]) for i in range(NB)]
            u = [sb2("u%d" % i, [128, n]) for i in range(NB)]
            ub = [sb2("ub%d" % i, [128, n], BF16) for i in range(NB)]
            rr = [sb2("rr%d" % i, [128, n]) for i in range(NB)]
            ii = [sb2("ii%d" % i, [128, n]) for i in range(NB)]
            aa = [sb2("aa%d" % i, [128, n]) for i in range(NB)]
            hh = [sb2("hh%d" % i, [128, n]) for i in range(NB)]
            rec = sb2("rec", [128, 2, 8, 8]); tm = sb2("tm", [128, 8])
            I("dve", lambda e: e.memset(rec.t[:], 0.0), w=[rec.b])
            work = []
            for g in range(8):
                for z in range(2):
                    segs = list(range((2048 * 7 - 127) // n + 1)) if z == 0 else list(range(NSG - 1, (2048 + 126) // n - 1, -1))
                    for si, s in enumerate(segs):
                        work.append((g, z, si, s, si == len(segs) - 1))

            def stage_a(it, part):
                g, z, si, s, last_ = work[it]
                k = it % NB
                x_, u_, ub_, rr_, ii_, aa_ = xrp[k], u[k], ub[k], rr[k], ii[k], aa[k]
                if part == 2:
                    lru_gates_a(u_, ub_, rr_, ii_, aa_, z, g, n)
                    return
                if z == 0 or s > (2048 * 7 - 127) // n:
                    a0 = s * n - 2
                    c_lo = 2 if s == 0 else 0
                    c_hi = n + 2 if s == NSG - 1 else n + 3
                    if s == 0:
                        I("dve", lambda e: e.memset(x_.t[:, 0:2], 0.0), w=[x_.b])
                    if s == NSG - 1:
                        I("dve", lambda e: e.memset(x_.t[:, n + 2:n + 3], 0.0), w=[x_.b])
                    I("sp", lambda e: e.dma_start(out=x_.t[:, c_lo:c_hi], in_=XRF[g, :, a0 + c_lo:a0 + c_hi]), r=[SXF], w=[x_.b], dma=True)
                    lru_conv(x_, u_, ub_, g, n, cast="act")
                    if z == 0:
                        I("pool", lambda e: e.dma_start(out=UF[g, :, s * n:(s + 1) * n], in_=u_.t[:]), r=[u_.b], w=[SUF], dma=True)
                else:
                    I("sp", lambda e: e.dma_start(out=u_.t[:], in_=UF[g, :, s * n:(s + 1) * n]), r=[SUF], w=[u_.b], dma=True)
                    I("act", lambda e: e.activation(out=ub_.t[:], in_=u_.t[:], func=AF.Identity), r=[u_.b], w=[ub_.b])

            def stage_b(it, ph):
                g, z, si, s, last_ = work[it]
                k = it % NB
                u_, ub_, rr_, ii_, aa_, hh_, hp = u[k], ub[k], rr[k], ii[k], aa[k], hh[k], hh[(it - 1) % NB]
                if ph == 1:
                    lru_gates_b1(u_, ub_, rr_, ii_, aa_, z, g, n)
                    return
                if ph == 2:
                    lru_gates_b2(u_, ub_, rr_, ii_, aa_, z, g, n)
                    return
                lru_gates_b3(u_, ub_, rr_, ii_, aa_, z, g, n)
                if z == 0:
                    init = 0.0 if si == 0 else hp.t[:, n - 1:n]
                    I("dve", lambda e: e.tensor_tensor_scan(out=hh_.t[:], data0=aa_.t[:], data1=ii_.t[:], initial=init, op0=ALU.mult, op1=ALU.add),
                      r=[aa_.b, ii_.b, hp.b], w=[hh_.b])
                    for j_ in range(1, 8):
                        tk_ = 2048 * j_ - 127
                        if tk_ // n == s:
                            I("dve", lambda e: e.tensor_copy(out=rec.t[:, 0, g, j_:j_ + 1], in_=hh_.t[:, tk_ % n:tk_ % n + 1]), r=[hh_.b], w=[rec.b])
                else:
                    init = 0.0 if si == 0 else hp.t[:, 0:1]
                    I("dve", lambda e: e.tensor_tensor_scan(out=hh_.t[:, ::-1], data0=aa_.t[:, ::-1], data1=ii_.t[:, ::-1], initial=init, op0=ALU.mult, op1=ALU.add),
                      r=[aa_.b, ii_.b, hp.b], w=[hh_.b])
                    for j_ in range(0, 7):
                        tk_ = 2048 * (j_ + 1) + 126
                        if tk_ // n == s:
                            I("dve", lambda e: e.tensor_copy(out=rec.t[:, 1, g, j_:j_ + 1], in_=hh_.t[:, tk_ % n:tk_ % n + 1]), r=[hh_.b], w=[rec.b])
                if last_:
                    Hx = HF if z == 0 else HB
                    I("dve", lambda e: e.tensor_tensor(out=tm.t[:], in0=rec.t[:, z, g, :], in1=sel.t[:, z, :], op=ALU.mult), r=[rec.b, sel.b], w=[tm.b])
                    I("dve", lambda e: e.reduce_sum(out=Hx.t[:, g:g + 1], in_=tm.t[:], axis=AX.X), r=[tm.b], w=[Hx.b])

            LA = 2
            for it in range(min(LA, len(work))):
                stage_a(it, 1)
                stage_a(it, 2)
            for it in range(len(work)):
                if it + LA < len(work):
                    stage_a(it + LA, 1)
                for ph in (1, 2, 3):
                    stage_b(it, ph)
                if it + LA < len(work):
                    stage_a(it + LA, 2)
            S.barrier()

    try:
        run_seq("s", xs_d, CH, 0, CH, 0, cos_s_d, sin_s_d, ys_d, False, None, None)
        if prompt_on:
            if prepass_on:
                prepass()
            else:
                I("dve", lambda e: e.memset(HF.t[:], 0.0), w=[HF.b])
                I("dve", lambda e: e.memset(HB.t[:], 0.0), w=[HB.b])
            run_seq("p", xp_d, 4352, 1024, 2304, 128, cos_p_d, sin_p_d, yp_d, True, HF, HB)
    except _Stop:
        pass
    S.finish()
    return nc, es


def _rope_tables(pos):
    half = 16
    inv = (500000.0 ** (-np.arange(half, dtype=np.float32) / half)).astype(np.float32)
    ang = pos.astype(np.float32)[None, :] * inv[:, None]
    c = np.cos(ang).astype(np.float32)
    s = np.sin(ang).astype(np.float32)
    return np.concatenate([c, c], 0), np.concatenate([s, s], 0)


def _chunk16(v):
    return np.ascontiguousarray(v.reshape(-1, 128).T)


PROMPT_ON = True
PREPASS_ON = True


def kernel(x_prompt, x_sample, g_mix, w_in, w_out, g_attn_out, g_lru_out, conv_rg_w, conv_rg_b,
           rg_w_a, rg_b_a, rg_w_x, rg_b_x, rg_lam, g_mlp, w_up, conv_ff_w, conv_ff_b, w_down, g_final):
    f = np.float32
    xpf = np.ascontiguousarray(x_prompt[0], dtype=f)
    common = {
        "xf": xpf,
        "w_in": np.ascontiguousarray(w_in[0], f), "w_out": np.ascontiguousarray(w_out[0], f),
        "w_up": np.ascontiguousarray(w_up[0], f), "w_down": np.ascontiguousarray(w_down[0], f),
        "g_mix": _chunk16(g_mix[0]), "g_mlp": _chunk16(g_mlp[0]),
        "g_mo": _chunk16(np.concatenate([g_attn_out[0], g_lru_out[0]])),
        "crw": np.ascontiguousarray(conv_rg_w[0].reshape(4, 8, 128).transpose(2, 1, 0)),
        "crb": _chunk16(conv_rg_b[0]),
        "rg_w_a": np.ascontiguousarray(rg_w_a[0], f), "rg_w_x": np.ascontiguousarray(rg_w_x[0], f),
        "rg_b_a": np.ascontiguousarray(rg_b_a[0].reshape(2, 8, 128).transpose(2, 0, 1)),
        "rg_b_x": np.ascontiguousarray(rg_b_x[0].reshape(2, 8, 128).transpose(2, 0, 1)),
        "rg_lam": np.ascontiguousarray(rg_lam[0].reshape(2, 8, 128).transpose(2, 0, 1)),
        "cfw": np.ascontiguousarray(conv_ff_w[0].reshape(3, 96, 128).transpose(2, 1, 0)),
        "cfb": _chunk16(conv_ff_b[0]),
        "gfin": np.ascontiguousarray(np.broadcast_to(g_final[None, :], (128, D)), f),
    }
    cs, sn = _rope_tables(np.arange(CH))
    common["cos_s"], common["sin_s"] = cs, sn
    pmm = np.zeros((128, 128), f)
    for m in range(16):
        pmm[m + 16, m] = -1.0
        pmm[m, m + 16] = 1.0
    common["pm"] = pmm
    jj = np.arange(128)[:, None]
    cc = np.arange(512)[None, :]
    common["bmask"] = ((cc - jj >= 0) & (cc - jj <= 128)).astype(f)
    common["ident"] = np.eye(128, dtype=f)
    if not (PROMPT_ON and PREPASS_ON):
        common["xf"] = xpf[:128]
    in_maps = []
    for c in range(NCORE):
        m = dict(common)
        m["xs"] = np.ascontiguousarray(x_sample[c], f)
        a0 = c * CH - 1152
        pos = np.arange(a0, a0 + 4352)
        valid = (pos >= 0) & (pos < SEQ_P)
        xp = np.zeros((4352, D), f)
        xp[valid] = xpf[pos[valid]]
        m["xp"] = xp
        cp, sp_ = _rope_tables(np.where(valid, pos, 0))
        m["cos_p"], m["sin_p"] = cp, sp_
        kb = np.zeros((128, 3, 16, 34), f)
        for b, dil in enumerate(DILS):
            L = 4352 // dil
            for r in range(dil):
                for kbi in range((L + 127) // 128):
                    l = kbi * 128 + np.arange(128)
                    t = r + dil * l
                    ok = (l < L) & valid[np.minimum(t, 4351)]
                    kb[:, b, r, kbi] = np.where(ok, 0.0, -30000.0)
        m["kbias"] = kb
        sel = np.zeros((128, 2, 8), f)
        if c > 0:
            sel[:, 0, c] = 1.0
        if c < 7:
            sel[:, 1, c] = 1.0
        m["sel"] = sel
        vm = np.ones((128, 2), f)
        if c == 0:
            vm[:, 0] = 0.0
        if c == 7:
            vm[:, 1] = 0.0
        m["vmask"] = vm
        tm = np.ones((128, 256), f)
        tm[:, 0:128] = ((c * CH - 128 + np.arange(128)) >= 0).astype(f)[None, :]
        tm[:, 128:256] = ((c * CH + CH + np.arange(128)) < SEQ_P).astype(f)[None, :]
        m["tmask"] = tm
        in_maps.append(m)
    nc, es = build(PROMPT_ON, PREPASS_ON)
    res = run_bass_kernel_spmd(nc, in_maps, core_ids=list(range(NCORE)))
    try:
        es.close()
    except AssertionError:
        pass
    if DEBUG["scratch_out"]:
        DEBUG["res"] = res
        DEBUG["in_maps"] = in_maps
    yp = np.concatenate([np.asarray(res.results[c]["yp"], f) for c in range(NCORE)], 0)[None]
    ys = np.stack([np.asarray(res.results[c]["ys"], f) for c in range(NCORE)], 0)
    return (yp, ys)
```

```python
import contextlib
import math
import numpy as np
import concourse.bass as bass
import concourse.mybir as mybir
from concourse.bass_utils import run_bass_kernel_spmd

F32 = mybir.dt.float32
BF16 = mybir.dt.bfloat16
AF = mybir.ActivationFunctionType
ALU = mybir.AluOpType
AX = mybir.AxisListType

D = 2048
NCORE = 8
SEQ_P = 16384
CH = 2048
EPS = 1e-6
NR = 16
NRQ = {"sp": 16, "pool": 4}


class Buf:
    __slots__ = ("w", "r", "excl")

    def __init__(self, excl=False):
        self.w = None
        self.r = {}
        self.excl = excl


class Sched:
    def __init__(self, nc, es):
        self.nc = nc
        self.E = {"pe": nc.tensor, "act": nc.scalar, "dve": nc.vector, "pool": nc.gpsimd, "sp": nc.sync}
        self.cs = {k: es.enter_context(nc.semaphore("c_" + k)) for k in ("pe", "act", "dve", "pool")}
        self.cc = {k: 0 for k in self.cs}
        self.ring = {q: [es.enter_context(nc.semaphore("d_%s%d" % (q, i))) for i in range(NRQ[q])] for q in ("sp", "pool")}
        self.rt = {q: [0] * NRQ[q] for q in self.ring}
        self.rk = {q: 0 for q in self.ring}
        self.waited = {k: {} for k in self.E}
        self.pr = []
        self.pw = []

    def _wait(self, e, sem, val):
        k = id(sem)
        if self.waited[e].get(k, 0) < val:
            self.E[e].wait_ge(sem, val)
            self.waited[e][k] = val

    def I(self, e, fn, r=(), w=(), dma=False, inc=True):
        deps = []
        for b in r:
            if b.w is not None:
                deps.append(b.w)
            if b.excl:
                deps.extend(b.r.values())
        for b in w:
            if b.w is not None:
                deps.append(b.w)
            deps.extend(b.r.values())
        for sem, val in deps:
            if e == "pe" and sem is self.cs["pe"]:
                continue
            self._wait(e, sem, val)
        eng = self.E[e]
        if dma:
            i = self.rk[e] % NRQ[e]
            self.rk[e] += 1
            sem = self.ring[e][i]
            if self.rt[e][i] > 0:
                self._wait(e, sem, self.rt[e][i])
            inst = fn(eng)
            self.rt[e][i] += 16
            inst.then_inc(sem, 16)
            t = (sem, self.rt[e][i])
        else:
            inst = fn(eng)
            if not inc:
                self.pr += list(r)
                self.pw += list(w)
                return
            self.cc[e] += 1
            inst.then_inc(self.cs[e], 1)
            t = (self.cs[e], self.cc[e])
            if e == "pe":
                r = list(r) + self.pr
                w = list(w) + self.pw
                self.pr = []
                self.pw = []
        for b in r:
            b.r[id(t[0])] = t
        for b in w:
            b.w = t
            b.r = {}

    def barrier(self):
        for e in self.E:
            for k in self.cs:
                if k != e and self.cc[k] > 0:
                    self._wait(e, self.cs[k], self.cc[k])
            for q in self.ring:
                for i in range(NRQ[q]):
                    if self.rt[q][i] > 0:
                        self._wait(e, self.ring[q][i], self.rt[q][i])

    def finish(self):
        for q in self.ring:
            for i in range(NRQ[q]):
                if self.rt[q][i] > 0:
                    self._wait("sp", self.ring[q][i], self.rt[q][i])
        for k in self.cs:
            if self.cc[k] > 0:
                self._wait("sp", self.cs[k], self.cc[k])


class T:
    def __init__(self, ap, excl=False):
        self.t = ap
        self.b = Buf(excl)


DILS = (1, 4, 16)
DEBUG = {"stop": 99, "scratch_out": False, "outs": (), "maxsteps": 999, "sub": 9}


class _Stop(Exception):
    pass


def build(prompt_on=True, prepass_on=True):
    nc = bass.Bass("TRN2", target_bir_lowering=False)
    es = contextlib.ExitStack()
    S = Sched(nc, es)
    I = S.I

    def din(name, shape, dt=F32):
        return nc.dram_tensor(name, list(shape), dt, kind="ExternalInput").ap()

    def dscr(name, shape, dt):
        return nc.dram_tensor(name, list(shape), dt, kind=("ExternalOutput" if name in DEBUG["outs"] else "Internal")).ap()

    def stage(n):
        if DEBUG["stop"] <= n:
            raise _Stop()

    xs_d = din("xs", [CH, D])
    xp_d = din("xp", [4352, D])
    xf_d = din("xf", [SEQ_P if prepass_on and prompt_on else 128, D])
    w_in_d = din("w_in", [D, 5120])
    w_out_d = din("w_out", [D, D])
    w_up_d = din("w_up", [D, 12288])
    w_dn_d = din("w_down", [6144, D])
    g_mix_d = din("g_mix", [128, 16])
    g_mlp_d = din("g_mlp", [128, 16])
    g_mo_d = din("g_mo", [128, 16])
    crw_d = din("crw", [128, 8, 4])
    crb_d = din("crb", [128, 8])
    wa_d = din("rg_w_a", [2, 8, 128, 128])
    wx_d = din("rg_w_x", [2, 8, 128, 128])
    ba_d = din("rg_b_a", [128, 2, 8])
    bx_d = din("rg_b_x", [128, 2, 8])
    lam_d = din("rg_lam", [128, 2, 8])
    cfw_d = din("cfw", [128, 96, 3])
    cfb_d = din("cfb", [128, 96])
    gfin_d = din("gfin", [128, D])
    cos_s_d = din("cos_s", [32, CH])
    sin_s_d = din("sin_s", [32, CH])
    cos_p_d = din("cos_p", [32, 4352])
    sin_p_d = din("sin_p", [32, 4352])
    pm_d = din("pm", [128, 128])
    mask_d = din("bmask", [128, 512])
    ident_d = din("ident", [128, 128])
    kb_d = din("kbias", [128, 3, 16, 34])
    sel_d = din("sel", [128, 2, 8])
    vm_d = din("vmask", [128, 2])
    tm_d = din("tmask", [128, 256])
    ys_d = nc.dram_tensor("ys", [CH, D], F32, kind="ExternalOutput").ap()
    yp_d = nc.dram_tensor("yp", [CH, D], F32, kind="ExternalOutput").ap()

    WIN = dscr("WIN", [40, 128, 16, 128], BF16)
    WUP = dscr("WUP", [96, 128, 16, 128], BF16)
    WOUT = dscr("WOUT", [16, 128, D], BF16)
    WDN = dscr("WDN", [48, 128, D], BF16)
    XRF = dscr("XRF", [8, 128, SEQ_P], F32)

    def sb(name, shape, dt=F32):
        return T(es.enter_context(nc.sbuf_tensor("sb_" + name, list(shape), dt)))

    PS = [T(es.enter_context(nc.psum_tensor("ps%d" % i, [128, 512], F32)), True) for i in range(6)]
    PB = [T(es.enter_context(nc.psum_tensor("pb%d" % i, [128, 1024], BF16)), True) for i in range(2)]

    def load(dst, src, q="sp"):
        I(q, lambda e: e.dma_start(out=dst.t[:], in_=src), w=[dst.b], dma=True)

    gmix = sb("gmix", [128, 16]); load(gmix, g_mix_d[:, :])
    gmlp = sb("gmlp", [128, 16]); load(gmlp, g_mlp_d[:, :])
    gmo = sb("gmo", [128, 16]); load(gmo, g_mo_d[:, :])
    crw = sb("crw", [128, 8, 4]); load(crw, crw_d[:, :, :])
    crb = sb("crb", [128, 8]); load(crb, crb_d[:, :])
    ba = sb("ba", [128, 2, 8]); load(ba, ba_d[:, :, :])
    bx = sb("bx", [128, 2, 8]); load(bx, bx_d[:, :, :])
    lam = sb("lam", [128, 2, 8]); load(lam, lam_d[:, :, :])
    cfw = sb("cfw", [128, 96, 3]); load(cfw, cfw_d[:, :, :])
    cfb = sb("cfb", [128, 96]); load(cfb, cfb_d[:, :])
    gfin = sb("gfin", [128, D]); load(gfin, gfin_d[:, :])
    kbias = sb("kbias", [128, 3, 16, 34]); load(kbias, kb_d[:, :, :, :])
    sel = sb("sel", [128, 2, 8]); load(sel, sel_d[:, :, :])
    vmask = sb("vmask", [128, 2]); load(vmask, vm_d[:, :])
    tmask = sb("tmask", [128, 256]); load(tmask, tm_d[:, :])
    tmpc = sb("tmpc", [128, 512])
    bmask = sb("bmaskb", [128, 512], BF16)
    load(tmpc, mask_d[:, :])
    I("dve", lambda e: e.tensor_copy(out=bmask.t[:], in_=tmpc.t[:]), r=[tmpc.b], w=[bmask.b])
    ident = sb("identb", [128, 128], BF16)
    tmpi = sb("tmpi", [128, 128])
    load(tmpi, ident_d[:, :])
    I("dve", lambda e: e.tensor_copy(out=ident.t[:], in_=tmpi.t[:]), r=[tmpi.b], w=[ident.b])
    pm = sb("pmb", [128, 128], BF16)
    tmpp = sb("tmpp", [128, 128])
    load(tmpp, pm_d[:, :])
    I("dve", lambda e: e.tensor_copy(out=pm.t[:], in_=tmpp.t[:]), r=[tmpp.b], w=[pm.b])
    ones = sb("onesb", [128, 128], BF16)
    I("dve", lambda e: e.memset(ones.t[:], 1.0), w=[ones.b])
    cl = sb("cl", [128, 2, 8]); cl2 = sb("cl2", [128, 2, 8])
    I("act", lambda e: e.activation(out=cl.t[:], in_=lam.t[:], func=AF.Exp, scale=-1.0), r=[lam.b], w=[cl.b])
    I("act", lambda e: e.activation(out=cl.t[:], in_=cl.t[:], func=AF.Ln, bias=1.0), r=[cl.b], w=[cl.b])
    I("dve", lambda e: e.tensor_scalar(out=cl2.t[:], in0=cl.t[:], scalar1=-16.0, scalar2=None, op0=ALU.mult), r=[cl.b], w=[cl2.b])
    I("dve", lambda e: e.tensor_scalar(out=cl.t[:], in0=cl.t[:], scalar1=-8.0, scalar2=None, op0=ALU.mult), r=[cl.b], w=[cl.b])
    hba = sb("hba", [128, 2, 8]); hbx = sb("hbx", [128, 2, 8]); hcl = sb("hcl", [128, 2, 8])
    I("dve", lambda e: e.tensor_scalar(out=hba.t[:], in0=ba.t[:], scalar1=0.5, scalar2=None, op0=ALU.mult), r=[ba.b], w=[hba.b])
    I("dve", lambda e: e.tensor_scalar(out=hbx.t[:], in0=bx.t[:], scalar1=0.5, scalar2=None, op0=ALU.mult), r=[bx.b], w=[hbx.b])
    I("dve", lambda e: e.tensor_scalar(out=hcl.t[:], in0=cl.t[:], scalar1=0.5, scalar2=None, op0=ALU.mult), r=[cl.b], w=[hcl.b])
    wg = sb("wg", [128, 32, 128], BF16)
    esg = contextlib.ExitStack()
    wgt = T(esg.enter_context(nc.sbuf_tensor("wgt", [128, 16, 128], F32)))
    for gi, src in enumerate((wa_d, wx_d)):
        I("sp", lambda e, src=src: e.dma_start(out=wgt.t[:], in_=src.rearrange("z g i j -> i (z g) j")), w=[wgt.b], dma=True)
        I("dve", lambda e, gi=gi: e.tensor_copy(out=wg.t[:, gi * 16:(gi + 1) * 16, :], in_=wgt.t[:]), r=[wgt.b], w=[wg.b])

    S.barrier()
    esg.close()

    st4 = [sb("st4_%d" % i, [128, 4]) for i in range(3)]
    HF = sb("HF", [128, 8])
    HB = sb("HB", [128, 8])
    hbds = {"s": sb("hbd_s", [128, 16, 8], BF16), "p": sb("hbd_p", [128, 16, 8], BF16)}
    esp = contextlib.ExitStack()
    wst = [T(esp.enter_context(nc.sbuf_tensor("wst%d" % i, [128, 2048], F32))) for i in range(4)]
    wsb = [T(esp.enter_context(nc.sbuf_tensor("wsb%d" % i, [128, 2048], BF16))) for i in range(4)]
    cnt = [0]

    def prep(Wd, K, N, g, dst, tiled, WB):
        for kc in range(K // 128):
            for cb in range(0, N, 2048):
                cw = min(2048, N - cb)
                k = cnt[0] % 4
                cnt[0] += 1
                a, b = wst[k], wsb[k]
                I("sp", lambda e: e.dma_start(out=a.t[:, :cw], in_=Wd[kc * 128:(kc + 1) * 128, cb:cb + cw]), w=[a.b], dma=True)
                if g is not None:
                    I("dve", lambda e: e.tensor_scalar(out=b.t[:, :cw], in0=a.t[:, :cw], scalar1=g.t[:, kc:kc + 1], scalar2=None, op0=ALU.mult), r=[a.b, g.b], w=[b.b])
                else:
                    I("dve", lambda e: e.tensor_copy(out=b.t[:, :cw], in_=a.t[:, :cw]), r=[a.b], w=[b.b])
                if tiled:
                    j0 = cb // 128
                    I("pool", lambda e: e.dma_start(out=dst[j0:j0 + cw // 128, :, kc, :].rearrange("j p c -> p j c"),
                                                    in_=b.t[:, :cw].rearrange("p (j c) -> p j c", c=128)), r=[b.b], w=[WB], dma=True)
                else:
                    I("pool", lambda e: e.dma_start(out=dst[kc, :, cb:cb + cw], in_=b.t[:, :cw]), r=[b.b], w=[WB], dma=True)
                yield None

    WBi, WBo, WBu, WBd = Buf(), Buf(), Buf(), Buf()
    import itertools
    for _ in prep(w_in_d, D, 5120, gmix, WIN, True, WBi):
        pass
    prep_rest = itertools.chain(prep(w_out_d, D, D, gmo, WOUT, False, WBo), prep(w_up_d, D, 12288, gmlp, WUP, True, WBu),
                                prep(w_dn_d, 6144, D, None, WDN, False, WBd))

    A = {}
    uid = [0]

    def alloc_front(esx, full=True, hw=512):
        uid[0] += 1
        p = "f%d_" % uid[0]

        def sbx(name, shape, dt=F32):
            return T(esx.enter_context(nc.sbuf_tensor(p + name, list(shape), dt)))
        A["hsb"] = [sbx("hs%d" % i, [128, D], BF16) for i in range(3 if full else 2)]
        A["junk"] = sbx("junk", [128, D], BF16)
        A["hT"] = [sbx("hT%d" % i, [128, 16, hw], BF16) for i in range(2)]
        if full:
            A["xtb"] = [sbx("xt%d" % i, [128, D]) for i in range(3)]
            A["wch"] = [sbx("wch%d" % i, [128, 16, 128], BF16) for i in range(4)]
            A["zb"] = [sbx("zb%d" % i, [128, 512], BF16) for i in range(3)]
            A["zf"] = [sbx("zf%d" % i, [128, 512]) for i in range(3)]
            A["r1"] = sbx("r1", [32, 512]); A["r2"] = sbx("r2", [32, 512])
            A["cst"] = [sbx("cst%d" % i, [32, 512]) for i in range(2)]
            A["snt"] = [sbx("snt%d" % i, [32, 512]) for i in range(2)]
    wk = [0]
    tk = [0]
    DVE_EVAC = [False]

    def front(xsrc_rows, hTt, col0):
        k = tk[0] % 3
        tk[0] += 1
        xt, hs, s4 = A["xtb"][k], A["hsb"][k], st4[k]
        junk = A["junk"]
        I("sp", lambda e: e.dma_start(out=xt.t[:], in_=xsrc_rows), w=[xt.b], dma=True)
        I("act", lambda e: e.activation(out=junk.t[:], in_=xt.t[:], func=AF.Square, accum_out=s4.t[:, 0:1]), r=[xt.b], w=[junk.b, s4.b])
        I("dve", lambda e: e.tensor_scalar(out=s4.t[:, 1:2], in0=s4.t[:, 0:1], scalar1=1.0 / D, scalar2=EPS, op0=ALU.mult, op1=ALU.add), r=[s4.b], w=[s4.b])
        I("act", lambda e: e.activation(out=s4.t[:, 2:3], in_=s4.t[:, 1:2], func=AF.Sqrt), r=[s4.b], w=[s4.b])
        I("dve", lambda e: e.reciprocal(out=s4.t[:, 3:4], in_=s4.t[:, 2:3]), r=[s4.b], w=[s4.b])
        I("act", lambda e: e.activation(out=hs.t[:], in_=xt.t[:], func=AF.Identity, scale=s4.t[:, 3:4]), r=[xt.b, s4.b], w=[hs.b])
        transpose16(hs, hTt, col0)

    def transpose16(hs, hTt, col0):
        for half in range(2):
            pb = PB[half]
            for kk in range(8):
                kc = half * 8 + kk
                I("pe", lambda e: e.transpose(pb.t[:, kk * 128:(kk + 1) * 128], hs.t[:, kc * 128:(kc + 1) * 128], ident.t[:]),
                  r=[hs.b, ident.b], w=[pb.b], inc=(kk == 7))
            eng = "dve" if (half == 0 or DVE_EVAC[0]) else "act"
            src = pb.t[:].rearrange("p (k c) -> p k c", c=128)
            dst = hTt.t[:, half * 8:(half + 1) * 8, col0:col0 + 128]
            if eng == "dve":
                I("dve", lambda e: e.tensor_copy(out=dst, in_=src), r=[pb.b], w=[hTt.b])
            else:
                I("act", lambda e: e.activation(out=dst, in_=src, func=AF.Identity), r=[pb.b], w=[hTt.b])

    def proj_chunk(Wt, j, hTt, W, ps, WB):
        wt = A["wch"][wk[0] % 4]
        wk[0] += 1
        I("sp", lambda e: e.dma_start(out=wt.t[:], in_=Wt[j]), r=[WB], w=[wt.b], dma=True)
        for kc in range(16):
            I("pe", lambda e: e.matmul(ps.t[:, :W], lhsT=wt.t[:, kc, :], rhs=hTt.t[:, kc, :W], start=(kc == 0), stop=(kc == 15)),
              r=[wt.b, hTt.b], w=[ps.b], inc=(kc == 15))

    zk = [0]

    def pipelined_steps(nsteps, front_fn, proj_fn):
        for f in front_fn(0):
            f()
        for st in range(nsteps):
            nxt = front_fn(st + 1) if st + 1 < nsteps else []
            pj = proj_fn(st)
            per = max(1, len(pj) // max(1, len(nxt)))
            for idx, p in enumerate(pj):
                if nxt and idx % per == 0:
                    nxt.pop(0)()
                p()
            for f in nxt:
                f()

    def run_seq(tag, x_d, R, Q0, TQ, M0, cos_d, sin_d, y_d, is_prompt, HF, HB):
        QT = dscr(tag + "QT", [8, 128, TQ], BF16)
        KT = dscr(tag + "KT", [8, 128, R], BF16)
        VT = dscr(tag + "VT", [8, 128, R], BF16)
        XRT = dscr(tag + "XRT", [8, 128, TQ], F32)
        GT = dscr(tag + "GT", [8, 128, TQ], F32)
        X1 = dscr(tag + "X1", [TQ, D], F32)
        X2 = dscr(tag + "X2", [CH, D], F32)
        H2 = dscr(tag + "H2", [128, 16, TQ], BF16)
        SQT, SKT, SVT, SXR, SGT, SX1, SX2, SH2 = (Buf() for _ in range(8))
        hbd = hbds[tag]
        I("dve", lambda e: e.memset(hbd.t[:], 0.0), w=[hbd.b])
        bnd = {}
        for gi_ in range(4):
            for side_, tok_ in ((0, M0 + 512 * gi_ - 1), (1, M0 + 512 * gi_ + 512)):
                if 0 <= tok_ < TQ:
                    bnd[tok_] = 2 * gi_ + side_

        stage(2)
        esA = contextlib.ExitStack()
        alloc_front(esA)
        zb, zf, r1, r2, cst, snt = A["zb"], A["zf"], A["r1"], A["r2"], A["cst"], A["snt"]
        nsteps = min((R + 511) // 512, DEBUG["maxsteps"])

        def front_fn(st):
            s0 = st * 512
            W = min(512, R - s0)
            hTt = A["hT"][st % 2]
            return [(lambda ti=ti: front(x_d[s0 + ti * 128:s0 + (ti + 1) * 128, :], hTt, ti * 128)) for ti in range(W // 128)]

        def proj_fn(st):
            s0 = st * 512
            W = min(512, R - s0)
            hTt = A["hT"][st % 2]
            inq = (s0 >= Q0) and (s0 < Q0 + TQ)
            wq = min(W, Q0 + TQ - s0) if inq else 0
            ct, sn = cst[st % 2], snt[st % 2]
            chunks = list(range(8, 24)) + (list(range(0, 8)) + list(range(24, 40)) if inq else [])

            def body(j, first):
                if first:
                    I("sp", lambda e: e.dma_start(out=ct.t[:, :W], in_=cos_d[:, s0:s0 + W]), w=[ct.b], dma=True)
                    I("sp", lambda e: e.dma_start(out=sn.t[:, :W], in_=sin_d[:, s0:s0 + W]), w=[sn.b], dma=True)
                ps = PS[zk[0] % 2]
                k3 = zk[0] % 3
                zk[0] += 1
                proj_chunk(WIN, j, hTt, W, ps, WBi)
                if tag == "s":
                    next(prep_rest, None)
                if j < 16:
                    z = zb[k3]
                    I("act", lambda e: e.activation(out=z.t[:, :W], in_=ps.t[:, :W], func=AF.Identity), r=[ps.b], w=[z.b])
                    rp = PS[2]
                    I("pe", lambda e: e.matmul(rp.t[:, :W], lhsT=pm.t[:, :], rhs=z.t[:, :W], start=True, stop=True), r=[pm.b, z.b], w=[rp.b])
                    I("dve", lambda e: e.tensor_tensor(out=r1.t[:, :W], in0=ps.t[0:32, :W], in1=ct.t[:, :W], op=ALU.mult), r=[ps.b, ct.b], w=[r1.b])
                    I("dve", lambda e: e.tensor_tensor(out=r2.t[:, :W], in0=rp.t[0:32, :W], in1=sn.t[:, :W], op=ALU.mult), r=[rp.b, sn.b], w=[r2.b])
                    I("dve", lambda e: e.tensor_tensor(out=z.t[0:32, :W], in0=r1.t[:, :W], in1=r2.t[:, :W], op=ALU.add), r=[r1.b, r2.b], w=[z.b])
                    if j < 8:
                        I("pool", lambda e: e.dma_start(out=QT[j, :, s0 - Q0:s0 - Q0 + wq], in_=z.t[:, :wq]), r=[z.b], w=[SQT], dma=True)
                    else:
                        I("pool", lambda e: e.dma_start(out=KT[j - 8, :, s0:s0 + W], in_=z.t[:, :W]), r=[z.b], w=[SKT], dma=True)
                elif j < 24:
                    z = zb[k3]
                    I("act", lambda e: e.activation(out=z.t[:, :W], in_=ps.t[:, :W], func=AF.Identity), r=[ps.b], w=[z.b])
                    I("pool", lambda e: e.dma_start(out=VT[j - 16, :, s0:s0 + W], in_=z.t[:, :W]), r=[z.b], w=[SVT], dma=True)
                else:
                    z = zf[k3]
                    I("act", lambda e: e.activation(out=z.t[:, :W], in_=ps.t[:, :W], func=AF.Identity), r=[ps.b], w=[z.b])
                    if j < 32:
                        I("pool", lambda e: e.dma_start(out=XRT[j - 24, :, s0 - Q0:s0 - Q0 + wq], in_=z.t[:, :wq]), r=[z.b], w=[SXR], dma=True)
                    else:
                        I("pool", lambda e: e.dma_start(out=GT[j - 32, :, s0 - Q0:s0 - Q0 + wq], in_=z.t[:, :wq]), r=[z.b], w=[SGT], dma=True)
            return [(lambda j=j, f=(i == 0): body(j, f)) for i, j in enumerate(chunks)]

        pipelined_steps(nsteps, front_fn, proj_fn)

        if tag == "s":
            for _ in prep_rest:
                pass
        S.barrier()
        esA.close()
        if tag == "s":
            esp.close()
        esq = contextlib.ExitStack()
        mixT = T(esq.enter_context(nc.sbuf_tensor("t_" + tag + "mixT", [128, 16, TQ], BF16)))

        stage(3)
        with contextlib.ExitStack() as es2:
            def sb2(name, shape, dt=F32):
                return T(es2.enter_context(nc.sbuf_tensor("t_" + tag + name, list(shape), dt)))
            RP = R + (1792 if is_prompt else 0)
            kT = [sb2("kT%d" % i, [128, RP], BF16) for i in range(2)]
            vT = [sb2("vT%d" % i, [128, RP], BF16) for i in range(1)]
            if RP > R:
                for t_ in kT + vT:
                    I("dve", lambda e: e.memset(t_.t[:, R:RP], 0.0), w=[t_.b])
            qT = [sb2("qT%d" % i, [128, TQ], BF16) for i in range(2)]
            nblk = [((R // d) + 127) // 128 for d in DILS]
            vc = [sb2("vc%d" % b, [128, DILS[b] * nblk[b], 128], BF16) for b in range(3)]
            pt = [sb2("pt%d" % i, [128, 256], BF16) for i in range(4)]
            rd = [sb2("rd%d" % i, [128, 512]) for i in range(2)]
            pk = 0
            sc = 1.0 / math.sqrt(128.0)
            for h in range(8):
                k_, v_, q_ = kT[h % 2], vT[0], qT[h % 2]
                I("sp", lambda e: e.dma_start(out=k_.t[:, :R], in_=KT[h]), r=[SKT], w=[k_.b], dma=True)
                I("sp", lambda e: e.dma_start(out=v_.t[:, :R], in_=VT[h]), r=[SVT], w=[v_.b], dma=True)
                I("sp", lambda e: e.dma_start(out=q_.t[:], in_=QT[h]), r=[SQT], w=[q_.b], dma=True)
                for b, dil in enumerate(DILS):
                    L = R // dil
                    items = [(r, kb) for r in range(dil) for kb in range(nblk[b])]
                    for i0 in range(0, len(items), 8):
                        grp = items[i0:i0 + 8]
                        pb = PB[(i0 // 8) % 2]
                        for n_, (r, kb) in enumerate(grp):
                            k0 = kb * 128
                            nk = 128
                            a0 = r + dil * k0
                            I("pe", lambda e: e.transpose(pb.t[:nk, n_ * 128:(n_ + 1) * 128], v_.t[:, a0:a0 + dil * (nk - 1) + 1:dil], ident.t[:]),
                              r=[v_.b, ident.b], w=[pb.b], inc=(n_ == len(grp) - 1))
                        for n_, (r, kb) in enumerate(grp):
                            nk = 128
                            idx = r * nblk[b] + kb
                            eng = "act" if (i0 // 8) % 2 else "dve"
                            if eng == "dve":
                                I("dve", lambda e: e.tensor_copy(out=vc[b].t[:nk, idx, :], in_=pb.t[:nk, n_ * 128:(n_ + 1) * 128]), r=[pb.b], w=[vc[b].b])
                            else:
                                I("act", lambda e: e.activation(out=vc[b].t[:nk, idx, :], in_=pb.t[:nk, n_ * 128:(n_ + 1) * 128], func=AF.Identity), r=[pb.b], w=[vc[b].b])
                nbank = (TQ + 511) // 512
                for qb in range(nbank):
                    b0 = qb * 512
                    Wq = min(512, TQ - b0)
                    num, den = PS[2 + (qb % 2) * 2], PS[3 + (qb % 2) * 2]
                    blocks = []
                    for b, dil in enumerate(DILS):
                        L = R // dil
                        for r in range(dil):
                            i0 = (r - (Q0 + b0)) % dil
                            nq = len(range(i0, Wq, dil))
                            if nq <= 0:
                                continue
                            lq0 = (Q0 + b0 + i0 - r) // dil
                            lo = max(0, lq0 - 64)
                            hi = min(L - 1, lq0 + nq - 1 + 64)
                            for kb in range(lo // 128, hi // 128 + 1):
                                k0 = kb * 128
                                nk = 128
                                qa = max(lq0, k0 - 64)
                                qe = min(lq0 + nq, k0 + min(128, L - k0) + 64)
                                n = qe - qa
                                if n <= 0:
                                    continue
                                blocks.append((b, dil, r, kb, k0, nk, qa, n))

                    def emit_S(i):
                        b, dil, r, kb, k0, nk, qa, n = blocks[i]
                        sp_ = PS[(pk + i) % 2]
                        p_ = pt[(pk + i) % 4]
                        ka = r + dil * k0
                        qc = r + dil * qa - Q0
                        I("pe", lambda e: e.matmul(sp_.t[:nk, :n], lhsT=k_.t[:, ka:ka + dil * (nk - 1) + 1:dil],
                                                   rhs=q_.t[:, qc:qc + dil * (n - 1) + 1:dil], start=True, stop=True),
                          r=[k_.b, q_.b], w=[sp_.b])
                        if is_prompt:
                            I("act", lambda e: e.activation(out=p_.t[:nk, :n], in_=sp_.t[:nk, :n], func=AF.Exp, scale=sc,
                                                            bias=kbias.t[:nk, b, r, kb:kb + 1]), r=[sp_.b, kbias.b], w=[p_.b])
                        else:
                            I("act", lambda e: e.activation(out=p_.t[:nk, :n], in_=sp_.t[:nk, :n], func=AF.Exp, scale=sc), r=[sp_.b], w=[p_.b])
                        off = qa - k0 + 64
                        meng = "dve" if i % 3 else "pool"
                        I(meng, lambda e: e.tensor_tensor(out=p_.t[:nk, :n], in0=p_.t[:nk, :n], in1=bmask.t[:nk, off:off + n], op=ALU.mult),
                          r=[p_.b, bmask.b], w=[p_.b])

                    def emit_PV(i):
                        b, dil, r, kb, k0, nk, qa, n = blocks[i]
                        p_ = pt[(pk + i) % 4]
                        c0 = r + dil * qa - Q0 - b0
                        idx = r * nblk[b] + kb
                        I("pe", lambda e: e.matmul(num.t[:, c0:c0 + dil * (n - 1) + 1:dil], lhsT=vc[b].t[:nk, idx, :], rhs=p_.t[:nk, :n],
                                                   start=(i == 0), stop=False, skip_group_check=True), r=[vc[b].b, p_.b], w=[num.b], inc=False)
                        I("pe", lambda e: e.matmul(den.t[:, c0:c0 + dil * (n - 1) + 1:dil], lhsT=ones.t[:nk, :], rhs=p_.t[:nk, :n],
                                                   start=(i == 0), stop=False, skip_group_check=True), r=[ones.b, p_.b], w=[den.b])

                    emit_S(0)
                    for i in range(len(blocks)):
                        if i + 1 < len(blocks):
                            emit_S(i + 1)
                        emit_PV(i)
                    pk += len(blocks)
                    rd_ = rd[qb % 2]
                    I("dve", lambda e: e.tensor_scalar(out=rd_.t[:, :Wq], in0=den.t[:, :Wq], scalar1=1e-30, scalar2=None, op0=ALU.add), r=[den.b], w=[rd_.b])
                    I("dve", lambda e: e.reciprocal(out=rd_.t[:, :Wq], in_=rd_.t[:, :Wq]), r=[rd_.b], w=[rd_.b])
                    I("dve", lambda e: e.tensor_tensor(out=mixT.t[:, h, b0:b0 + Wq], in0=num.t[:, :Wq], in1=rd_.t[:, :Wq], op=ALU.mult),
                      r=[num.b, rd_.b], w=[mixT.b])
            S.barrier()

        stage(4)
        with contextlib.ExitStack() as es2:
            def sb2(name, shape, dt=F32):
                return T(es2.enter_context(nc.sbuf_tensor("t_" + tag + name, list(shape), dt)))
            xrp = sb2("xrp", [128, TQ + 3])
            gt = sb2("gt", [128, TQ])
            u = sb2("u", [128, TQ])
            ub = sb2("ub", [128, TQ], BF16)
            rr = sb2("rr", [128, TQ])
            ii = sb2("ii", [128, TQ])
            aa = sb2("aa", [128, TQ])
            hh = [sb2("hh%d" % i, [128, TQ]) for i in range(2)]
            I("dve", lambda e: e.memset(xrp.t[:, 0:2], 0.0), w=[xrp.b])
            I("dve", lambda e: e.memset(xrp.t[:, TQ + 2:TQ + 3], 0.0), w=[xrp.b])
            for g in range(8):
                I("sp", lambda e: e.dma_start(out=xrp.t[:, 2:TQ + 2], in_=XRT[g]), r=[SXR], w=[xrp.b], dma=True)
                I("sp", lambda e: e.dma_start(out=gt.t[:], in_=GT[g]), r=[SGT], w=[gt.b], dma=True)
                lru_conv(xrp, u, ub, g, TQ)
                for z in range(2):
                    lru_gates(u, ub, rr, ii, aa, z, g, TQ)
                    if is_prompt:
                        I("pool", lambda e: e.tensor_tensor(out=ii.t[:, 0:128], in0=ii.t[:, 0:128], in1=tmask.t[:, 0:128], op=ALU.mult), r=[ii.b, tmask.b], w=[ii.b])
                        I("pool", lambda e: e.tensor_tensor(out=ii.t[:, TQ - 128:TQ], in0=ii.t[:, TQ - 128:TQ], in1=tmask.t[:, 128:256], op=ALU.mult), r=[ii.b, tmask.b], w=[ii.b])
                    if z == 0:
                        lo, hi = (2, TQ) if is_prompt else (0, TQ)
                        init = HF.t[:, g:g + 1] if is_prompt else 0.0
                        if is_prompt:
                            I("dve", lambda e: e.memset(hh[0].t[:, 0:2], 0.0), w=[hh[0].b])
                        I("dve", lambda e: e.tensor_tensor_scan(out=hh[0].t[:, lo:hi], data0=aa.t[:, lo:hi], data1=ii.t[:, lo:hi], initial=init,
                                                                op0=ALU.mult, op1=ALU.add), r=[aa.b, ii.b] + ([HF.b] if is_prompt else []), w=[hh[0].b])
                    else:
                        lo, hi = (0, TQ - 2) if is_prompt else (0, TQ)
                        init = HB.t[:, g:g + 1] if is_prompt else 0.0
                        if is_prompt:
                            I("dve", lambda e: e.memset(hh[1].t[:, TQ - 2:TQ], 0.0), w=[hh[1].b])
                        I("dve", lambda e: e.tensor_tensor_scan(out=hh[1].t[:, lo:hi][:, ::-1], data0=aa.t[:, lo:hi][:, ::-1], data1=ii.t[:, lo:hi][:, ::-1],
                                                                initial=init, op0=ALU.mult, op1=ALU.add), r=[aa.b, ii.b] + ([HB.b] if is_prompt else []), w=[hh[1].b])
                I("pool", lambda e: e.tensor_tensor(out=hh[0].t[:], in0=hh[0].t[:], in1=hh[1].t[:], op=ALU.add), r=[hh[0].b, hh[1].b], w=[hh[0].b])
                I("act", lambda e: e.activation(out=rr.t[:], in_=gt.t[:], func=AF.Gelu), r=[gt.b], w=[rr.b])
                I("dve", lambda e: e.tensor_tensor(out=mixT.t[:, 8 + g, :TQ], in0=hh[0].t[:], in1=rr.t[:], op=ALU.mult), r=[hh[0].b, rr.b], w=[mixT.b])
            S.barrier()

        stage(5)
        with contextlib.ExitStack() as es2:
            def sb2(name, shape, dt=F32):
                return T(es2.enter_context(nc.sbuf_tensor("t_" + tag + name, list(shape), dt)))
            alloc_front(es2, full=False, hw=128)
            hsb, junk, hT = A["hsb"], A["junk"], A["hT"]
            sq = [sb2("sq%d" % i, [128, 256], BF16) for i in range(2)]
            rs = [sb2("rs%d" % i, [128, 256]) for i in range(2)]
            mm = sb2("mm", [128, 16, 256], BF16)
            wo = [sb2("wo%d" % i, [128, 16, 512], BF16) for i in range(2)]
            x1b = [sb2("x1b%d" % i, [128, D]) for i in range(2)]
            nbank = (TQ + 255) // 256
            xk = 0
            for tb in range(nbank):
                b0 = tb * 256
                W = min(256, TQ - b0)
                for part in range(2):
                    ssp = PS[part]
                    for c in range(8):
                        s_ = sq[c % 2]
                        I("pool", lambda e: e.tensor_tensor(out=s_.t[:, :W], in0=mixT.t[:, part * 8 + c, b0:b0 + W], in1=mixT.t[:, part * 8 + c, b0:b0 + W], op=ALU.mult),
                          r=[mixT.b], w=[s_.b])
                        I("pe", lambda e: e.matmul(ssp.t[:, :W], lhsT=ones.t[:, :], rhs=s_.t[:, :W], start=(c == 0), stop=(c == 7)), r=[ones.b, s_.b], w=[ssp.b])
                    r_ = rs[part]
                    I("dve", lambda e: e.tensor_scalar(out=r_.t[:, :W], in0=ssp.t[:, :W], scalar1=1.0 / 1024, scalar2=EPS, op0=ALU.mult, op1=ALU.add), r=[ssp.b], w=[r_.b])
                    I("act", lambda e: e.activation(out=r_.t[:, :W], in_=r_.t[:, :W], func=AF.Sqrt), r=[r_.b], w=[r_.b])
                    I("dve", lambda e: e.reciprocal(out=r_.t[:, :W], in_=r_.t[:, :W]), r=[r_.b], w=[r_.b])
                    for c in range(8):
                        I("dve", lambda e: e.tensor_tensor(out=mm.t[:, part * 8 + c, :W], in0=mixT.t[:, part * 8 + c, b0:b0 + W], in1=r_.t[:, :W], op=ALU.mult),
                          r=[mixT.b, r_.b], w=[mm.b])
                nt = W // 128
                for cg in range(4):
                    w_ = wo[cg % 2]
                    I("sp", lambda e: e.dma_start(out=w_.t[:], in_=WOUT[:, :, cg * 512:(cg + 1) * 512].rearrange("k p c -> p k c")), r=[WBo], w=[w_.b], dma=True)
                    for ti in range(nt):
                        ps = PS[2 + (ti % 2)]
                        if cg == 0:
                            tok = Q0 + b0 + ti * 128
                            I("sp", lambda e: e.dma_start(out=x1b[ti].t[:], in_=x_d[tok:tok + 128, :]), w=[x1b[ti].b], dma=True)
                        for kc in range(16):
                            I("pe", lambda e: e.matmul(ps.t[:, :], lhsT=mm.t[:, kc, ti * 128:(ti + 1) * 128], rhs=w_.t[:, kc, :], start=(kc == 0), stop=(kc == 15)),
                              r=[mm.b, w_.b], w=[ps.b], inc=(kc == 15))
                        I("dve", lambda e: e.tensor_tensor(out=x1b[ti].t[:, cg * 512:(cg + 1) * 512], in0=ps.t[:, :], in1=x1b[ti].t[:, cg * 512:(cg + 1) * 512], op=ALU.add),
                          r=[ps.b, x1b[ti].b], w=[x1b[ti].b])
                for ti in range(nt):
                    tl = b0 + ti * 128
                    x1 = x1b[ti]
                    I("pool", lambda e: e.dma_start(out=X1[tl:tl + 128, :], in_=x1.t[:]), r=[x1.b], w=[SX1], dma=True)
                    k = xk % 2
                    xk += 1
                    hs, s4 = hsb[k], st4[k]
                    I("act", lambda e: e.activation(out=junk.t[:], in_=x1.t[:], func=AF.Square, accum_out=s4.t[:, 0:1]), r=[x1.b], w=[junk.b, s4.b])
                    I("dve", lambda e: e.tensor_scalar(out=s4.t[:, 1:2], in0=s4.t[:, 0:1], scalar1=1.0 / D, scalar2=EPS, op0=ALU.mult, op1=ALU.add), r=[s4.b], w=[s4.b])
                    I("act", lambda e: e.activation(out=s4.t[:, 2:3], in_=s4.t[:, 1:2], func=AF.Sqrt), r=[s4.b], w=[s4.b])
                    I("dve", lambda e: e.reciprocal(out=s4.t[:, 3:4], in_=s4.t[:, 2:3]), r=[s4.b], w=[s4.b])
                    I("act", lambda e: e.activation(out=hs.t[:], in_=x1.t[:], func=AF.Identity, scale=s4.t[:, 3:4]), r=[x1.b, s4.b], w=[hs.b])
                    hTt = hT[k]
                    transpose16(hs, hTt, 0)
                    I("pool", lambda e: e.dma_start(out=H2[:, :, tl:tl + 128], in_=hTt.t[:, :, 0:128]), r=[hTt.b], w=[SH2], dma=True)
                    for tok_, idx_ in bnd.items():
                        if tl <= tok_ < tl + 128:
                            cc_ = tok_ - tl
                            edge_ = is_prompt and idx_ in (0, 7)
                            if edge_:
                                I("dve", lambda e: e.tensor_scalar(out=hbd.t[:, :, idx_:idx_ + 1], in0=hTt.t[:, :, cc_:cc_ + 1], scalar1=vmask.t[:, (0 if idx_ == 0 else 1):(1 if idx_ == 0 else 2)], scalar2=None, op0=ALU.mult),
                                  r=[hTt.b, vmask.b], w=[hbd.b])
                            else:
                                I("dve", lambda e: e.tensor_copy(out=hbd.t[:, :, idx_:idx_ + 1], in_=hTt.t[:, :, cc_:cc_ + 1]), r=[hTt.b], w=[hbd.b])
            S.barrier()
        esq.close()

        stage(6)
        with contextlib.ExitStack() as es2:
            def sb2(name, shape, dt=F32):
                return T(es2.enter_context(nc.sbuf_tensor("t_" + tag + name, list(shape), dt)))
            h2 = [sb2("h2_%d" % i, [128, 16, 514], BF16) for i in range(2)]
            actT = sb2("actT", [128, 48, 512], BF16)
            wd = [sb2("wd%d" % i, [128, 12, 512], BF16) for i in range(2)]
            A["wch"] = [sb2("wch%d" % i, [128, 16, 128], BF16) for i in range(4)]
            wch = A["wch"]
            U = [sb2("U%d" % i, [128, 514]) for i in range(2)]
            cv = [sb2("cv%d" % i, [128, 512]) for i in range(2)]
            gl = sb2("gl", [128, 512])
            x1p = [sb2("x1p%d" % i, [128, 512]) for i in range(2)]
            x2p = [sb2("x2p%d" % i, [128, 512]) for i in range(2)]
            UB = sb2("UB", [128, 96, 8])
            ssq = sb2("ssq", [128, 4, 4])
            s3 = sb2("s3", [128, 4])
            x2r = [sb2("x2r%d" % i, [128, D]) for i in range(1)]
            yt = [sb2("yt%d" % i, [128, D]) for i in range(1)]
            uk = 0
            dk = 0
            for gi in range(4):
                t0 = M0 + gi * 512
                h2t = h2[gi % 2]
                lo = t0 - 1
                hi = t0 + 513
                left_ok = lo >= 0
                right_ok = hi <= TQ
                c_lo = 0 if left_ok else 1
                c_hi = 514 if right_ok else 513
                I("sp", lambda e: e.dma_start(out=h2t.t[:, :, c_lo:c_hi], in_=H2[:, :, lo + c_lo:lo + c_hi]), r=[SH2], w=[h2t.b], dma=True)
                for jj in range(48):
                    for half in range(2):
                        j = jj + 48 * half
                        ps = PS[uk % 2]
                        pbd = PS[2]
                        U_ = U[uk % 2]
                        c_ = cv[half]
                        uk += 1
                        wt = wch[wk[0] % 4]
                        wk[0] += 1
                        I("sp", lambda e: e.dma_start(out=wt.t[:], in_=WUP[j]), r=[WBu], w=[wt.b], dma=True)
                        for kc in range(16):
                            I("pe", lambda e: e.matmul(ps.t[:, :], lhsT=wt.t[:, kc, :], rhs=h2t.t[:, kc, 1:513], start=(kc == 0), stop=(kc == 15)),
                              r=[wt.b, h2t.b], w=[ps.b], inc=(kc == 15))
                        I("act", lambda e: e.activation(out=U_.t[:, 1:513], in_=ps.t[:, :], func=AF.Identity), r=[ps.b], w=[U_.b])
                        if gi == 0:
                            for kc in range(16):
                                I("pe", lambda e: e.matmul(pbd.t[:, 0:8], lhsT=wt.t[:, kc, :], rhs=hbd.t[:, kc, :], start=(kc == 0), stop=(kc == 15)),
                                  r=[wt.b, hbd.b], w=[pbd.b], inc=(kc == 15))
                            I("dve", lambda e: e.tensor_copy(out=UB.t[:, j, :], in_=pbd.t[:, 0:8]), r=[pbd.b], w=[UB.b])
                        I("pool", lambda e: e.tensor_copy(out=U_.t[:, 0:1], in_=UB.t[:, j, 2 * gi:2 * gi + 1]), r=[UB.b], w=[U_.b])
                        I("pool", lambda e: e.tensor_copy(out=U_.t[:, 513:514], in_=UB.t[:, j, 2 * gi + 1:2 * gi + 2]), r=[UB.b], w=[U_.b])
                        I("dve", lambda e: e.tensor_scalar(out=c_.t[:], in0=U_.t[:, 0:512], scalar1=cfw.t[:, j, 0:1], scalar2=cfb.t[:, j:j + 1], op0=ALU.mult, op1=ALU.add),
                          r=[U_.b, cfw.b, cfb.b], w=[c_.b])
                        I("dve", lambda e: e.scalar_tensor_tensor(out=c_.t[:], in0=U_.t[:, 1:513], scalar=cfw.t[:, j, 1:2], in1=c_.t[:], op0=ALU.mult, op1=ALU.add),
                          r=[U_.b, cfw.b, c_.b], w=[c_.b])
                        I("dve", lambda e: e.scalar_tensor_tensor(out=c_.t[:], in0=U_.t[:, 2:514], scalar=cfw.t[:, j, 2:3], in1=c_.t[:], op0=ALU.mult, op1=ALU.add),
                          r=[U_.b, cfw.b, c_.b], w=[c_.b])
                    I("act", lambda e: e.activation(out=gl.t[:], in_=cv[0].t[:], func=AF.Gelu), r=[cv[0].b], w=[gl.b])
                    I("pool", lambda e: e.tensor_tensor(out=actT.t[:, jj, :], in0=gl.t[:], in1=cv[1].t[:], op=ALU.mult), r=[gl.b, cv[1].b], w=[actT.b])
                for cg in range(4):
                    for jb in range(4):
                        w_ = wd[dk % 2]
                        dk += 1
                        I("sp", lambda e: e.dma_start(out=w_.t[:], in_=WDN[jb * 12:(jb + 1) * 12, :, cg * 512:(cg + 1) * 512].rearrange("k p c -> p k c")), r=[WBd], w=[w_.b], dma=True)
                        for ti in range(4):
                            ps = PS[2 + ti]
                            for j2 in range(12):
                                jx = jb * 12 + j2
                                I("pe", lambda e: e.matmul(ps.t[:, :], lhsT=actT.t[:, jx, ti * 128:(ti + 1) * 128], rhs=w_.t[:, j2, :], start=(jx == 0), stop=(jx == 47)),
                                  r=[actT.b, w_.b], w=[ps.b], inc=(j2 == 11))
                    for ti in range(4):
                        ps = PS[2 + ti]
                        tl = t0 + ti * 128
                        to = gi * 512 + ti * 128
                        a_, b_ = x1p[ti % 2], x2p[ti % 2]
                        I("sp", lambda e: e.dma_start(out=a_.t[:], in_=X1[tl:tl + 128, cg * 512:(cg + 1) * 512]), r=[SX1], w=[a_.b], dma=True)
                        I("dve", lambda e: e.tensor_tensor(out=b_.t[:], in0=ps.t[:, :], in1=a_.t[:], op=ALU.add), r=[ps.b, a_.b], w=[b_.b])
                        I("act", lambda e: e.activation(out=a_.t[:], in_=b_.t[:], func=AF.Square, accum_out=ssq.t[:, ti, cg:cg + 1]), r=[b_.b], w=[a_.b, ssq.b])
                        I("pool", lambda e: e.dma_start(out=X2[to:to + 128, cg * 512:(cg + 1) * 512], in_=b_.t[:]), r=[b_.b], w=[SX2], dma=True)
                for ti in range(4):
                    to = gi * 512 + ti * 128
                    xr2, y_ = x2r[0], yt[0]
                    I("dve", lambda e: e.reduce_sum(out=s3.t[:, 0:1], in_=ssq.t[:, ti, :], axis=AX.X), r=[ssq.b], w=[s3.b])
                    I("dve", lambda e: e.tensor_scalar(out=s3.t[:, 1:2], in0=s3.t[:, 0:1], scalar1=1.0 / D, scalar2=EPS, op0=ALU.mult, op1=ALU.add), r=[s3.b], w=[s3.b])
                    I("act", lambda e: e.activation(out=s3.t[:, 2:3], in_=s3.t[:, 1:2], func=AF.Sqrt), r=[s3.b], w=[s3.b])
                    I("dve", lambda e: e.reciprocal(out=s3.t[:, 3:4], in_=s3.t[:, 2:3]), r=[s3.b], w=[s3.b])
                    I("sp", lambda e: e.dma_start(out=xr2.t[:], in_=X2[to:to + 128, :]), r=[SX2], w=[xr2.b], dma=True)
                    I("dve", lambda e: e.scalar_tensor_tensor(out=y_.t[:], in0=xr2.t[:], scalar=s3.t[:, 3:4], in1=gfin.t[:], op0=ALU.mult, op1=ALU.mult),
                      r=[xr2.b, s3.b, gfin.b], w=[y_.b])
                    I("pool", lambda e: e.dma_start(out=y_d[to:to + 128, :], in_=y_.t[:]), r=[y_.b], w=[YOUT], dma=True)
            S.barrier()

    def lru_conv(xrp, u, ub, g, n, cast="pool"):
        I("dve", lambda e: e.tensor_scalar(out=u.t[:, :n], in0=xrp.t[:, 0:n], scalar1=crw.t[:, g, 0:1], scalar2=crb.t[:, g:g + 1], op0=ALU.mult, op1=ALU.add),
          r=[xrp.b, crw.b, crb.b], w=[u.b])
        for j in range(1, 4):
            I("dve", lambda e: e.scalar_tensor_tensor(out=u.t[:, :n], in0=xrp.t[:, j:j + n], scalar=crw.t[:, g, j:j + 1], in1=u.t[:, :n], op0=ALU.mult, op1=ALU.add),
              r=[xrp.b, crw.b, u.b], w=[u.b])
        if cast == "act":
            I("act", lambda e: e.activation(out=ub.t[:, :n], in_=u.t[:, :n], func=AF.Identity), r=[u.b], w=[ub.b])
        elif cast == "pool":
            I("pool", lambda e: e.tensor_copy(out=ub.t[:, :n], in_=u.t[:, :n]), r=[u.b], w=[ub.b])

    def lru_gates_a(u, ub, rr, ii, aa, z, g, n):
        k = 0
        for gi, (dst, bias) in enumerate(((rr, hba), (ii, hbx))):
            for c0 in range(0, n, 512):
                W = min(512, n - c0)
                ps = PS[k % 2]
                k += 1
                I("pe", lambda e: e.matmul(ps.t[:, :W], lhsT=wg.t[:, gi * 16 + z * 8 + g, :], rhs=ub.t[:, c0:c0 + W], start=True, stop=True), r=[wg.b, ub.b], w=[ps.b])
                I("act", lambda e: e.activation(out=dst.t[:, c0:c0 + W], in_=ps.t[:, :W], func=AF.Tanh, scale=0.5, bias=bias.t[:, z, g:g + 1]), r=[ps.b, bias.b], w=[dst.b])

    def lru_gates_b1(u, ub, rr, ii, aa, z, g, n):
        I("act", lambda e: e.activation(out=aa.t[:, :n], in_=rr.t[:, :n], func=AF.Exp, scale=hcl.t[:, z, g:g + 1], bias=hcl.t[:, z, g:g + 1]), r=[rr.b, hcl.b], w=[aa.b])
        I("act", lambda e: e.activation(out=rr.t[:, :n], in_=rr.t[:, :n], func=AF.Exp, scale=cl.t[:, z, g:g + 1], bias=cl.t[:, z, g:g + 1]), r=[rr.b, cl.b], w=[rr.b])

    def lru_gates_b2(u, ub, rr, ii, aa, z, g, n):
        I("act", lambda e: e.activation(out=rr.t[:, :n], in_=rr.t[:, :n], func=AF.Sqrt, scale=-0.25, bias=0.25), r=[rr.b], w=[rr.b])

    def lru_gates_b3(u, ub, rr, ii, aa, z, g, n, split=False):
        I("dve", lambda e: e.scalar_tensor_tensor(out=ii.t[:, :n], in0=ii.t[:, :n], scalar=1.0, in1=u.t[:, :n], op0=ALU.add, op1=ALU.mult), r=[ii.b, u.b], w=[ii.b])
        I("pool", lambda e: e.tensor_tensor(out=ii.t[:, :n], in0=ii.t[:, :n], in1=rr.t[:, :n], op=ALU.mult), r=[ii.b, rr.b], w=[ii.b])

    def lru_gates_b(u, ub, rr, ii, aa, z, g, n, split=False):
        lru_gates_b1(u, ub, rr, ii, aa, z, g, n)
        lru_gates_b2(u, ub, rr, ii, aa, z, g, n)
        lru_gates_b3(u, ub, rr, ii, aa, z, g, n)

    def lru_gates(u, ub, rr, ii, aa, z, g, n):
        lru_gates_a(u, ub, rr, ii, aa, z, g, n)
        lru_gates_b(u, ub, rr, ii, aa, z, g, n)

    YOUT = Buf()

    def prepass():
        SXF = Buf()
        esA = contextlib.ExitStack()
        alloc_front(esA)
        zf = A["zf"]
        DVE_EVAC[0] = True
        wres = T(esA.enter_context(nc.sbuf_tensor("t_wres", [128, 8, 16, 128], BF16)))
        I("sp", lambda e: e.dma_start(out=wres.t[:], in_=WIN[24:32].rearrange("j p k c -> p j k c")), r=[WBi], w=[wres.b], dma=True)
        def front_fn(st):
            s0 = st * 512
            hTt = A["hT"][st % 2]
            return [(lambda ti=ti: front(xf_d[s0 + ti * 128:s0 + (ti + 1) * 128, :], hTt, ti * 128)) for ti in range(4)]

        def proj_fn(st):
            s0 = st * 512
            hTt = A["hT"][st % 2]

            def body(g):
                ps = PS[zk[0] % 2]
                z = zf[zk[0] % 3]
                zk[0] += 1
                for kc in range(16):
                    I("pe", lambda e: e.matmul(ps.t[:, :], lhsT=wres.t[:, g, kc, :], rhs=hTt.t[:, kc, :], start=(kc == 0), stop=(kc == 15)),
                      r=[wres.b, hTt.b], w=[ps.b], inc=(kc == 15))
                I("act", lambda e: e.activation(out=z.t[:, :], in_=ps.t[:, :], func=AF.Identity), r=[ps.b], w=[z.b])
                I("pool", lambda e: e.dma_start(out=XRF[g, :, s0:s0 + 512], in_=z.t[:, :]), r=[z.b], w=[SXF], dma=True)
            return [(lambda g=g: body(g)) for g in range(8)]

        pipelined_steps(SEQ_P // 512, front_fn, proj_fn)
        DVE_EVAC[0] = False
        S.barrier()
        esA.close()
        with contextlib.ExitStack() as es2:
            def sb2(name, shape, dt=F32):
                return T(es2.enter_context(nc.sbuf_tensor("t_pp" + name, list(shape), dt)))
            n = 1024
            NSG = SEQ_P // n
            NB = 4
            UF = dscr("UF", [8, 128, SEQ_P], F32)
            SUF = Buf()
            xrp = [sb2("xrp%d" % i, [128, n + 3]) for i in range(NB)]
            u = [sb2("u%d" % i, [128, n]) for i in range(NB)]
            ub = [sb2("ub%d" % i, [128, n], BF16) for i in range(NB)]
            rr = [sb2("rr%d" % i, [128, n]) for i in range(NB)]
            ii = [sb2("ii%d" % i, [128, n]) for i in range(NB)]
            aa = [sb2("aa%d" % i, [128, n]) for i in range(NB)]
            hh = [sb2("hh%d" % i, [128, n]) for i in range(NB)]
            rec = sb2("rec", [128, 2, 8, 8]); tm = sb2("tm", [128, 8])
            I("dve", lambda e: e.memset(rec.t[:], 0.0), w=[rec.b])
            work = []
            for g in range(8):
                for z in range(2):
                    segs = list(range((2048 * 7 - 127) // n + 1)) if z == 0 else list(range(NSG - 1, (2048 + 126) // n - 1, -1))
                    for si, s in enumerate(segs):
                        work.append((g, z, si, s, si == len(segs) - 1))

            def stage_a(it, part):
                g, z, si, s, last_ = work[it]
                k = it % NB
                x_, u_, ub_, rr_, ii_, aa_ = xrp[k], u[k], ub[k], rr[k], ii[k], aa[k]
                if part == 2:
                    lru_gates_a(u_, ub_, rr_, ii_, aa_, z, g, n)
                    return
                if z == 0 or s > (2048 * 7 - 127) // n:
                    a0 = s * n - 2
                    c_lo = 2 if s == 0 else 0
                    c_hi = n + 2 if s == NSG - 1 else n + 3
                    if s == 0:
                        I("dve", lambda e: e.memset(x_.t[:, 0:2], 0.0), w=[x_.b])
                    if s == NSG - 1:
                        I("dve", lambda e: e.memset(x_.t[:, n + 2:n + 3], 0.0), w=[x_.b])
                    I("sp", lambda e: e.dma_start(out=x_.t[:, c_lo:c_hi], in_=XRF[g, :, a0 + c_lo:a0 + c_hi]), r=[SXF], w=[x_.b], dma=True)
                    lru_conv(x_, u_, ub_, g, n, cast="act")
                    if z == 0:
                        I("pool", lambda e: e.dma_start(out=UF[g, :, s * n:(s + 1) * n], in_=u_.t[:]), r=[u_.b], w=[SUF], dma=True)
                else:
                    I("sp", lambda e: e.dma_start(out=u_.t[:], in_=UF[g, :, s * n:(s + 1) * n]), r=[SUF], w=[u_.b], dma=True)
                    I("act", lambda e: e.activation(out=ub_.t[:], in_=u_.t[:], func=AF.Identity), r=[u_.b], w=[ub_.b])

            def stage_b(it, ph):
                g, z, si, s, last_ = work[it]
                k = it % NB
                u_, ub_, rr_, ii_, aa_, hh_, hp = u[k], ub[k], rr[k], ii[k], aa[k], hh[k], hh[(it - 1) % NB]
                if ph == 1:
                    lru_gates_b1(u_, ub_, rr_, ii_, aa_, z, g, n)
                    return
                if ph == 2:
                    lru_gates_b2(u_, ub_, rr_, ii_, aa_, z, g, n)
                    return
                lru_gates_b3(u_, ub_, rr_, ii_, aa_, z, g, n)
                if z == 0:
                    init = 0.0 if si == 0 else hp.t[:, n - 1:n]
                    I("dve", lambda e: e.tensor_tensor_scan(out=hh_.t[:], data0=aa_.t[:], data1=ii_.t[:], initial=init, op0=ALU.mult, op1=ALU.add),
                      r=[aa_.b, ii_.b, hp.b], w=[hh_.b])
                    for j_ in range(1, 8):
                        tk_ = 2048 * j_ - 127
                        if tk_ // n == s:
                            I("dve", lambda e: e.tensor_copy(out=rec.t[:, 0, g, j_:j_ + 1], in_=hh_.t[:, tk_ % n:tk_ % n + 1]), r=[hh_.b], w=[rec.b])
                else:
                    init = 0.0 if si == 0 else hp.t[:, 0:1]
                    I("dve", lambda e: e.tensor_tensor_scan(out=hh_.t[:, ::-1], data0=aa_.t[:, ::-1], data1=ii_.t[:, ::-1], initial=init, op0=ALU.mult, op1=ALU.add),
                      r=[aa_.b, ii_.b, hp.b], w=[hh_.b])
                    for j_ in range(0, 7):
                        tk_ = 2048 * (j_ + 1) + 126
                        if tk_ // n == s:
                            I("dve", lambda e: e.tensor_copy(out=rec.t[:, 1, g, j_:j_ + 1], in_=hh_.t[:, tk_ % n:tk_ % n + 1]), r=[hh_.b], w=[rec.b])
                if last_:
                    Hx = HF if z == 0 else HB
                    I("dve", lambda e: e.tensor_tensor(out=tm.t[:], in0=rec.t[:, z, g, :], in1=sel.t[:, z, :], op=ALU.mult), r=[rec.b, sel.b], w=[tm.b])
                    I("dve", lambda e: e.reduce_sum(out=Hx.t[:, g:g + 1], in_=tm.t[:], axis=AX.X), r=[tm.b], w=[Hx.b])

            LA = 2
            for it in range(min(LA, len(work))):
                stage_a(it, 1)
                stage_a(it, 2)
            for p in range(0, len(work), 2):
                pair = [it for it in (p, p + 1) if it < len(work)]
                nxt = [it + LA for it in pair if it + LA < len(work)]
                for it in nxt:
                    stage_a(it, 1)
                for ph in (1, 2, 3):
                    for it in pair:
                        stage_b(it, ph)
                for it in nxt:
                    stage_a(it, 2)
            S.barrier()

    try:
        run_seq("s", xs_d, CH, 0, CH, 0, cos_s_d, sin_s_d, ys_d, False, None, None)
        if prompt_on:
            if prepass_on:
                prepass()
            else:
                I("dve", lambda e: e.memset(HF.t[:], 0.0), w=[HF.b])
                I("dve", lambda e: e.memset(HB.t[:], 0.0), w=[HB.b])
            run_seq("p", xp_d, 4352, 1024, 2304, 128, cos_p_d, sin_p_d, yp_d, True, HF, HB)
    except _Stop:
        pass
    S.finish()
    return nc, es


def _rope_tables(pos):
    half = 16
    inv = (500000.0 ** (-np.arange(half, dtype=np.float32) / half)).astype(np.float32)
    ang = pos.astype(np.float32)[None, :] * inv[:, None]
    c = np.cos(ang).astype(np.float32)
    s = np.sin(ang).astype(np.float32)
    return np.concatenate([c, c], 0), np.concatenate([s, s], 0)


def _chunk16(v):
    return np.ascontiguousarray(v.reshape(-1, 128).T)


PROMPT_ON = True
PREPASS_ON = True


def kernel(x_prompt, x_sample, g_mix, w_in, w_out, g_attn_out, g_lru_out, conv_rg_w, conv_rg_b,
           rg_w_a, rg_b_a, rg_w_x, rg_b_x, rg_lam, g_mlp, w_up, conv_ff_w, conv_ff_b, w_down, g_final):
    f = np.float32
    xpf = np.ascontiguousarray(x_prompt[0], dtype=f)
    common = {
        "xf": xpf,
        "w_in": np.ascontiguousarray(w_in[0], f), "w_out": np.ascontiguousarray(w_out[0], f),
        "w_up": np.ascontiguousarray(w_up[0], f), "w_down": np.ascontiguousarray(w_down[0], f),
        "g_mix": _chunk16(g_mix[0]), "g_mlp": _chunk16(g_mlp[0]),
        "g_mo": _chunk16(np.concatenate([g_attn_out[0], g_lru_out[0]])),
        "crw": np.ascontiguousarray(conv_rg_w[0].reshape(4, 8, 128).transpose(2, 1, 0)),
        "crb": _chunk16(conv_rg_b[0]),
        "rg_w_a": np.ascontiguousarray(rg_w_a[0], f), "rg_w_x": np.ascontiguousarray(rg_w_x[0], f),
        "rg_b_a": np.ascontiguousarray(rg_b_a[0].reshape(2, 8, 128).transpose(2, 0, 1)),
        "rg_b_x": np.ascontiguousarray(rg_b_x[0].reshape(2, 8, 128).transpose(2, 0, 1)),
        "rg_lam": np.ascontiguousarray(rg_lam[0].reshape(2, 8, 128).transpose(2, 0, 1)),
        "cfw": np.ascontiguousarray(conv_ff_w[0].reshape(3, 96, 128).transpose(2, 1, 0)),
        "cfb": _chunk16(conv_ff_b[0]),
        "gfin": np.ascontiguousarray(np.broadcast_to(g_final[None, :], (128, D)), f),
    }
    cs, sn = _rope_tables(np.arange(CH))
    common["cos_s"], common["sin_s"] = cs, sn
    pmm = np.zeros((128, 128), f)
    for m in range(16):
        pmm[m + 16, m] = -1.0
        pmm[m, m + 16] = 1.0
    common["pm"] = pmm
    jj = np.arange(128)[:, None]
    cc = np.arange(512)[None, :]
    common["bmask"] = ((cc - jj >= 0) & (cc - jj <= 128)).astype(f)
    common["ident"] = np.eye(128, dtype=f)
    if not (PROMPT_ON and PREPASS_ON):
        common["xf"] = xpf[:128]
    in_maps = []
    for c in range(NCORE):
        m = dict(common)
        m["xs"] = np.ascontiguousarray(x_sample[c], f)
        a0 = c * CH - 1152
        pos = np.arange(a0, a0 + 4352)
        valid = (pos >= 0) & (pos < SEQ_P)
        xp = np.zeros((4352, D), f)
        xp[valid] = xpf[pos[valid]]
        m["xp"] = xp
        cp, sp_ = _rope_tables(np.where(valid, pos, 0))
        m["cos_p"], m["sin_p"] = cp, sp_
        kb = np.zeros((128, 3, 16, 34), f)
        for b, dil in enumerate(DILS):
            L = 4352 // dil
            for r in range(dil):
                for kbi in range((L + 127) // 128):
                    l = kbi * 128 + np.arange(128)
                    t = r + dil * l
                    ok = (l < L) & valid[np.minimum(t, 4351)]
                    kb[:, b, r, kbi] = np.where(ok, 0.0, -30000.0)
        m["kbias"] = kb
        sel = np.zeros((128, 2, 8), f)
        if c > 0:
            sel[:, 0, c] = 1.0
        if c < 7:
            sel[:, 1, c] = 1.0
        m["sel"] = sel
        vm = np.ones((128, 2), f)
        if c == 0:
            vm[:, 0] = 0.0
        if c == 7:
            vm[:, 1] = 0.0
        m["vmask"] = vm
        tm = np.ones((128, 256), f)
        tm[:, 0:128] = ((c * CH - 128 + np.arange(128)) >= 0).astype(f)[None, :]
        tm[:, 128:256] = ((c * CH + CH + np.arange(128)) < SEQ_P).astype(f)[None, :]
        m["tmask"] = tm
        in_maps.append(m)
    nc, es = build(PROMPT_ON, PREPASS_ON)
    res = run_bass_kernel_spmd(nc, in_maps, core_ids=list(range(NCORE)))
    try:
        es.close()
    except AssertionError:
        pass
    if DEBUG["scratch_out"]:
        DEBUG["res"] = res
        DEBUG["in_maps"] = in_maps
    yp = np.concatenate([np.asarray(res.results[c]["yp"], f) for c in range(NCORE)], 0)[None]
    ys = np.stack([np.asarray(res.results[c]["ys"], f) for c in range(NCORE)], 0)
    return (yp, ys)
```

```python
import contextlib
import math
import numpy as np
import concourse.bass as bass
import concourse.mybir as mybir
from concourse.bass_utils import run_bass_kernel_spmd

F32 = mybir.dt.float32
BF16 = mybir.dt.bfloat16
AF = mybir.ActivationFunctionType
ALU = mybir.AluOpType
AX = mybir.AxisListType

D = 2048
NCORE = 8
SEQ_P = 16384
CH = 2048
EPS = 1e-6
NR = 16
NRQ = {"sp": 16, "pool": 4}


class Buf:
    __slots__ = ("w", "r", "excl")

    def __init__(self, excl=False):
        self.w = None
        self.r = {}
        self.excl = excl


class Sched:
    def __init__(self, nc, es):
        self.nc = nc
        self.E = {"pe": nc.tensor, "act": nc.scalar, "dve": nc.vector, "pool": nc.gpsimd, "sp": nc.sync}
        self.cs = {k: es.enter_context(nc.semaphore("c_" + k)) for k in ("pe", "act", "dve", "pool")}
        self.cc = {k: 0 for k in self.cs}
        self.ring = {q: [es.enter_context(nc.semaphore("d_%s%d" % (q, i))) for i in range(NRQ[q])] for q in ("sp", "pool")}
        self.rt = {q: [0] * NRQ[q] for q in self.ring}
        self.rk = {q: 0 for q in self.ring}
        self.waited = {k: {} for k in self.E}
        self.pr = []
        self.pw = []

    def _wait(self, e, sem, val):
        k = id(sem)
        if self.waited[e].get(k, 0) < val:
            self.E[e].wait_ge(sem, val)
            self.waited[e][k] = val

    def I(self, e, fn, r=(), w=(), dma=False, inc=True):
        deps = []
        for b in r:
            if b.w is not None:
                deps.append(b.w)
            if b.excl:
                deps.extend(b.r.values())
        for b in w:
            if b.w is not None:
                deps.append(b.w)
            deps.extend(b.r.values())
        for sem, val in deps:
            if e == "pe" and sem is self.cs["pe"]:
                continue
            self._wait(e, sem, val)
        eng = self.E[e]
        if dma:
            i = self.rk[e] % NRQ[e]
            self.rk[e] += 1
            sem = self.ring[e][i]
            if self.rt[e][i] > 0:
                self._wait(e, sem, self.rt[e][i])
            inst = fn(eng)
            self.rt[e][i] += 16
            inst.then_inc(sem, 16)
            t = (sem, self.rt[e][i])
        else:
            inst = fn(eng)
            if not inc:
                self.pr += list(r)
                self.pw += list(w)
                return
            self.cc[e] += 1
            inst.then_inc(self.cs[e], 1)
            t = (self.cs[e], self.cc[e])
            if e == "pe":
                r = list(r) + self.pr
                w = list(w) + self.pw
                self.pr = []
                self.pw = []
        for b in r:
            b.r[id(t[0])] = t
        for b in w:
            b.w = t
            b.r = {}

    def barrier(self):
        for e in self.E:
            for k in self.cs:
                if k != e and self.cc[k] > 0:
                    self._wait(e, self.cs[k], self.cc[k])
            for q in self.ring:
                for i in range(NRQ[q]):
                    if self.rt[q][i] > 0:
                        self._wait(e, self.ring[q][i], self.rt[q][i])

    def finish(self):
        for q in self.ring:
            for i in range(NRQ[q]):
                if self.rt[q][i] > 0:
                    self._wait("sp", self.ring[q][i], self.rt[q][i])
        for k in self.cs:
            if self.cc[k] > 0:
                self._wait("sp", self.cs[k], self.cc[k])


class T:
    def __init__(self, ap, excl=False):
        self.t = ap
        self.b = Buf(excl)


DILS = (1, 4, 16)
DEBUG = {"stop": 99, "scratch_out": False, "outs": (), "maxsteps": 999, "sub": 9}


class _Stop(Exception):
    pass


def build(prompt_on=True, prepass_on=True):
    nc = bass.Bass("TRN2", target_bir_lowering=False)
    es = contextlib.ExitStack()
    S = Sched(nc, es)
    I = S.I

    def din(name, shape, dt=F32):
        return nc.dram_tensor(name, list(shape), dt, kind="ExternalInput").ap()

    def dscr(name, shape, dt):
        return nc.dram_tensor(name, list(shape), dt, kind=("ExternalOutput" if name in DEBUG["outs"] else "Internal")).ap()

    def stage(n):
        if DEBUG["stop"] <= n:
            raise _Stop()

    xs_d = din("xs", [CH, D])
    xp_d = din("xp", [4352, D])
    xf_d = din("xf", [SEQ_P if prepass_on and prompt_on else 128, D])
    w_in_d = din("w_in", [D, 5120])
    w_out_d = din("w_out", [D, D])
    w_up_d = din("w_up", [D, 12288])
    w_dn_d = din("w_down", [6144, D])
    g_mix_d = din("g_mix", [128, 16])
    g_mlp_d = din("g_mlp", [128, 16])
    g_mo_d = din("g_mo", [128, 16])
    crw_d = din("crw", [128, 8, 4])
    crb_d = din("crb", [128, 8])
    wa_d = din("rg_w_a", [2, 8, 128, 128])
    wx_d = din("rg_w_x", [2, 8, 128, 128])
    ba_d = din("rg_b_a", [128, 2, 8])
    bx_d = din("rg_b_x", [128, 2, 8])
    lam_d = din("rg_lam", [128, 2, 8])
    cfw_d = din("cfw", [128, 96, 3])
    cfb_d = din("cfb", [128, 96])
    gfin_d = din("gfin", [128, D])
    cos_s_d = din("cos_s", [32, CH])
    sin_s_d = din("sin_s", [32, CH])
    cos_p_d = din("cos_p", [32, 4352])
    sin_p_d = din("sin_p", [32, 4352])
    pm_d = din("pm", [128, 128])
    mask_d = din("bmask", [128, 512])
    ident_d = din("ident", [128, 128])
    kb_d = din("kbias", [128, 3, 16, 34])
    sel_d = din("sel", [128, 2, 8])
    vm_d = din("vmask", [128, 2])
    tm_d = din("tmask", [128, 256])
    ys_d = nc.dram_tensor("ys", [CH, D], F32, kind="ExternalOutput").ap()
    yp_d = nc.dram_tensor("yp", [CH, D], F32, kind="ExternalOutput").ap()

    WIN = dscr("WIN", [40, 128, 16, 128], BF16)
    WUP = dscr("WUP", [96, 128, 16, 128], BF16)
    WOUT = dscr("WOUT", [16, 128, D], BF16)
    WDN = dscr("WDN", [48, 128, D], BF16)
    XRF = dscr("XRF", [8, 128, SEQ_P], F32)

    def sb(name, shape, dt=F32):
        return T(es.enter_context(nc.sbuf_tensor("sb_" + name, list(shape), dt)))

    PS = [T(es.enter_context(nc.psum_tensor("ps%d" % i, [128, 512], F32)), True) for i in range(6)]
    PB = [T(es.enter_context(nc.psum_tensor("pb%d" % i, [128, 1024], BF16)), True) for i in range(2)]

    def load(dst, src, q="sp"):
        I(q, lambda e: e.dma_start(out=dst.t[:], in_=src), w=[dst.b], dma=True)

    gmix = sb("gmix", [128, 16]); load(gmix, g_mix_d[:, :])
    gmlp = sb("gmlp", [128, 16]); load(gmlp, g_mlp_d[:, :])
    gmo = sb("gmo", [128, 16]); load(gmo, g_mo_d[:, :])
    crw = sb("crw", [128, 8, 4]); load(crw, crw_d[:, :, :])
    crb = sb("crb", [128, 8]); load(crb, crb_d[:, :])
    ba = sb("ba", [128, 2, 8]); load(ba, ba_d[:, :, :])
    bx = sb("bx", [128, 2, 8]); load(bx, bx_d[:, :, :])
    lam = sb("lam", [128, 2, 8]); load(lam, lam_d[:, :, :])
    cfw = sb("cfw", [128, 96, 3]); load(cfw, cfw_d[:, :, :])
    cfb = sb("cfb", [128, 96]); load(cfb, cfb_d[:, :])
    gfin = sb("gfin", [128, D]); load(gfin, gfin_d[:, :])
    kbias = sb("kbias", [128, 3, 16, 34]); load(kbias, kb_d[:, :, :, :])
    sel = sb("sel", [128, 2, 8]); load(sel, sel_d[:, :, :])
    vmask = sb("vmask", [128, 2]); load(vmask, vm_d[:, :])
    tmask = sb("tmask", [128, 256]); load(tmask, tm_d[:, :])
    tmpc = sb("tmpc", [128, 512])
    bmask = sb("bmaskb", [128, 512], BF16)
    load(tmpc, mask_d[:, :])
    I("dve", lambda e: e.tensor_copy(out=bmask.t[:], in_=tmpc.t[:]), r=[tmpc.b], w=[bmask.b])
    ident = sb("identb", [128, 128], BF16)
    tmpi = sb("tmpi", [128, 128])
    load(tmpi, ident_d[:, :])
    I("dve", lambda e: e.tensor_copy(out=ident.t[:], in_=tmpi.t[:]), r=[tmpi.b], w=[ident.b])
    pm = sb("pmb", [128, 128], BF16)
    tmpp = sb("tmpp", [128, 128])
    load(tmpp, pm_d[:, :])
    I("dve", lambda e: e.tensor_copy(out=pm.t[:], in_=tmpp.t[:]), r=[tmpp.b], w=[pm.b])
    ones = sb("onesb", [128, 128], BF16)
    I("dve", lambda e: e.memset(ones.t[:], 1.0), w=[ones.b])
    cl = sb("cl", [128, 2, 8]); cl2 = sb("cl2", [128, 2, 8])
    I("act", lambda e: e.activation(out=cl.t[:], in_=lam.t[:], func=AF.Exp, scale=-1.0), r=[lam.b], w=[cl.b])
    I("act", lambda e: e.activation(out=cl.t[:], in_=cl.t[:], func=AF.Ln, bias=1.0), r=[cl.b], w=[cl.b])
    I("dve", lambda e: e.tensor_scalar(out=cl2.t[:], in0=cl.t[:], scalar1=-16.0, scalar2=None, op0=ALU.mult), r=[cl.b], w=[cl2.b])
    I("dve", lambda e: e.tensor_scalar(out=cl.t[:], in0=cl.t[:], scalar1=-8.0, scalar2=None, op0=ALU.mult), r=[cl.b], w=[cl.b])
    hba = sb("hba", [128, 2, 8]); hbx = sb("hbx", [128, 2, 8]); hcl = sb("hcl", [128, 2, 8])
    I("dve", lambda e: e.tensor_scalar(out=hba.t[:], in0=ba.t[:], scalar1=0.5, scalar2=None, op0=ALU.mult), r=[ba.b], w=[hba.b])
    I("dve", lambda e: e.tensor_scalar(out=hbx.t[:], in0=bx.t[:], scalar1=0.5, scalar2=None, op0=ALU.mult), r=[bx.b], w=[hbx.b])
    I("dve", lambda e: e.tensor_scalar(out=hcl.t[:], in0=cl.t[:], scalar1=0.5, scalar2=None, op0=ALU.mult), r=[cl.b], w=[hcl.b])
    wg = sb("wg", [128, 32, 128], BF16)
    esg = contextlib.ExitStack()
    wgt = T(esg.enter_context(nc.sbuf_tensor("wgt", [128, 16, 128], F32)))
    for gi, src in enumerate((wa_d, wx_d)):
        I("sp", lambda e, src=src: e.dma_start(out=wgt.t[:], in_=src.rearrange("z g i j -> i (z g) j")), w=[wgt.b], dma=True)
        I("dve", lambda e, gi=gi: e.tensor_copy(out=wg.t[:, gi * 16:(gi + 1) * 16, :], in_=wgt.t[:]), r=[wgt.b], w=[wg.b])

    S.barrier()
    esg.close()

    st4 = [sb("st4_%d" % i, [128, 4]) for i in range(3)]
    HF = sb("HF", [128, 8])
    HB = sb("HB", [128, 8])
    hbds = {"s": sb("hbd_s", [128, 16, 8], BF16), "p": sb("hbd_p", [128, 16, 8], BF16)}
    esp = contextlib.ExitStack()
    wst = [T(esp.enter_context(nc.sbuf_tensor("wst%d" % i, [128, 2048], F32))) for i in range(4)]
    wsb = [T(esp.enter_context(nc.sbuf_tensor("wsb%d" % i, [128, 2048], BF16))) for i in range(4)]
    cnt = [0]

    def prep(Wd, K, N, g, dst, tiled, WB):
        for kc in range(K // 128):
            for cb in range(0, N, 2048):
                cw = min(2048, N - cb)
                k = cnt[0] % 4
                cnt[0] += 1
                a, b = wst[k], wsb[k]
                I("sp", lambda e: e.dma_start(out=a.t[:, :cw], in_=Wd[kc * 128:(kc + 1) * 128, cb:cb + cw]), w=[a.b], dma=True)
                if g is not None:
                    I("dve", lambda e: e.tensor_scalar(out=b.t[:, :cw], in0=a.t[:, :cw], scalar1=g.t[:, kc:kc + 1], scalar2=None, op0=ALU.mult), r=[a.b, g.b], w=[b.b])
                else:
                    I("dve", lambda e: e.tensor_copy(out=b.t[:, :cw], in_=a.t[:, :cw]), r=[a.b], w=[b.b])
                if tiled:
                    j0 = cb // 128
                    I("pool", lambda e: e.dma_start(out=dst[j0:j0 + cw // 128, :, kc, :].rearrange("j p c -> p j c"),
                                                    in_=b.t[:, :cw].rearrange("p (j c) -> p j c", c=128)), r=[b.b], w=[WB], dma=True)
                else:
                    I("pool", lambda e: e.dma_start(out=dst[kc, :, cb:cb + cw], in_=b.t[:, :cw]), r=[b.b], w=[WB], dma=True)
                yield None

    WBi, WBo, WBu, WBd = Buf(), Buf(), Buf(), Buf()
    import itertools
    for _ in prep(w_in_d, D, 5120, gmix, WIN, True, WBi):
        pass
    prep_rest = itertools.chain(prep(w_out_d, D, D, gmo, WOUT, False, WBo), prep(w_up_d, D, 12288, gmlp, WUP, True, WBu),
                                prep(w_dn_d, 6144, D, None, WDN, False, WBd))

    A = {}
    uid = [0]

    def alloc_front(esx, full=True, hw=512):
        uid[0] += 1
        p = "f%d_" % uid[0]

        def sbx(name, shape, dt=F32):
            return T(esx.enter_context(nc.sbuf_tensor(p + name, list(shape), dt)))
        A["hsb"] = [sbx("hs%d" % i, [128, D], BF16) for i in range(3 if full else 2)]
        A["junk"] = sbx("junk", [128, D], BF16)
        A["hT"] = [sbx("hT%d" % i, [128, 16, hw], BF16) for i in range(2)]
        if full:
            A["xtb"] = [sbx("xt%d" % i, [128, D]) for i in range(3)]
            A["wch"] = [sbx("wch%d" % i, [128, 16, 128], BF16) for i in range(4)]
            A["zb"] = [sbx("zb%d" % i, [128, 512], BF16) for i in range(3)]
            A["zf"] = [sbx("zf%d" % i, [128, 512]) for i in range(3)]
            A["r1"] = sbx("r1", [32, 512]); A["r2"] = sbx("r2", [32, 512])
            A["cst"] = [sbx("cst%d" % i, [32, 512]) for i in range(2)]
            A["snt"] = [sbx("snt%d" % i, [32, 512]) for i in range(2)]
    wk = [0]
    tk = [0]
    DVE_EVAC = [False]

    def front_act(xsrc_rows):
        k = tk[0] % 3
        tk[0] += 1
        xt, hs, s4 = A["xtb"][k], A["hsb"][k], st4[k]
        junk = A["junk"]
        I("sp", lambda e: e.dma_start(out=xt.t[:], in_=xsrc_rows), w=[xt.b], dma=True)
        I("act", lambda e: e.activation(out=junk.t[:], in_=xt.t[:], func=AF.Square, accum_out=s4.t[:, 0:1]), r=[xt.b], w=[junk.b, s4.b])
        I("dve", lambda e: e.tensor_scalar(out=s4.t[:, 1:2], in0=s4.t[:, 0:1], scalar1=1.0 / D, scalar2=EPS, op0=ALU.mult, op1=ALU.add), r=[s4.b], w=[s4.b])
        I("act", lambda e: e.activation(out=s4.t[:, 2:3], in_=s4.t[:, 1:2], func=AF.Sqrt), r=[s4.b], w=[s4.b])
        I("dve", lambda e: e.reciprocal(out=s4.t[:, 3:4], in_=s4.t[:, 2:3]), r=[s4.b], w=[s4.b])
        I("act", lambda e: e.activation(out=hs.t[:], in_=xt.t[:], func=AF.Identity, scale=s4.t[:, 3:4]), r=[xt.b, s4.b], w=[hs.b])
        return hs

    def transpose16(hs, hTt, col0):
        for half in range(2):
            pb = PB[half]
            for kk in range(8):
                kc = half * 8 + kk
                I("pe", lambda e: e.transpose(pb.t[:, kk * 128:(kk + 1) * 128], hs.t[:, kc * 128:(kc + 1) * 128], ident.t[:]),
                  r=[hs.b, ident.b], w=[pb.b], inc=(kk == 7))
            eng = "dve" if (half == 0 or DVE_EVAC[0]) else "act"
            src = pb.t[:].rearrange("p (k c) -> p k c", c=128)
            dst = hTt.t[:, half * 8:(half + 1) * 8, col0:col0 + 128]
            if eng == "dve":
                I("dve", lambda e: e.tensor_copy(out=dst, in_=src), r=[pb.b], w=[hTt.b])
            else:
                I("act", lambda e: e.activation(out=dst, in_=src, func=AF.Identity), r=[pb.b], w=[hTt.b])

    def proj_chunk(Wt, j, hTt, W, ps, WB):
        wt = A["wch"][wk[0] % 4]
        wk[0] += 1
        I("sp", lambda e: e.dma_start(out=wt.t[:], in_=Wt[j]), r=[WB], w=[wt.b], dma=True)
        for kc in range(16):
            I("pe", lambda e: e.matmul(ps.t[:, :W], lhsT=wt.t[:, kc, :], rhs=hTt.t[:, kc, :W], start=(kc == 0), stop=(kc == 15)),
              r=[wt.b, hTt.b], w=[ps.b], inc=(kc == 15))

    zk = [0]

    def pipelined_steps(nsteps, front_fn, proj_fn):
        tiles = []
        first_of = []
        for st in range(nsteps):
            first_of.append(len(tiles))
            tiles += front_fn(st)
        first_of.append(len(tiles))
        hs_of = {}
        na = [0]
        LEAD = 2

        def emit_pe(idx):
            while na[0] <= min(idx + LEAD, len(tiles) - 1):
                hs_of[na[0]] = front_act(tiles[na[0]][0])
                na[0] += 1
            transpose16(hs_of.pop(idx), tiles[idx][1], tiles[idx][2])

        for idx in range(first_of[0], first_of[1]):
            emit_pe(idx)
        for st in range(nsteps):
            nxt = list(range(first_of[st + 1], first_of[st + 2])) if st + 1 < nsteps else []
            pj = proj_fn(st)
            per = max(1, len(pj) // max(1, len(nxt)))
            for i_, p in enumerate(pj):
                if nxt and i_ % per == 0:
                    emit_pe(nxt.pop(0))
                p()
            for idx in nxt:
                emit_pe(idx)

    def run_seq(tag, x_d, R, Q0, TQ, M0, cos_d, sin_d, y_d, is_prompt, HF, HB):
        QT = dscr(tag + "QT", [8, 128, TQ], BF16)
        KT = dscr(tag + "KT", [8, 128, R], BF16)
        VT = dscr(tag + "VT", [8, 128, R], BF16)
        XRT = dscr(tag + "XRT", [8, 128, TQ], F32)
        GT = dscr(tag + "GT", [8, 128, TQ], F32)
        X1 = dscr(tag + "X1", [TQ, D], F32)
        X2 = dscr(tag + "X2", [CH, D], F32)
        H2 = dscr(tag + "H2", [128, 16, TQ], BF16)
        SQT, SKT, SVT, SXR, SGT, SX1, SX2, SH2 = (Buf() for _ in range(8))
        hbd = hbds[tag]
        I("dve", lambda e: e.memset(hbd.t[:], 0.0), w=[hbd.b])
        bnd = {}
        for gi_ in range(4):
            for side_, tok_ in ((0, M0 + 512 * gi_ - 1), (1, M0 + 512 * gi_ + 512)):
                if 0 <= tok_ < TQ:
                    bnd[tok_] = 2 * gi_ + side_

        stage(2)
        esA = contextlib.ExitStack()
        alloc_front(esA)
        zb, zf, r1, r2, cst, snt = A["zb"], A["zf"], A["r1"], A["r2"], A["cst"], A["snt"]
        nsteps = min((R + 511) // 512, DEBUG["maxsteps"])

        def front_fn(st):
            s0 = st * 512
            W = min(512, R - s0)
            hTt = A["hT"][st % 2]
            return [(x_d[s0 + ti * 128:s0 + (ti + 1) * 128, :], hTt, ti * 128) for ti in range(W // 128)]

        def proj_fn(st):
            s0 = st * 512
            W = min(512, R - s0)
            hTt = A["hT"][st % 2]
            inq = (s0 >= Q0) and (s0 < Q0 + TQ)
            wq = min(W, Q0 + TQ - s0) if inq else 0
            ct, sn = cst[st % 2], snt[st % 2]
            chunks = list(range(8, 24)) + (list(range(0, 8)) + list(range(24, 40)) if inq else [])

            def body(j, first):
                if first:
                    I("sp", lambda e: e.dma_start(out=ct.t[:, :W], in_=cos_d[:, s0:s0 + W]), w=[ct.b], dma=True)
                    I("sp", lambda e: e.dma_start(out=sn.t[:, :W], in_=sin_d[:, s0:s0 + W]), w=[sn.b], dma=True)
                ps = PS[zk[0] % 2]
                k3 = zk[0] % 3
                zk[0] += 1
                proj_chunk(WIN, j, hTt, W, ps, WBi)
                if tag == "s":
                    next(prep_rest, None)
                if j < 16:
                    z = zb[k3]
                    I("act", lambda e: e.activation(out=z.t[:, :W], in_=ps.t[:, :W], func=AF.Identity), r=[ps.b], w=[z.b])
                    rp = PS[2]
                    I("pe", lambda e: e.matmul(rp.t[:, :W], lhsT=pm.t[:, :], rhs=z.t[:, :W], start=True, stop=True), r=[pm.b, z.b], w=[rp.b])
                    I("dve", lambda e: e.tensor_tensor(out=r1.t[:, :W], in0=ps.t[0:32, :W], in1=ct.t[:, :W], op=ALU.mult), r=[ps.b, ct.b], w=[r1.b])
                    I("dve", lambda e: e.tensor_tensor(out=r2.t[:, :W], in0=rp.t[0:32, :W], in1=sn.t[:, :W], op=ALU.mult), r=[rp.b, sn.b], w=[r2.b])
                    I("dve", lambda e: e.tensor_tensor(out=z.t[0:32, :W], in0=r1.t[:, :W], in1=r2.t[:, :W], op=ALU.add), r=[r1.b, r2.b], w=[z.b])
                    if j < 8:
                        I("pool", lambda e: e.dma_start(out=QT[j, :, s0 - Q0:s0 - Q0 + wq], in_=z.t[:, :wq]), r=[z.b], w=[SQT], dma=True)
                    else:
                        I("pool", lambda e: e.dma_start(out=KT[j - 8, :, s0:s0 + W], in_=z.t[:, :W]), r=[z.b], w=[SKT], dma=True)
                elif j < 24:
                    z = zb[k3]
                    I("act", lambda e: e.activation(out=z.t[:, :W], in_=ps.t[:, :W], func=AF.Identity), r=[ps.b], w=[z.b])
                    I("pool", lambda e: e.dma_start(out=VT[j - 16, :, s0:s0 + W], in_=z.t[:, :W]), r=[z.b], w=[SVT], dma=True)
                else:
                    z = zf[k3]
                    I("act", lambda e: e.activation(out=z.t[:, :W], in_=ps.t[:, :W], func=AF.Identity), r=[ps.b], w=[z.b])
                    if j < 32:
                        I("pool", lambda e: e.dma_start(out=XRT[j - 24, :, s0 - Q0:s0 - Q0 + wq], in_=z.t[:, :wq]), r=[z.b], w=[SXR], dma=True)
                    else:
                        I("pool", lambda e: e.dma_start(out=GT[j - 32, :, s0 - Q0:s0 - Q0 + wq], in_=z.t[:, :wq]), r=[z.b], w=[SGT], dma=True)
            return [(lambda j=j, f=(i == 0): body(j, f)) for i, j in enumerate(chunks)]

        pipelined_steps(nsteps, front_fn, proj_fn)

        if tag == "s":
            for _ in prep_rest:
                pass
        S.barrier()
        esA.close()
        if tag == "s":
            esp.close()
        esq = contextlib.ExitStack()
        mixT = T(esq.enter_context(nc.sbuf_tensor("t_" + tag + "mixT", [128, 16, TQ], BF16)))

        stage(3)
        with contextlib.ExitStack() as es2:
            def sb2(name, shape, dt=F32):
                return T(es2.enter_context(nc.sbuf_tensor("t_" + tag + name, list(shape), dt)))
            RP = R + (1792 if is_prompt else 0)
            kT = [sb2("kT%d" % i, [128, RP], BF16) for i in range(2)]
            vT = [sb2("vT%d" % i, [128, RP], BF16) for i in range(1)]
            if RP > R:
                for t_ in kT + vT:
                    I("dve", lambda e: e.memset(t_.t[:, R:RP], 0.0), w=[t_.b])
            qT = [sb2("qT%d" % i, [128, TQ], BF16) for i in range(2)]
            nblk = [((R // d) + 127) // 128 for d in DILS]
            vc = [sb2("vc%d" % b, [128, DILS[b] * nblk[b], 128], BF16) for b in range(3)]
            pt = [sb2("pt%d" % i, [128, 256], BF16) for i in range(4)]
            rd = [sb2("rd%d" % i, [128, 512]) for i in range(2)]
            pk = 0
            sc = 1.0 / math.sqrt(128.0)
            for h in range(8):
                k_, v_, q_ = kT[h % 2], vT[0], qT[h % 2]
                I("sp", lambda e: e.dma_start(out=k_.t[:, :R], in_=KT[h]), r=[SKT], w=[k_.b], dma=True)
                I("sp", lambda e: e.dma_start(out=v_.t[:, :R], in_=VT[h]), r=[SVT], w=[v_.b], dma=True)
                I("sp", lambda e: e.dma_start(out=q_.t[:], in_=QT[h]), r=[SQT], w=[q_.b], dma=True)
                for b, dil in enumerate(DILS):
                    L = R // dil
                    items = [(r, kb) for r in range(dil) for kb in range(nblk[b])]
                    for i0 in range(0, len(items), 8):
                        grp = items[i0:i0 + 8]
                        pb = PB[(i0 // 8) % 2]
                        for n_, (r, kb) in enumerate(grp):
                            k0 = kb * 128
                            nk = 128
                            a0 = r + dil * k0
                            I("pe", lambda e: e.transpose(pb.t[:nk, n_ * 128:(n_ + 1) * 128], v_.t[:, a0:a0 + dil * (nk - 1) + 1:dil], ident.t[:]),
                              r=[v_.b, ident.b], w=[pb.b], inc=(n_ == len(grp) - 1))
                        for n_, (r, kb) in enumerate(grp):
                            nk = 128
                            idx = r * nblk[b] + kb
                            eng = "act" if (i0 // 8) % 2 else "dve"
                            if eng == "dve":
                                I("dve", lambda e: e.tensor_copy(out=vc[b].t[:nk, idx, :], in_=pb.t[:nk, n_ * 128:(n_ + 1) * 128]), r=[pb.b], w=[vc[b].b])
                            else:
                                I("act", lambda e: e.activation(out=vc[b].t[:nk, idx, :], in_=pb.t[:nk, n_ * 128:(n_ + 1) * 128], func=AF.Identity), r=[pb.b], w=[vc[b].b])
                nbank = (TQ + 511) // 512
                for qb in range(nbank):
                    b0 = qb * 512
                    Wq = min(512, TQ - b0)
                    num, den = PS[2 + (qb % 2) * 2], PS[3 + (qb % 2) * 2]
                    blocks = []
                    for b, dil in enumerate(DILS):
                        L = R // dil
                        for r in range(dil):
                            i0 = (r - (Q0 + b0)) % dil
                            nq = len(range(i0, Wq, dil))
                            if nq <= 0:
                                continue
                            lq0 = (Q0 + b0 + i0 - r) // dil
                            lo = max(0, lq0 - 64)
                            hi = min(L - 1, lq0 + nq - 1 + 64)
                            for kb in range(lo // 128, hi // 128 + 1):
                                k0 = kb * 128
                                nk = 128
                                qa = max(lq0, k0 - 64)
                                qe = min(lq0 + nq, k0 + min(128, L - k0) + 64)
                                n = qe - qa
                                if n <= 0:
                                    continue
                                blocks.append((b, dil, r, kb, k0, nk, qa, n))

                    def emit_S(i):
                        b, dil, r, kb, k0, nk, qa, n = blocks[i]
                        sp_ = PS[(pk + i) % 2]
                        p_ = pt[(pk + i) % 4]
                        ka = r + dil * k0
                        qc = r + dil * qa - Q0
                        I("pe", lambda e: e.matmul(sp_.t[:nk, :n], lhsT=k_.t[:, ka:ka + dil * (nk - 1) + 1:dil],
                                                   rhs=q_.t[:, qc:qc + dil * (n - 1) + 1:dil], start=True, stop=True),
                          r=[k_.b, q_.b], w=[sp_.b])
                        if is_prompt:
                            I("act", lambda e: e.activation(out=p_.t[:nk, :n], in_=sp_.t[:nk, :n], func=AF.Exp, scale=sc,
                                                            bias=kbias.t[:nk, b, r, kb:kb + 1]), r=[sp_.b, kbias.b], w=[p_.b])
                        else:
                            I("act", lambda e: e.activation(out=p_.t[:nk, :n], in_=sp_.t[:nk, :n], func=AF.Exp, scale=sc), r=[sp_.b], w=[p_.b])
                        off = qa - k0 + 64
                        meng = "dve" if i % 3 else "pool"
                        I(meng, lambda e: e.tensor_tensor(out=p_.t[:nk, :n], in0=p_.t[:nk, :n], in1=bmask.t[:nk, off:off + n], op=ALU.mult),
                          r=[p_.b, bmask.b], w=[p_.b])

                    def emit_PV(i):
                        b, dil, r, kb, k0, nk, qa, n = blocks[i]
                        p_ = pt[(pk + i) % 4]
                        c0 = r + dil * qa - Q0 - b0
                        idx = r * nblk[b] + kb
                        I("pe", lambda e: e.matmul(num.t[:, c0:c0 + dil * (n - 1) + 1:dil], lhsT=vc[b].t[:nk, idx, :], rhs=p_.t[:nk, :n],
                                                   start=(i == 0), stop=False, skip_group_check=True), r=[vc[b].b, p_.b], w=[num.b], inc=False)
                        I("pe", lambda e: e.matmul(den.t[:, c0:c0 + dil * (n - 1) + 1:dil], lhsT=ones.t[:nk, :], rhs=p_.t[:nk, :n],
                                                   start=(i == 0), stop=False, skip_group_check=True), r=[ones.b, p_.b], w=[den.b])

                    emit_S(0)
                    for i in range(len(blocks)):
                        if i + 1 < len(blocks):
                            emit_S(i + 1)
                        emit_PV(i)
                    pk += len(blocks)
                    rd_ = rd[qb % 2]
                    I("dve", lambda e: e.tensor_scalar(out=rd_.t[:, :Wq], in0=den.t[:, :Wq], scalar1=1e-30, scalar2=None, op0=ALU.add), r=[den.b], w=[rd_.b])
                    I("dve", lambda e: e.reciprocal(out=rd_.t[:, :Wq], in_=rd_.t[:, :Wq]), r=[rd_.b], w=[rd_.b])
                    I("dve", lambda e: e.tensor_tensor(out=mixT.t[:, h, b0:b0 + Wq], in0=num.t[:, :Wq], in1=rd_.t[:, :Wq], op=ALU.mult),
                      r=[num.b, rd_.b], w=[mixT.b])
            S.barrier()

        stage(4)
        with contextlib.ExitStack() as es2:
            def sb2(name, shape, dt=F32):
                return T(es2.enter_context(nc.sbuf_tensor("t_" + tag + name, list(shape), dt)))
            xrp = sb2("xrp", [128, TQ + 3])
            gt = sb2("gt", [128, TQ])
            u = sb2("u", [128, TQ])
            ub = sb2("ub", [128, TQ], BF16)
            rr = sb2("rr", [128, TQ])
            ii = sb2("ii", [128, TQ])
            aa = sb2("aa", [128, TQ])
            hh = [sb2("hh%d" % i, [128, TQ]) for i in range(2)]
            I("dve", lambda e: e.memset(xrp.t[:, 0:2], 0.0), w=[xrp.b])
            I("dve", lambda e: e.memset(xrp.t[:, TQ + 2:TQ + 3], 0.0), w=[xrp.b])
            for g in range(8):
                I("sp", lambda e: e.dma_start(out=xrp.t[:, 2:TQ + 2], in_=XRT[g]), r=[SXR], w=[xrp.b], dma=True)
                I("sp", lambda e: e.dma_start(out=gt.t[:], in_=GT[g]), r=[SGT], w=[gt.b], dma=True)
                lru_conv(xrp, u, ub, g, TQ)
                for z in range(2):
                    lru_gates(u, ub, rr, ii, aa, z, g, TQ)
                    if is_prompt:
                        I("pool", lambda e: e.tensor_tensor(out=ii.t[:, 0:128], in0=ii.t[:, 0:128], in1=tmask.t[:, 0:128], op=ALU.mult), r=[ii.b, tmask.b], w=[ii.b])
                        I("pool", lambda e: e.tensor_tensor(out=ii.t[:, TQ - 128:TQ], in0=ii.t[:, TQ - 128:TQ], in1=tmask.t[:, 128:256], op=ALU.mult), r=[ii.b, tmask.b], w=[ii.b])
                    if z == 0:
                        lo, hi = (2, TQ) if is_prompt else (0, TQ)
                        init = HF.t[:, g:g + 1] if is_prompt else 0.0
                        if is_prompt:
                            I("dve", lambda e: e.memset(hh[0].t[:, 0:2], 0.0), w=[hh[0].b])
                        I("dve", lambda e: e.tensor_tensor_scan(out=hh[0].t[:, lo:hi], data0=aa.t[:, lo:hi], data1=ii.t[:, lo:hi], initial=init,
                                                                op0=ALU.mult, op1=ALU.add), r=[aa.b, ii.b] + ([HF.b] if is_prompt else []), w=[hh[0].b])
                    else:
                        lo, hi = (0, TQ - 2) if is_prompt else (0, TQ)
                        init = HB.t[:, g:g + 1] if is_prompt else 0.0
                        if is_prompt:
                            I("dve", lambda e: e.memset(hh[1].t[:, TQ - 2:TQ], 0.0), w=[hh[1].b])
                        I("dve", lambda e: e.tensor_tensor_scan(out=hh[1].t[:, lo:hi][:, ::-1], data0=aa.t[:, lo:hi][:, ::-1], data1=ii.t[:, lo:hi][:, ::-1],
                                                                initial=init, op0=ALU.mult, op1=ALU.add), r=[aa.b, ii.b] + ([HB.b] if is_prompt else []), w=[hh[1].b])
                I("pool", lambda e: e.tensor_tensor(out=hh[0].t[:], in0=hh[0].t[:], in1=hh[1].t[:], op=ALU.add), r=[hh[0].b, hh[1].b], w=[hh[0].b])
                I("act", lambda e: e.activation(out=rr.t[:], in_=gt.t[:], func=AF.Gelu), r=[gt.b], w=[rr.b])
                I("dve", lambda e: e.tensor_tensor(out=mixT.t[:, 8 + g, :TQ], in0=hh[0].t[:], in1=rr.t[:], op=ALU.mult), r=[hh[0].b, rr.b], w=[mixT.b])
            S.barrier()

        stage(5)
        with contextlib.ExitStack() as es2:
            def sb2(name, shape, dt=F32):
                return T(es2.enter_context(nc.sbuf_tensor("t_" + tag + name, list(shape), dt)))
            alloc_front(es2, full=False, hw=128)
            hsb, junk, hT = A["hsb"], A["junk"], A["hT"]
            sq = [sb2("sq%d" % i, [128, 256], BF16) for i in range(2)]
            rs = [sb2("rs%d" % i, [128, 256]) for i in range(2)]
            mm = sb2("mm", [128, 16, 256], BF16)
            wo = [sb2("wo%d" % i, [128, 16, 512], BF16) for i in range(2)]
            x1b = [sb2("x1b%d" % i, [128, D]) for i in range(2)]
            nbank = (TQ + 255) // 256
            xk = 0
            for tb in range(nbank):
                b0 = tb * 256
                W = min(256, TQ - b0)
                for part in range(2):
                    ssp = PS[part]
                    for c in range(8):
                        s_ = sq[c % 2]
                        I("pool", lambda e: e.tensor_tensor(out=s_.t[:, :W], in0=mixT.t[:, part * 8 + c, b0:b0 + W], in1=mixT.t[:, part * 8 + c, b0:b0 + W], op=ALU.mult),
                          r=[mixT.b], w=[s_.b])
                        I("pe", lambda e: e.matmul(ssp.t[:, :W], lhsT=ones.t[:, :], rhs=s_.t[:, :W], start=(c == 0), stop=(c == 7)), r=[ones.b, s_.b], w=[ssp.b])
                    r_ = rs[part]
                    I("dve", lambda e: e.tensor_scalar(out=r_.t[:, :W], in0=ssp.t[:, :W], scalar1=1.0 / 1024, scalar2=EPS, op0=ALU.mult, op1=ALU.add), r=[ssp.b], w=[r_.b])
                    I("act", lambda e: e.activation(out=r_.t[:, :W], in_=r_.t[:, :W], func=AF.Sqrt), r=[r_.b], w=[r_.b])
                    I("dve", lambda e: e.reciprocal(out=r_.t[:, :W], in_=r_.t[:, :W]), r=[r_.b], w=[r_.b])
                    for c in range(8):
                        I("dve", lambda e: e.tensor_tensor(out=mm.t[:, part * 8 + c, :W], in0=mixT.t[:, part * 8 + c, b0:b0 + W], in1=r_.t[:, :W], op=ALU.mult),
                          r=[mixT.b, r_.b], w=[mm.b])
                nt = W // 128
                for cg in range(4):
                    w_ = wo[cg % 2]
                    I("sp", lambda e: e.dma_start(out=w_.t[:], in_=WOUT[:, :, cg * 512:(cg + 1) * 512].rearrange("k p c -> p k c")), r=[WBo], w=[w_.b], dma=True)
                    for ti in range(nt):
                        ps = PS[2 + (ti % 2)]
                        if cg == 0:
                            tok = Q0 + b0 + ti * 128
                            I("sp", lambda e: e.dma_start(out=x1b[ti].t[:], in_=x_d[tok:tok + 128, :]), w=[x1b[ti].b], dma=True)
                        for kc in range(16):
                            I("pe", lambda e: e.matmul(ps.t[:, :], lhsT=mm.t[:, kc, ti * 128:(ti + 1) * 128], rhs=w_.t[:, kc, :], start=(kc == 0), stop=(kc == 15)),
                              r=[mm.b, w_.b], w=[ps.b], inc=(kc == 15))
                        I("dve", lambda e: e.tensor_tensor(out=x1b[ti].t[:, cg * 512:(cg + 1) * 512], in0=ps.t[:, :], in1=x1b[ti].t[:, cg * 512:(cg + 1) * 512], op=ALU.add),
                          r=[ps.b, x1b[ti].b], w=[x1b[ti].b])
                for ti in range(nt):
                    tl = b0 + ti * 128
                    x1 = x1b[ti]
                    I("pool", lambda e: e.dma_start(out=X1[tl:tl + 128, :], in_=x1.t[:]), r=[x1.b], w=[SX1], dma=True)
                    k = xk % 2
                    xk += 1
                    hs, s4 = hsb[k], st4[k]
                    I("act", lambda e: e.activation(out=junk.t[:], in_=x1.t[:], func=AF.Square, accum_out=s4.t[:, 0:1]), r=[x1.b], w=[junk.b, s4.b])
                    I("dve", lambda e: e.tensor_scalar(out=s4.t[:, 1:2], in0=s4.t[:, 0:1], scalar1=1.0 / D, scalar2=EPS, op0=ALU.mult, op1=ALU.add), r=[s4.b], w=[s4.b])
                    I("act", lambda e: e.activation(out=s4.t[:, 2:3], in_=s4.t[:, 1:2], func=AF.Sqrt), r=[s4.b], w=[s4.b])
                    I("dve", lambda e: e.reciprocal(out=s4.t[:, 3:4], in_=s4.t[:, 2:3]), r=[s4.b], w=[s4.b])
                    I("act", lambda e: e.activation(out=hs.t[:], in_=x1.t[:], func=AF.Identity, scale=s4.t[:, 3:4]), r=[x1.b, s4.b], w=[hs.b])
                    hTt = hT[k]
                    transpose16(hs, hTt, 0)
                    I("pool", lambda e: e.dma_start(out=H2[:, :, tl:tl + 128], in_=hTt.t[:, :, 0:128]), r=[hTt.b], w=[SH2], dma=True)
                    for tok_, idx_ in bnd.items():
                        if tl <= tok_ < tl + 128:
                            cc_ = tok_ - tl
                            edge_ = is_prompt and idx_ in (0, 7)
                            if edge_:
                                I("dve", lambda e: e.tensor_scalar(out=hbd.t[:, :, idx_:idx_ + 1], in0=hTt.t[:, :, cc_:cc_ + 1], scalar1=vmask.t[:, (0 if idx_ == 0 else 1):(1 if idx_ == 0 else 2)], scalar2=None, op0=ALU.mult),
                                  r=[hTt.b, vmask.b], w=[hbd.b])
                            else:
                                I("dve", lambda e: e.tensor_copy(out=hbd.t[:, :, idx_:idx_ + 1], in_=hTt.t[:, :, cc_:cc_ + 1]), r=[hTt.b], w=[hbd.b])
            S.barrier()
        esq.close()

        stage(6)
        with contextlib.ExitStack() as es2:
            def sb2(name, shape, dt=F32):
                return T(es2.enter_context(nc.sbuf_tensor("t_" + tag + name, list(shape), dt)))
            h2 = [sb2("h2_%d" % i, [128, 16, 514], BF16) for i in range(2)]
            actT = sb2("actT", [128, 48, 512], BF16)
            wd = [sb2("wd%d" % i, [128, 12, 512], BF16) for i in range(2)]
            A["wch"] = [sb2("wch%d" % i, [128, 16, 128], BF16) for i in range(4)]
            wch = A["wch"]
            U = [sb2("U%d" % i, [128, 514]) for i in range(2)]
            cv = [sb2("cv%d" % i, [128, 512]) for i in range(2)]
            gl = sb2("gl", [128, 512])
            x1p = [sb2("x1p%d" % i, [128, 512]) for i in range(2)]
            x2p = [sb2("x2p%d" % i, [128, 512]) for i in range(2)]
            UB = sb2("UB", [128, 96, 8])
            ssq = sb2("ssq", [128, 4, 4])
            s3 = sb2("s3", [128, 4])
            x2r = [sb2("x2r%d" % i, [128, D]) for i in range(1)]
            yt = [sb2("yt%d" % i, [128, D]) for i in range(1)]
            uk = 0
            dk = 0
            for gi in range(4):
                t0 = M0 + gi * 512
                h2t = h2[gi % 2]
                lo = t0 - 1
                hi = t0 + 513
                left_ok = lo >= 0
                right_ok = hi <= TQ
                c_lo = 0 if left_ok else 1
                c_hi = 514 if right_ok else 513
                I("sp", lambda e: e.dma_start(out=h2t.t[:, :, c_lo:c_hi], in_=H2[:, :, lo + c_lo:lo + c_hi]), r=[SH2], w=[h2t.b], dma=True)
                for jj in range(48):
                    for half in range(2):
                        j = jj + 48 * half
                        ps = PS[uk % 2]
                        pbd = PS[2]
                        U_ = U[uk % 2]
                        c_ = cv[half]
                        uk += 1
                        wt = wch[wk[0] % 4]
                        wk[0] += 1
                        I("sp", lambda e: e.dma_start(out=wt.t[:], in_=WUP[j]), r=[WBu], w=[wt.b], dma=True)
                        for kc in range(16):
                            I("pe", lambda e: e.matmul(ps.t[:, :], lhsT=wt.t[:, kc, :], rhs=h2t.t[:, kc, 1:513], start=(kc == 0), stop=(kc == 15)),
                              r=[wt.b, h2t.b], w=[ps.b], inc=(kc == 15))
                        I("act", lambda e: e.activation(out=U_.t[:, 1:513], in_=ps.t[:, :], func=AF.Identity), r=[ps.b], w=[U_.b])
                        if gi == 0:
                            for kc in range(16):
                                I("pe", lambda e: e.matmul(pbd.t[:, 0:8], lhsT=wt.t[:, kc, :], rhs=hbd.t[:, kc, :], start=(kc == 0), stop=(kc == 15)),
                                  r=[wt.b, hbd.b], w=[pbd.b], inc=(kc == 15))
                            I("dve", lambda e: e.tensor_copy(out=UB.t[:, j, :], in_=pbd.t[:, 0:8]), r=[pbd.b], w=[UB.b])
                        I("pool", lambda e: e.tensor_copy(out=U_.t[:, 0:1], in_=UB.t[:, j, 2 * gi:2 * gi + 1]), r=[UB.b], w=[U_.b])
                        I("pool", lambda e: e.tensor_copy(out=U_.t[:, 513:514], in_=UB.t[:, j, 2 * gi + 1:2 * gi + 2]), r=[UB.b], w=[U_.b])
                        I("dve", lambda e: e.tensor_scalar(out=c_.t[:], in0=U_.t[:, 0:512], scalar1=cfw.t[:, j, 0:1], scalar2=cfb.t[:, j:j + 1], op0=ALU.mult, op1=ALU.add),
                          r=[U_.b, cfw.b, cfb.b], w=[c_.b])
                        I("dve", lambda e: e.scalar_tensor_tensor(out=c_.t[:], in0=U_.t[:, 1:513], scalar=cfw.t[:, j, 1:2], in1=c_.t[:], op0=ALU.mult, op1=ALU.add),
                          r=[U_.b, cfw.b, c_.b], w=[c_.b])
                        I("dve", lambda e: e.scalar_tensor_tensor(out=c_.t[:], in0=U_.t[:, 2:514], scalar=cfw.t[:, j, 2:3], in1=c_.t[:], op0=ALU.mult, op1=ALU.add),
                          r=[U_.b, cfw.b, c_.b], w=[c_.b])
                    I("act", lambda e: e.activation(out=gl.t[:], in_=cv[0].t[:], func=AF.Gelu), r=[cv[0].b], w=[gl.b])
                    I("pool", lambda e: e.tensor_tensor(out=actT.t[:, jj, :], in0=gl.t[:], in1=cv[1].t[:], op=ALU.mult), r=[gl.b, cv[1].b], w=[actT.b])
                for cg in range(4):
                    for jb in range(4):
                        w_ = wd[dk % 2]
                        dk += 1
                        I("sp", lambda e: e.dma_start(out=w_.t[:], in_=WDN[jb * 12:(jb + 1) * 12, :, cg * 512:(cg + 1) * 512].rearrange("k p c -> p k c")), r=[WBd], w=[w_.b], dma=True)
                        for ti in range(4):
                            ps = PS[2 + ti]
                            for j2 in range(12):
                                jx = jb * 12 + j2
                                I("pe", lambda e: e.matmul(ps.t[:, :], lhsT=actT.t[:, jx, ti * 128:(ti + 1) * 128], rhs=w_.t[:, j2, :], start=(jx == 0), stop=(jx == 47)),
                                  r=[actT.b, w_.b], w=[ps.b], inc=(j2 == 11))
                    for ti in range(4):
                        ps = PS[2 + ti]
                        tl = t0 + ti * 128
                        to = gi * 512 + ti * 128
                        a_, b_ = x1p[ti % 2], x2p[ti % 2]
                        I("sp", lambda e: e.dma_start(out=a_.t[:], in_=X1[tl:tl + 128, cg * 512:(cg + 1) * 512]), r=[SX1], w=[a_.b], dma=True)
                        I("dve", lambda e: e.tensor_tensor(out=b_.t[:], in0=ps.t[:, :], in1=a_.t[:], op=ALU.add), r=[ps.b, a_.b], w=[b_.b])
                        I("act", lambda e: e.activation(out=a_.t[:], in_=b_.t[:], func=AF.Square, accum_out=ssq.t[:, ti, cg:cg + 1]), r=[b_.b], w=[a_.b, ssq.b])
                        I("pool", lambda e: e.dma_start(out=X2[to:to + 128, cg * 512:(cg + 1) * 512], in_=b_.t[:]), r=[b_.b], w=[SX2], dma=True)
                for ti in range(4):
                    to = gi * 512 + ti * 128
                    xr2, y_ = x2r[0], yt[0]
                    I("dve", lambda e: e.reduce_sum(out=s3.t[:, 0:1], in_=ssq.t[:, ti, :], axis=AX.X), r=[ssq.b], w=[s3.b])
                    I("dve", lambda e: e.tensor_scalar(out=s3.t[:, 1:2], in0=s3.t[:, 0:1], scalar1=1.0 / D, scalar2=EPS, op0=ALU.mult, op1=ALU.add), r=[s3.b], w=[s3.b])
                    I("act", lambda e: e.activation(out=s3.t[:, 2:3], in_=s3.t[:, 1:2], func=AF.Sqrt), r=[s3.b], w=[s3.b])
                    I("dve", lambda e: e.reciprocal(out=s3.t[:, 3:4], in_=s3.t[:, 2:3]), r=[s3.b], w=[s3.b])
                    I("sp", lambda e: e.dma_start(out=xr2.t[:], in_=X2[to:to + 128, :]), r=[SX2], w=[xr2.b], dma=True)
                    I("dve", lambda e: e.scalar_tensor_tensor(out=y_.t[:], in0=xr2.t[:], scalar=s3.t[:, 3:4], in1=gfin.t[:], op0=ALU.mult, op1=ALU.mult),
                      r=[xr2.b, s3.b, gfin.b], w=[y_.b])
                    I("pool", lambda e: e.dma_start(out=y_d[to:to + 128, :], in_=y_.t[:]), r=[y_.b], w=[YOUT], dma=True)
            S.barrier()

    def lru_conv(xrp, u, ub, g, n, cast="pool"):
        I("dve", lambda e: e.tensor_scalar(out=u.t[:, :n], in0=xrp.t[:, 0:n], scalar1=crw.t[:, g, 0:1], scalar2=crb.t[:, g:g + 1], op0=ALU.mult, op1=ALU.add),
          r=[xrp.b, crw.b, crb.b], w=[u.b])
        for j in range(1, 4):
            I("dve", lambda e: e.scalar_tensor_tensor(out=u.t[:, :n], in0=xrp.t[:, j:j + n], scalar=crw.t[:, g, j:j + 1], in1=u.t[:, :n], op0=ALU.mult, op1=ALU.add),
              r=[xrp.b, crw.b, u.b], w=[u.b])
        if cast == "act":
            I("act", lambda e: e.activation(out=ub.t[:, :n], in_=u.t[:, :n], func=AF.Identity), r=[u.b], w=[ub.b])
        elif cast == "pool":
            I("pool", lambda e: e.tensor_copy(out=ub.t[:, :n], in_=u.t[:, :n]), r=[u.b], w=[ub.b])

    def lru_gates_a(u, ub, rr, ii, aa, z, g, n):
        k = 0
        for gi, (dst, bias) in enumerate(((rr, hba), (ii, hbx))):
            for c0 in range(0, n, 512):
                W = min(512, n - c0)
                ps = PS[k % 2]
                k += 1
                I("pe", lambda e: e.matmul(ps.t[:, :W], lhsT=wg.t[:, gi * 16 + z * 8 + g, :], rhs=ub.t[:, c0:c0 + W], start=True, stop=True), r=[wg.b, ub.b], w=[ps.b])
                I("act", lambda e: e.activation(out=dst.t[:, c0:c0 + W], in_=ps.t[:, :W], func=AF.Tanh, scale=0.5, bias=bias.t[:, z, g:g + 1]), r=[ps.b, bias.b], w=[dst.b])

    def lru_gates_b1(u, ub, rr, ii, aa, z, g, n):
        I("act", lambda e: e.activation(out=aa.t[:, :n], in_=rr.t[:, :n], func=AF.Exp, scale=hcl.t[:, z, g:g + 1], bias=hcl.t[:, z, g:g + 1]), r=[rr.b, hcl.b], w=[aa.b])
        I("act", lambda e: e.activation(out=rr.t[:, :n], in_=rr.t[:, :n], func=AF.Exp, scale=cl.t[:, z, g:g + 1], bias=cl.t[:, z, g:g + 1]), r=[rr.b, cl.b], w=[rr.b])

    def lru_gates_b2(u, ub, rr, ii, aa, z, g, n):
        I("act", lambda e: e.activation(out=rr.t[:, :n], in_=rr.t[:, :n], func=AF.Sqrt, scale=-0.25, bias=0.25), r=[rr.b], w=[rr.b])

    def lru_gates_b3(u, ub, rr, ii, aa, z, g, n, split=False):
        I("dve", lambda e: e.scalar_tensor_tensor(out=ii.t[:, :n], in0=ii.t[:, :n], scalar=1.0, in1=u.t[:, :n], op0=ALU.add, op1=ALU.mult), r=[ii.b, u.b], w=[ii.b])
        I("pool", lambda e: e.tensor_tensor(out=ii.t[:, :n], in0=ii.t[:, :n], in1=rr.t[:, :n], op=ALU.mult), r=[ii.b, rr.b], w=[ii.b])

    def lru_gates_b(u, ub, rr, ii, aa, z, g, n, split=False):
        lru_gates_b1(u, ub, rr, ii, aa, z, g, n)
        lru_gates_b2(u, ub, rr, ii, aa, z, g, n)
        lru_gates_b3(u, ub, rr, ii, aa, z, g, n)

    def lru_gates(u, ub, rr, ii, aa, z, g, n):
        lru_gates_a(u, ub, rr, ii, aa, z, g, n)
        lru_gates_b(u, ub, rr, ii, aa, z, g, n)

    YOUT = Buf()

    def prepass():
        SXF = Buf()
        esA = contextlib.ExitStack()
        alloc_front(esA)
        zf = A["zf"]
        DVE_EVAC[0] = True
        wres = T(esA.enter_context(nc.sbuf_tensor("t_wres", [128, 8, 16, 128], BF16)))
        I("sp", lambda e: e.dma_start(out=wres.t[:], in_=WIN[24:32].rearrange("j p k c -> p j k c")), r=[WBi], w=[wres.b], dma=True)
        def front_fn(st):
            s0 = st * 512
            hTt = A["hT"][st % 2]
            return [(xf_d[s0 + ti * 128:s0 + (ti + 1) * 128, :], hTt, ti * 128) for ti in range(4)]

        def proj_fn(st):
            s0 = st * 512
            hTt = A["hT"][st % 2]

            def body(g):
                ps = PS[zk[0] % 2]
                z = zf[zk[0] % 3]
                zk[0] += 1
                for kc in range(16):
                    I("pe", lambda e: e.matmul(ps.t[:, :], lhsT=wres.t[:, g, kc, :], rhs=hTt.t[:, kc, :], start=(kc == 0), stop=(kc == 15)),
                      r=[wres.b, hTt.b], w=[ps.b], inc=(kc == 15))
                I("dve", lambda e: e.tensor_copy(out=z.t[:, :], in_=ps.t[:, :]), r=[ps.b], w=[z.b])
                I("pool", lambda e: e.dma_start(out=XRF[g, :, s0:s0 + 512], in_=z.t[:, :]), r=[z.b], w=[SXF], dma=True)
            return [(lambda g=g: body(g)) for g in range(8)]

        pipelined_steps(SEQ_P // 512, front_fn, proj_fn)
        DVE_EVAC[0] = False
        S.barrier()
        esA.close()
        with contextlib.ExitStack() as es2:
            def sb2(name, shape, dt=F32):
                return T(es2.enter_context(nc.sbuf_tensor("t_pp" + name, list(shape), dt)))
            n = 1024
            NSG = SEQ_P // n
            NB = 4
            UF = dscr("UF", [8, 128, SEQ_P], F32)
            SUF = Buf()
            xrp = [sb2("xrp%d" % i, [128, n + 3]) for i in range(NB)]
            u = [sb2("u%d" % i, [128, n]) for i in range(NB)]
            ub = [sb2("ub%d" % i, [128, n], BF16) for i in range(NB)]
            rr = [sb2("rr%d" % i, [128, n]) for i in range(NB)]
            ii = [sb2("ii%d" % i, [128, n]) for i in range(NB)]
            aa = [sb2("aa%d" % i, [128, n]) for i in range(NB)]
            hh = [sb2("hh%d" % i, [128, n]) for i in range(NB)]
            rec = sb2("rec", [128, 2, 8, 8]); tm = sb2("tm", [128, 8])
            I("dve", lambda e: e.memset(rec.t[:], 0.0), w=[rec.b])
            work = []
            for g in range(8):
                for z in range(2):
                    segs = list(range((2048 * 7 - 127) // n + 1)) if z == 0 else list(range(NSG - 1, (2048 + 126) // n - 1, -1))
                    for si, s in enumerate(segs):
                        work.append((g, z, si, s, si == len(segs) - 1))

            def stage_a(it, part):
                g, z, si, s, last_ = work[it]
                k = it % NB
                x_, u_, ub_, rr_, ii_, aa_ = xrp[k], u[k], ub[k], rr[k], ii[k], aa[k]
                if part == 2:
                    lru_gates_a(u_, ub_, rr_, ii_, aa_, z, g, n)
                    return
                if z == 0 or s > (2048 * 7 - 127) // n:
                    a0 = s * n - 2
                    c_lo = 2 if s == 0 else 0
                    c_hi = n + 2 if s == NSG - 1 else n + 3
                    if s == 0:
                        I("dve", lambda e: e.memset(x_.t[:, 0:2], 0.0), w=[x_.b])
                    if s == NSG - 1:
                        I("dve", lambda e: e.memset(x_.t[:, n + 2:n + 3], 0.0), w=[x_.b])
                    I("sp", lambda e: e.dma_start(out=x_.t[:, c_lo:c_hi], in_=XRF[g, :, a0 + c_lo:a0 + c_hi]), r=[SXF], w=[x_.b], dma=True)
                    lru_conv(x_, u_, ub_, g, n, cast="act")
                    if z == 0:
                        I("pool", lambda e: e.dma_start(out=UF[g, :, s * n:(s + 1) * n], in_=u_.t[:]), r=[u_.b], w=[SUF], dma=True)
                else:
                    I("sp", lambda e: e.dma_start(out=u_.t[:], in_=UF[g, :, s * n:(s + 1) * n]), r=[SUF], w=[u_.b], dma=True)
                    I("act", lambda e: e.activation(out=ub_.t[:], in_=u_.t[:], func=AF.Identity), r=[u_.b], w=[ub_.b])

            def stage_b(it, ph):
                g, z, si, s, last_ = work[it]
                k = it % NB
                u_, ub_, rr_, ii_, aa_, hh_, hp = u[k], ub[k], rr[k], ii[k], aa[k], hh[k], hh[(it - 1) % NB]
                if ph == 1:
                    lru_gates_b1(u_, ub_, rr_, ii_, aa_, z, g, n)
                    return
                if ph == 2:
                    lru_gates_b2(u_, ub_, rr_, ii_, aa_, z, g, n)
                    return
                lru_gates_b3(u_, ub_, rr_, ii_, aa_, z, g, n)
                if z == 0:
                    init = 0.0 if si == 0 else hp.t[:, n - 1:n]
                    I("dve", lambda e: e.tensor_tensor_scan(out=hh_.t[:], data0=aa_.t[:], data1=ii_.t[:], initial=init, op0=ALU.mult, op1=ALU.add),
                      r=[aa_.b, ii_.b, hp.b], w=[hh_.b])
                    for j_ in range(1, 8):
                        tk_ = 2048 * j_ - 127
                        if tk_ // n == s:
                            I("dve", lambda e: e.tensor_copy(out=rec.t[:, 0, g, j_:j_ + 1], in_=hh_.t[:, tk_ % n:tk_ % n + 1]), r=[hh_.b], w=[rec.b])
                else:
                    init = 0.0 if si == 0 else hp.t[:, 0:1]
                    I("dve", lambda e: e.tensor_tensor_scan(out=hh_.t[:, ::-1], data0=aa_.t[:, ::-1], data1=ii_.t[:, ::-1], initial=init, op0=ALU.mult, op1=ALU.add),
                      r=[aa_.b, ii_.b, hp.b], w=[hh_.b])
                    for j_ in range(0, 7):
                        tk_ = 2048 * (j_ + 1) + 126
                        if tk_ // n == s:
                            I("dve", lambda e: e.tensor_copy(out=rec.t[:, 1, g, j_:j_ + 1], in_=hh_.t[:, tk_ % n:tk_ % n + 1]), r=[hh_.b], w=[rec.b])
                if last_:
                    Hx = HF if z == 0 else HB
                    I("dve", lambda e: e.tensor_tensor(out=tm.t[:], in0=rec.t[:, z, g, :], in1=sel.t[:, z, :], op=ALU.mult), r=[rec.b, sel.b], w=[tm.b])
                    I("dve", lambda e: e.reduce_sum(out=Hx.t[:, g:g + 1], in_=tm.t[:], axis=AX.X), r=[tm.b], w=[Hx.b])

            LA = 2
            for it in range(min(LA, len(work))):
                stage_a(it, 1)
                stage_a(it, 2)
            for p in range(0, len(work), 2):
                pair = [it for it in (p, p + 1) if it < len(work)]
                nxt = [it + LA for it in pair if it + LA < len(work)]
                for it in nxt:
                    stage_a(it, 1)
                for ph in (1, 2, 3):
                    for it in pair:
                        stage_b(it, ph)
                for it in nxt:
                    stage_a(it, 2)
            S.barrier()

    try:
        run_seq("s", xs_d, CH, 0, CH, 0, cos_s_d, sin_s_d, ys_d, False, None, None)
        if prompt_on:
            if prepass_on:
                prepass()
            else:
                I("dve", lambda e: e.memset(HF.t[:], 0.0), w=[HF.b])
                I("dve", lambda e: e.memset(HB.t[:], 0.0), w=[HB.b])
            run_seq("p", xp_d, 4352, 1024, 2304, 128, cos_p_d, sin_p_d, yp_d, True, HF, HB)
    except _Stop:
        pass
    S.finish()
    return nc, es


def _rope_tables(pos):
    half = 16
    inv = (500000.0 ** (-np.arange(half, dtype=np.float32) / half)).astype(np.float32)
    ang = pos.astype(np.float32)[None, :] * inv[:, None]
    c = np.cos(ang).astype(np.float32)
    s = np.sin(ang).astype(np.float32)
    return np.concatenate([c, c], 0), np.concatenate([s, s], 0)


def _chunk16(v):
    return np.ascontiguousarray(v.reshape(-1, 128).T)


PROMPT_ON = True
PREPASS_ON = True


def kernel(x_prompt, x_sample, g_mix, w_in, w_out, g_attn_out, g_lru_out, conv_rg_w, conv_rg_b,
           rg_w_a, rg_b_a, rg_w_x, rg_b_x, rg_lam, g_mlp, w_up, conv_ff_w, conv_ff_b, w_down, g_final):
    f = np.float32
    xpf = np.ascontiguousarray(x_prompt[0], dtype=f)
    common = {
        "xf": xpf,
        "w_in": np.ascontiguousarray(w_in[0], f), "w_out": np.ascontiguousarray(w_out[0], f),
        "w_up": np.ascontiguousarray(w_up[0], f), "w_down": np.ascontiguousarray(w_down[0], f),
        "g_mix": _chunk16(g_mix[0]), "g_mlp": _chunk16(g_mlp[0]),
        "g_mo": _chunk16(np.concatenate([g_attn_out[0], g_lru_out[0]])),
        "crw": np.ascontiguousarray(conv_rg_w[0].reshape(4, 8, 128).transpose(2, 1, 0)),
        "crb": _chunk16(conv_rg_b[0]),
        "rg_w_a": np.ascontiguousarray(rg_w_a[0], f), "rg_w_x": np.ascontiguousarray(rg_w_x[0], f),
        "rg_b_a": np.ascontiguousarray(rg_b_a[0].reshape(2, 8, 128).transpose(2, 0, 1)),
        "rg_b_x": np.ascontiguousarray(rg_b_x[0].reshape(2, 8, 128).transpose(2, 0, 1)),
        "rg_lam": np.ascontiguousarray(rg_lam[0].reshape(2, 8, 128).transpose(2, 0, 1)),
        "cfw": np.ascontiguousarray(conv_ff_w[0].reshape(3, 96, 128).transpose(2, 1, 0)),
        "cfb": _chunk16(conv_ff_b[0]),
        "gfin": np.ascontiguousarray(np.broadcast_to(g_final[None, :], (128, D)), f),
    }
    cs, sn = _rope_tables(np.arange(CH))
    common["cos_s"], common["sin_s"] = cs, sn
    pmm = np.zeros((128, 128), f)
    for m in range(16):
        pmm[m + 16, m] = -1.0
        pmm[m, m + 16] = 1.0
    common["pm"] = pmm
    jj = np.arange(128)[:, None]
    cc = np.arange(512)[None, :]
    common["bmask"] = ((cc - jj >= 0) & (cc - jj <= 128)).astype(f)
    common["ident"] = np.eye(128, dtype=f)
    if not (PROMPT_ON and PREPASS_ON):
        common["xf"] = xpf[:128]
    in_maps = []
    for c in range(NCORE):
        m = dict(common)
        m["xs"] = np.ascontiguousarray(x_sample[c], f)
        a0 = c * CH - 1152
        pos = np.arange(a0, a0 + 4352)
        valid = (pos >= 0) & (pos < SEQ_P)
        xp = np.zeros((4352, D), f)
        xp[valid] = xpf[pos[valid]]
        m["xp"] = xp
        cp, sp_ = _rope_tables(np.where(valid, pos, 0))
        m["cos_p"], m["sin_p"] = cp, sp_
        kb = np.zeros((128, 3, 16, 34), f)
        for b, dil in enumerate(DILS):
            L = 4352 // dil
            for r in range(dil):
                for kbi in range((L + 127) // 128):
                    l = kbi * 128 + np.arange(128)
                    t = r + dil * l
                    ok = (l < L) & valid[np.minimum(t, 4351)]
                    kb[:, b, r, kbi] = np.where(ok, 0.0, -30000.0)
        m["kbias"] = kb
        sel = np.zeros((128, 2, 8), f)
        if c > 0:
            sel[:, 0, c] = 1.0
        if c < 7:
            sel[:, 1, c] = 1.0
        m["sel"] = sel
        vm = np.ones((128, 2), f)
        if c == 0:
            vm[:, 0] = 0.0
        if c == 7:
            vm[:, 1] = 0.0
        m["vmask"] = vm
        tm = np.ones((128, 256), f)
        tm[:, 0:128] = ((c * CH - 128 + np.arange(128)) >= 0).astype(f)[None, :]
        tm[:, 128:256] = ((c * CH + CH + np.arange(128)) < SEQ_P).astype(f)[None, :]
        m["tmask"] = tm
        in_maps.append(m)
    nc, es = build(PROMPT_ON, PREPASS_ON)
    res = run_bass_kernel_spmd(nc, in_maps, core_ids=list(range(NCORE)))
    try:
        es.close()
    except AssertionError:
        pass
    if DEBUG["scratch_out"]:
        DEBUG["res"] = res
        DEBUG["in_maps"] = in_maps
    yp = np.concatenate([np.asarray(res.results[c]["yp"], f) for c in range(NCORE)], 0)[None]
    ys = np.stack([np.asarray(res.results[c]["ys"], f) for c in range(NCORE)], 0)
    return (yp, ys)
```

```python
import contextlib
import math
import numpy as np
import concourse.bass as bass
import concourse.mybir as mybir
from concourse.bass_utils import run_bass_kernel_spmd

F32 = mybir.dt.float32
BF16 = mybir.dt.bfloat16
AF = mybir.ActivationFunctionType
ALU = mybir.AluOpType
AX = mybir.AxisListType

D = 2048
NCORE = 8
SEQ_P = 16384
CH = 2048
EPS = 1e-6
NR = 16
NRQ = {"sp": 16, "pool": 4}


class Buf:
    __slots__ = ("w", "r", "excl")

    def __init__(self, excl=False):
        self.w = None
        self.r = {}
        self.excl = excl


class Sched:
    def __init__(self, nc, es):
        self.nc = nc
        self.E = {"pe": nc.tensor, "act": nc.scalar, "dve": nc.vector, "pool": nc.gpsimd, "sp": nc.sync}
        self.cs = {k: es.enter_context(nc.semaphore("c_" + k)) for k in ("pe", "act", "dve", "pool")}
        self.cc = {k: 0 for k in self.cs}
        self.ring = {q: [es.enter_context(nc.semaphore("d_%s%d" % (q, i))) for i in range(NRQ[q])] for q in ("sp", "pool")}
        self.rt = {q: [0] * NRQ[q] for q in self.ring}
        self.rk = {q: 0 for q in self.ring}
        self.waited = {k: {} for k in self.E}
        self.pr = []
        self.pw = []

    def _wait(self, e, sem, val):
        k = id(sem)
        if self.waited[e].get(k, 0) < val:
            self.E[e].wait_ge(sem, val)
            self.waited[e][k] = val

    def I(self, e, fn, r=(), w=(), dma=False, inc=True):
        deps = []
        for b in r:
            if b.w is not None:
                deps.append(b.w)
            if b.excl:
                deps.extend(b.r.values())
        for b in w:
            if b.w is not None:
                deps.append(b.w)
            deps.extend(b.r.values())
        for sem, val in deps:
            if e == "pe" and sem is self.cs["pe"]:
                continue
            self._wait(e, sem, val)
        eng = self.E[e]
        if dma:
            i = self.rk[e] % NRQ[e]
            self.rk[e] += 1
            sem = self.ring[e][i]
            if self.rt[e][i] > 0:
                self._wait(e, sem, self.rt[e][i])
            inst = fn(eng)
            self.rt[e][i] += 16
            inst.then_inc(sem, 16)
            t = (sem, self.rt[e][i])
        else:
            inst = fn(eng)
            if not inc:
                self.pr += list(r)
                self.pw += list(w)
                return
            self.cc[e] += 1
            inst.then_inc(self.cs[e], 1)
            t = (self.cs[e], self.cc[e])
            if e == "pe":
                r = list(r) + self.pr
                w = list(w) + self.pw
                self.pr = []
                self.pw = []
        for b in r:
            b.r[id(t[0])] = t
        for b in w:
            b.w = t
            b.r = {}

    def barrier(self):
        for e in self.E:
            for k in self.cs:
                if k != e and self.cc[k] > 0:
                    self._wait(e, self.cs[k], self.cc[k])
            for q in self.ring:
                for i in range(NRQ[q]):
                    if self.rt[q][i] > 0:
                        self._wait(e, self.ring[q][i], self.rt[q][i])

    def finish(self):
        for q in self.ring:
            for i in range(NRQ[q]):
                if self.rt[q][i] > 0:
                    self._wait("sp", self.ring[q][i], self.rt[q][i])
        for k in self.cs:
            if self.cc[k] > 0:
                self._wait("sp", self.cs[k], self.cc[k])


class T:
    def __init__(self, ap, excl=False):
        self.t = ap
        self.b = Buf(excl)


DILS = (1, 4, 16)
DEBUG = {"stop": 99, "scratch_out": False, "outs": (), "maxsteps": 999, "sub": 9}


class _Stop(Exception):
    pass


def build(prompt_on=True, prepass_on=True):
    nc = bass.Bass("TRN2", target_bir_lowering=False)
    es = contextlib.ExitStack()
    S = Sched(nc, es)
    I = S.I

    def din(name, shape, dt=F32):
        return nc.dram_tensor(name, list(shape), dt, kind="ExternalInput").ap()

    def dscr(name, shape, dt):
        return nc.dram_tensor(name, list(shape), dt, kind=("ExternalOutput" if name in DEBUG["outs"] else "Internal")).ap()

    def stage(n):
        if DEBUG["stop"] <= n:
            raise _Stop()

    xs_d = din("xs", [CH, D])
    xp_d = din("xp", [4352, D])
    xf_d = din("xf", [SEQ_P if prepass_on and prompt_on else 128, D])
    w_in_d = din("w_in", [D, 5120])
    w_out_d = din("w_out", [D, D])
    w_up_d = din("w_up", [D, 12288])
    w_dn_d = din("w_down", [6144, D])
    g_mix_d = din("g_mix", [128, 16])
    g_mlp_d = din("g_mlp", [128, 16])
    g_mo_d = din("g_mo", [128, 16])
    crw_d = din("crw", [128, 8, 4])
    crb_d = din("crb", [128, 8])
    wa_d = din("rg_w_a", [2, 8, 128, 128])
    wx_d = din("rg_w_x", [2, 8, 128, 128])
    ba_d = din("rg_b_a", [128, 2, 8])
    bx_d = din("rg_b_x", [128, 2, 8])
    lam_d = din("rg_lam", [128, 2, 8])
    cfw_d = din("cfw", [128, 96, 3])
    cfb_d = din("cfb", [128, 96])
    gfin_d = din("gfin", [128, D])
    cos_s_d = din("cos_s", [32, CH])
    sin_s_d = din("sin_s", [32, CH])
    cos_p_d = din("cos_p", [32, 4352])
    sin_p_d = din("sin_p", [32, 4352])
    pm_d = din("pm", [128, 128])
    mask_d = din("bmask", [128, 512])
    ident_d = din("ident", [128, 128])
    kb_d = din("kbias", [128, 3, 16, 34])
    sel_d = din("sel", [128, 2, 8])
    vm_d = din("vmask", [128, 2])
    tm_d = din("tmask", [128, 256])
    ys_d = nc.dram_tensor("ys", [CH, D], F32, kind="ExternalOutput").ap()
    yp_d = nc.dram_tensor("yp", [CH, D], F32, kind="ExternalOutput").ap()

    WIN = dscr("WIN", [40, 128, 16, 128], BF16)
    WUP = dscr("WUP", [96, 128, 16, 128], BF16)
    WOUT = dscr("WOUT", [16, 128, D], BF16)
    WDN = dscr("WDN", [48, 128, D], BF16)
    XRF = dscr("XRF", [8, 128, SEQ_P], F32)

    def sb(name, shape, dt=F32):
        return T(es.enter_context(nc.sbuf_tensor("sb_" + name, list(shape), dt)))

    PS = [T(es.enter_context(nc.psum_tensor("ps%d" % i, [128, 512], F32)), True) for i in range(6)]
    PB = [T(es.enter_context(nc.psum_tensor("pb%d" % i, [128, 1024], BF16)), True) for i in range(2)]

    def load(dst, src, q="sp"):
        I(q, lambda e: e.dma_start(out=dst.t[:], in_=src), w=[dst.b], dma=True)

    gmix = sb("gmix", [128, 16]); load(gmix, g_mix_d[:, :])
    gmlp = sb("gmlp", [128, 16]); load(gmlp, g_mlp_d[:, :])
    gmo = sb("gmo", [128, 16]); load(gmo, g_mo_d[:, :])
    crw = sb("crw", [128, 8, 4]); load(crw, crw_d[:, :, :])
    crb = sb("crb", [128, 8]); load(crb, crb_d[:, :])
    ba = sb("ba", [128, 2, 8]); load(ba, ba_d[:, :, :])
    bx = sb("bx", [128, 2, 8]); load(bx, bx_d[:, :, :])
    lam = sb("lam", [128, 2, 8]); load(lam, lam_d[:, :, :])
    cfw = sb("cfw", [128, 96, 3]); load(cfw, cfw_d[:, :, :])
    cfb = sb("cfb", [128, 96]); load(cfb, cfb_d[:, :])
    gfin = sb("gfin", [128, D]); load(gfin, gfin_d[:, :])
    kbias = sb("kbias", [128, 3, 16, 34]); load(kbias, kb_d[:, :, :, :])
    sel = sb("sel", [128, 2, 8]); load(sel, sel_d[:, :, :])
    vmask = sb("vmask", [128, 2]); load(vmask, vm_d[:, :])
    tmask = sb("tmask", [128, 256]); load(tmask, tm_d[:, :])
    tmpc = sb("tmpc", [128, 512])
    bmask = sb("bmaskb", [128, 512], BF16)
    load(tmpc, mask_d[:, :])
    I("dve", lambda e: e.tensor_copy(out=bmask.t[:], in_=tmpc.t[:]), r=[tmpc.b], w=[bmask.b])
    ident = sb("identb", [128, 128], BF16)
    tmpi = sb("tmpi", [128, 128])
    load(tmpi, ident_d[:, :])
    I("dve", lambda e: e.tensor_copy(out=ident.t[:], in_=tmpi.t[:]), r=[tmpi.b], w=[ident.b])
    pm = sb("pmb", [128, 128], BF16)
    tmpp = sb("tmpp", [128, 128])
    load(tmpp, pm_d[:, :])
    I("dve", lambda e: e.tensor_copy(out=pm.t[:], in_=tmpp.t[:]), r=[tmpp.b], w=[pm.b])
    ones = sb("onesb", [128, 128], BF16)
    I("dve", lambda e: e.memset(ones.t[:], 1.0), w=[ones.b])
    cl = sb("cl", [128, 2, 8]); cl2 = sb("cl2", [128, 2, 8])
    I("act", lambda e: e.activation(out=cl.t[:], in_=lam.t[:], func=AF.Exp, scale=-1.0), r=[lam.b], w=[cl.b])
    I("act", lambda e: e.activation(out=cl.t[:], in_=cl.t[:], func=AF.Ln, bias=1.0), r=[cl.b], w=[cl.b])
    I("dve", lambda e: e.tensor_scalar(out=cl2.t[:], in0=cl.t[:], scalar1=-16.0, scalar2=None, op0=ALU.mult), r=[cl.b], w=[cl2.b])
    I("dve", lambda e: e.tensor_scalar(out=cl.t[:], in0=cl.t[:], scalar1=-8.0, scalar2=None, op0=ALU.mult), r=[cl.b], w=[cl.b])
    hba = sb("hba", [128, 2, 8]); hbx = sb("hbx", [128, 2, 8]); hcl = sb("hcl", [128, 2, 8])
    I("dve", lambda e: e.tensor_scalar(out=hba.t[:], in0=ba.t[:], scalar1=0.5, scalar2=None, op0=ALU.mult), r=[ba.b], w=[hba.b])
    I("dve", lambda e: e.tensor_scalar(out=hbx.t[:], in0=bx.t[:], scalar1=0.5, scalar2=None, op0=ALU.mult), r=[bx.b], w=[hbx.b])
    I("dve", lambda e: e.tensor_scalar(out=hcl.t[:], in0=cl.t[:], scalar1=0.5, scalar2=None, op0=ALU.mult), r=[cl.b], w=[hcl.b])
    wg = sb("wg", [128, 32, 128], BF16)
    esg = contextlib.ExitStack()
    wgt = T(esg.enter_context(nc.sbuf_tensor("wgt", [128, 16, 128], F32)))
    for gi, src in enumerate((wa_d, wx_d)):
        I("sp", lambda e, src=src: e.dma_start(out=wgt.t[:], in_=src.rearrange("z g i j -> i (z g) j")), w=[wgt.b], dma=True)
        I("dve", lambda e, gi=gi: e.tensor_copy(out=wg.t[:, gi * 16:(gi + 1) * 16, :], in_=wgt.t[:]), r=[wgt.b], w=[wg.b])

    S.barrier()
    esg.close()

    st4 = [sb("st4_%d" % i, [128, 4]) for i in range(3)]
    HF = sb("HF", [128, 8])
    HB = sb("HB", [128, 8])
    hbds = {"s": sb("hbd_s", [128, 16, 8], BF16), "p": sb("hbd_p", [128, 16, 8], BF16)}
    esp = contextlib.ExitStack()
    wst = [T(esp.enter_context(nc.sbuf_tensor("wst%d" % i, [128, 2048], F32))) for i in range(4)]
    wsb = [T(esp.enter_context(nc.sbuf_tensor("wsb%d" % i, [128, 2048], BF16))) for i in range(4)]
    cnt = [0]

    def prep(Wd, K, N, g, dst, tiled, WB):
        for kc in range(K // 128):
            for cb in range(0, N, 2048):
                cw = min(2048, N - cb)
                k = cnt[0] % 4
                cnt[0] += 1
                a, b = wst[k], wsb[k]
                I("sp", lambda e: e.dma_start(out=a.t[:, :cw], in_=Wd[kc * 128:(kc + 1) * 128, cb:cb + cw]), w=[a.b], dma=True)
                if g is not None:
                    I("dve", lambda e: e.tensor_scalar(out=b.t[:, :cw], in0=a.t[:, :cw], scalar1=g.t[:, kc:kc + 1], scalar2=None, op0=ALU.mult), r=[a.b, g.b], w=[b.b])
                else:
                    I("dve", lambda e: e.tensor_copy(out=b.t[:, :cw], in_=a.t[:, :cw]), r=[a.b], w=[b.b])
                if tiled:
                    j0 = cb // 128
                    I("pool", lambda e: e.dma_start(out=dst[j0:j0 + cw // 128, :, kc, :].rearrange("j p c -> p j c"),
                                                    in_=b.t[:, :cw].rearrange("p (j c) -> p j c", c=128)), r=[b.b], w=[WB], dma=True)
                else:
                    I("pool", lambda e: e.dma_start(out=dst[kc, :, cb:cb + cw], in_=b.t[:, :cw]), r=[b.b], w=[WB], dma=True)
                yield None

    WBi, WBo, WBu, WBd = Buf(), Buf(), Buf(), Buf()
    import itertools
    for _ in prep(w_in_d, D, 5120, gmix, WIN, True, WBi):
        pass
    prep_rest = itertools.chain(prep(w_out_d, D, D, gmo, WOUT, False, WBo), prep(w_up_d, D, 12288, gmlp, WUP, True, WBu),
                                prep(w_dn_d, 6144, D, None, WDN, False, WBd))

    A = {}
    uid = [0]

    def alloc_front(esx, full=True, hw=512):
        uid[0] += 1
        p = "f%d_" % uid[0]

        def sbx(name, shape, dt=F32):
            return T(esx.enter_context(nc.sbuf_tensor(p + name, list(shape), dt)))
        A["hsb"] = [sbx("hs%d" % i, [128, D], BF16) for i in range(3 if full else 2)]
        A["junk"] = sbx("junk", [128, D], BF16)
        A["hT"] = [sbx("hT%d" % i, [128, 16, hw], BF16) for i in range(2)]
        if full:
            A["xtb"] = [sbx("xt%d" % i, [128, D]) for i in range(3)]
            A["wch"] = [sbx("wch%d" % i, [128, 16, 128], BF16) for i in range(4)]
            A["zb"] = [sbx("zb%d" % i, [128, 512], BF16) for i in range(3)]
            A["zf"] = [sbx("zf%d" % i, [128, 512]) for i in range(3)]
            A["r1"] = sbx("r1", [32, 512]); A["r2"] = sbx("r2", [32, 512])
            A["cst"] = [sbx("cst%d" % i, [32, 512]) for i in range(2)]
            A["snt"] = [sbx("snt%d" % i, [32, 512]) for i in range(2)]
    wk = [0]
    tk = [0]
    DVE_EVAC = [False]

    def front_act(xsrc_rows):
        k = tk[0] % 3
        tk[0] += 1
        xt, hs, s4 = A["xtb"][k], A["hsb"][k], st4[k]
        junk = A["junk"]
        I("sp", lambda e: e.dma_start(out=xt.t[:], in_=xsrc_rows), w=[xt.b], dma=True)
        I("act", lambda e: e.activation(out=junk.t[:], in_=xt.t[:], func=AF.Square, accum_out=s4.t[:, 0:1]), r=[xt.b], w=[junk.b, s4.b])
        I("dve", lambda e: e.tensor_scalar(out=s4.t[:, 1:2], in0=s4.t[:, 0:1], scalar1=1.0 / D, scalar2=EPS, op0=ALU.mult, op1=ALU.add), r=[s4.b], w=[s4.b])
        I("act", lambda e: e.activation(out=s4.t[:, 2:3], in_=s4.t[:, 1:2], func=AF.Sqrt), r=[s4.b], w=[s4.b])
        I("dve", lambda e: e.reciprocal(out=s4.t[:, 3:4], in_=s4.t[:, 2:3]), r=[s4.b], w=[s4.b])
        I("act", lambda e: e.activation(out=hs.t[:], in_=xt.t[:], func=AF.Identity, scale=s4.t[:, 3:4]), r=[xt.b, s4.b], w=[hs.b])
        return hs

    def transpose16(hs, hTt, col0):
        for half in range(2):
            pb = PB[half]
            for kk in range(8):
                kc = half * 8 + kk
                I("pe", lambda e: e.transpose(pb.t[:, kk * 128:(kk + 1) * 128], hs.t[:, kc * 128:(kc + 1) * 128], ident.t[:]),
                  r=[hs.b, ident.b], w=[pb.b], inc=(kk == 7))
            eng = "dve" if (half == 0 or DVE_EVAC[0]) else "act"
            src = pb.t[:].rearrange("p (k c) -> p k c", c=128)
            dst = hTt.t[:, half * 8:(half + 1) * 8, col0:col0 + 128]
            if eng == "dve":
                I("dve", lambda e: e.tensor_copy(out=dst, in_=src), r=[pb.b], w=[hTt.b])
            else:
                I("act", lambda e: e.activation(out=dst, in_=src, func=AF.Identity), r=[pb.b], w=[hTt.b])

    def proj_chunk(Wt, j, hTt, W, ps, WB):
        wt = A["wch"][wk[0] % 4]
        wk[0] += 1
        I("sp", lambda e: e.dma_start(out=wt.t[:], in_=Wt[j]), r=[WB], w=[wt.b], dma=True)
        for kc in range(16):
            I("pe", lambda e: e.matmul(ps.t[:, :W], lhsT=wt.t[:, kc, :], rhs=hTt.t[:, kc, :W], start=(kc == 0), stop=(kc == 15)),
              r=[wt.b, hTt.b], w=[ps.b], inc=(kc == 15))

    zk = [0]

    def pipelined_steps(nsteps, front_fn, proj_fn):
        tiles = []
        first_of = []
        for st in range(nsteps):
            first_of.append(len(tiles))
            tiles += front_fn(st)
        first_of.append(len(tiles))
        hs_of = {}
        na = [0]
        LEAD = 2

        def emit_pe(idx):
            while na[0] <= min(idx + LEAD, len(tiles) - 1):
                hs_of[na[0]] = front_act(tiles[na[0]][0])
                na[0] += 1
            transpose16(hs_of.pop(idx), tiles[idx][1], tiles[idx][2])

        for idx in range(first_of[0], first_of[1]):
            emit_pe(idx)
        for st in range(nsteps):
            nxt = list(range(first_of[st + 1], first_of[st + 2])) if st + 1 < nsteps else []
            pj = proj_fn(st)
            per = max(1, len(pj) // max(1, len(nxt)))
            for i_, p in enumerate(pj):
                if nxt and i_ % per == 0:
                    emit_pe(nxt.pop(0))
                p()
            for idx in nxt:
                emit_pe(idx)

    def run_seq(tag, x_d, R, Q0, TQ, M0, cos_d, sin_d, y_d, is_prompt, HF, HB):
        QT = dscr(tag + "QT", [8, 128, TQ], BF16)
        KT = dscr(tag + "KT", [8, 128, R], BF16)
        VT = dscr(tag + "VT", [8, 128, R], BF16)
        XRT = dscr(tag + "XRT", [8, 128, TQ], F32)
        GT = dscr(tag + "GT", [8, 128, TQ], F32)
        X1 = dscr(tag + "X1", [TQ, D], F32)
        X2 = dscr(tag + "X2", [CH, D], F32)
        H2 = dscr(tag + "H2", [128, 16, TQ], BF16)
        SQT, SKT, SVT, SXR, SGT, SX1, SX2, SH2 = (Buf() for _ in range(8))
        hbd = hbds[tag]
        I("dve", lambda e: e.memset(hbd.t[:], 0.0), w=[hbd.b])
        bnd = {}
        for gi_ in range(4):
            for side_, tok_ in ((0, M0 + 512 * gi_ - 1), (1, M0 + 512 * gi_ + 512)):
                if 0 <= tok_ < TQ:
                    bnd[tok_] = 2 * gi_ + side_

        stage(2)
        esA = contextlib.ExitStack()
        alloc_front(esA)
        zb, zf, r1, r2, cst, snt = A["zb"], A["zf"], A["r1"], A["r2"], A["cst"], A["snt"]
        nsteps = min((R + 511) // 512, DEBUG["maxsteps"])

        def front_fn(st):
            s0 = st * 512
            W = min(512, R - s0)
            hTt = A["hT"][st % 2]
            return [(x_d[s0 + ti * 128:s0 + (ti + 1) * 128, :], hTt, ti * 128) for ti in range(W // 128)]

        def proj_fn(st):
            s0 = st * 512
            W = min(512, R - s0)
            hTt = A["hT"][st % 2]
            inq = (s0 >= Q0) and (s0 < Q0 + TQ)
            wq = min(W, Q0 + TQ - s0) if inq else 0
            ct, sn = cst[st % 2], snt[st % 2]
            chunks = list(range(8, 24)) + (list(range(0, 8)) + list(range(24, 40)) if inq else [])

            def body(j, first):
                if first:
                    I("sp", lambda e: e.dma_start(out=ct.t[:, :W], in_=cos_d[:, s0:s0 + W]), w=[ct.b], dma=True)
                    I("sp", lambda e: e.dma_start(out=sn.t[:, :W], in_=sin_d[:, s0:s0 + W]), w=[sn.b], dma=True)
                ps = PS[zk[0] % 2]
                k3 = zk[0] % 3
                zk[0] += 1
                proj_chunk(WIN, j, hTt, W, ps, WBi)
                if tag == "s":
                    next(prep_rest, None)
                if j < 16:
                    z = zb[k3]
                    I("act", lambda e: e.activation(out=z.t[:, :W], in_=ps.t[:, :W], func=AF.Identity), r=[ps.b], w=[z.b])
                    rp = PS[2]
                    I("pe", lambda e: e.matmul(rp.t[:, :W], lhsT=pm.t[:, :], rhs=z.t[:, :W], start=True, stop=True), r=[pm.b, z.b], w=[rp.b])
                    I("dve", lambda e: e.tensor_tensor(out=r1.t[:, :W], in0=ps.t[0:32, :W], in1=ct.t[:, :W], op=ALU.mult), r=[ps.b, ct.b], w=[r1.b])
                    I("dve", lambda e: e.tensor_tensor(out=r2.t[:, :W], in0=rp.t[0:32, :W], in1=sn.t[:, :W], op=ALU.mult), r=[rp.b, sn.b], w=[r2.b])
                    I("dve", lambda e: e.tensor_tensor(out=z.t[0:32, :W], in0=r1.t[:, :W], in1=r2.t[:, :W], op=ALU.add), r=[r1.b, r2.b], w=[z.b])
                    if j < 8:
                        I("pool", lambda e: e.dma_start(out=QT[j, :, s0 - Q0:s0 - Q0 + wq], in_=z.t[:, :wq]), r=[z.b], w=[SQT], dma=True)
                    else:
                        I("pool", lambda e: e.dma_start(out=KT[j - 8, :, s0:s0 + W], in_=z.t[:, :W]), r=[z.b], w=[SKT], dma=True)
                elif j < 24:
                    z = zb[k3]
                    I("act", lambda e: e.activation(out=z.t[:, :W], in_=ps.t[:, :W], func=AF.Identity), r=[ps.b], w=[z.b])
                    I("pool", lambda e: e.dma_start(out=VT[j - 16, :, s0:s0 + W], in_=z.t[:, :W]), r=[z.b], w=[SVT], dma=True)
                else:
                    z = zf[k3]
                    I("act", lambda e: e.activation(out=z.t[:, :W], in_=ps.t[:, :W], func=AF.Identity), r=[ps.b], w=[z.b])
                    if j < 32:
                        I("pool", lambda e: e.dma_start(out=XRT[j - 24, :, s0 - Q0:s0 - Q0 + wq], in_=z.t[:, :wq]), r=[z.b], w=[SXR], dma=True)
                    else:
                        I("pool", lambda e: e.dma_start(out=GT[j - 32, :, s0 - Q0:s0 - Q0 + wq], in_=z.t[:, :wq]), r=[z.b], w=[SGT], dma=True)
            return [(lambda j=j, f=(i == 0): body(j, f)) for i, j in enumerate(chunks)]

        pipelined_steps(nsteps, front_fn, proj_fn)

        if tag == "s":
            for _ in prep_rest:
                pass
        S.barrier()
        esA.close()
        if tag == "s":
            esp.close()
        esq = contextlib.ExitStack()
        mixT = T(esq.enter_context(nc.sbuf_tensor("t_" + tag + "mixT", [128, 16, TQ], BF16)))

        stage(3)
        with contextlib.ExitStack() as es2:
            def sb2(name, shape, dt=F32):
                return T(es2.enter_context(nc.sbuf_tensor("t_" + tag + name, list(shape), dt)))
            RP = R + (1792 if is_prompt else 0)
            kT = [sb2("kT%d" % i, [128, RP], BF16) for i in range(2)]
            vT = [sb2("vT%d" % i, [128, RP], BF16) for i in range(1)]
            if RP > R:
                for t_ in kT + vT:
                    I("dve", lambda e: e.memset(t_.t[:, R:RP], 0.0), w=[t_.b])
            qT = [sb2("qT%d" % i, [128, TQ], BF16) for i in range(2)]
            nblk = [((R // d) + 127) // 128 for d in DILS]
            vc = [sb2("vc%d" % b, [128, DILS[b] * nblk[b], 128], BF16) for b in range(3)]
            pt = [sb2("pt%d" % i, [128, 256], BF16) for i in range(4)]
            rd = [sb2("rd%d" % i, [128, 512]) for i in range(2)]
            pk = 0
            sc = 1.0 / math.sqrt(128.0)
            for h in range(8):
                k_, v_, q_ = kT[h % 2], vT[0], qT[h % 2]
                I("sp", lambda e: e.dma_start(out=k_.t[:, :R], in_=KT[h]), r=[SKT], w=[k_.b], dma=True)
                I("sp", lambda e: e.dma_start(out=v_.t[:, :R], in_=VT[h]), r=[SVT], w=[v_.b], dma=True)
                I("sp", lambda e: e.dma_start(out=q_.t[:], in_=QT[h]), r=[SQT], w=[q_.b], dma=True)
                for b, dil in enumerate(DILS):
                    L = R // dil
                    items = [(r, kb) for r in range(dil) for kb in range(nblk[b])]
                    for i0 in range(0, len(items), 8):
                        grp = items[i0:i0 + 8]
                        pb = PB[(i0 // 8) % 2]
                        for n_, (r, kb) in enumerate(grp):
                            k0 = kb * 128
                            nk = 128
                            a0 = r + dil * k0
                            I("pe", lambda e: e.transpose(pb.t[:nk, n_ * 128:(n_ + 1) * 128], v_.t[:, a0:a0 + dil * (nk - 1) + 1:dil], ident.t[:]),
                              r=[v_.b, ident.b], w=[pb.b], inc=(n_ == len(grp) - 1))
                        for n_, (r, kb) in enumerate(grp):
                            nk = 128
                            idx = r * nblk[b] + kb
                            eng = "act" if (i0 // 8) % 2 else "dve"
                            if eng == "dve":
                                I("dve", lambda e: e.tensor_copy(out=vc[b].t[:nk, idx, :], in_=pb.t[:nk, n_ * 128:(n_ + 1) * 128]), r=[pb.b], w=[vc[b].b])
                            else:
                                I("act", lambda e: e.activation(out=vc[b].t[:nk, idx, :], in_=pb.t[:nk, n_ * 128:(n_ + 1) * 128], func=AF.Identity), r=[pb.b], w=[vc[b].b])
                nbank = (TQ + 511) // 512
                for qb in range(nbank):
                    b0 = qb * 512
                    Wq = min(512, TQ - b0)
                    num, den = PS[2 + (qb % 2) * 2], PS[3 + (qb % 2) * 2]
                    blocks = []
                    for b, dil in enumerate(DILS):
                        L = R // dil
                        for r in range(dil):
                            i0 = (r - (Q0 + b0)) % dil
                            nq = len(range(i0, Wq, dil))
                            if nq <= 0:
                                continue
                            lq0 = (Q0 + b0 + i0 - r) // dil
                            lo = max(0, lq0 - 64)
                            hi = min(L - 1, lq0 + nq - 1 + 64)
                            for kb in range(lo // 128, hi // 128 + 1):
                                k0 = kb * 128
                                nk = 128
                                qa = max(lq0, k0 - 64)
                                qe = min(lq0 + nq, k0 + min(128, L - k0) + 64)
                                n = qe - qa
                                if n <= 0:
                                    continue
                                blocks.append((b, dil, r, kb, k0, nk, qa, n))

                    def emit_S(i):
                        b, dil, r, kb, k0, nk, qa, n = blocks[i]
                        sp_ = PS[(pk + i) % 2]
                        p_ = pt[(pk + i) % 4]
                        ka = r + dil * k0
                        qc = r + dil * qa - Q0
                        I("pe", lambda e: e.matmul(sp_.t[:nk, :n], lhsT=k_.t[:, ka:ka + dil * (nk - 1) + 1:dil],
                                                   rhs=q_.t[:, qc:qc + dil * (n - 1) + 1:dil], start=True, stop=True),
                          r=[k_.b, q_.b], w=[sp_.b])
                        if is_prompt:
                            I("act", lambda e: e.activation(out=p_.t[:nk, :n], in_=sp_.t[:nk, :n], func=AF.Exp, scale=sc,
                                                            bias=kbias.t[:nk, b, r, kb:kb + 1]), r=[sp_.b, kbias.b], w=[p_.b])
                        else:
                            I("act", lambda e: e.activation(out=p_.t[:nk, :n], in_=sp_.t[:nk, :n], func=AF.Exp, scale=sc), r=[sp_.b], w=[p_.b])
                        off = qa - k0 + 64
                        meng = "dve" if i % 3 else "pool"
                        I(meng, lambda e: e.tensor_tensor(out=p_.t[:nk, :n], in0=p_.t[:nk, :n], in1=bmask.t[:nk, off:off + n], op=ALU.mult),
                          r=[p_.b, bmask.b], w=[p_.b])

                    def emit_PV(i):
                        b, dil, r, kb, k0, nk, qa, n = blocks[i]
                        p_ = pt[(pk + i) % 4]
                        c0 = r + dil * qa - Q0 - b0
                        idx = r * nblk[b] + kb
                        I("pe", lambda e: e.matmul(num.t[:, c0:c0 + dil * (n - 1) + 1:dil], lhsT=vc[b].t[:nk, idx, :], rhs=p_.t[:nk, :n],
                                                   start=(i == 0), stop=False, skip_group_check=True), r=[vc[b].b, p_.b], w=[num.b], inc=False)
                        I("pe", lambda e: e.matmul(den.t[:, c0:c0 + dil * (n - 1) + 1:dil], lhsT=ones.t[:nk, :], rhs=p_.t[:nk, :n],
                                                   start=(i == 0), stop=False, skip_group_check=True), r=[ones.b, p_.b], w=[den.b])

                    emit_S(0)
                    for i in range(len(blocks)):
                        if i + 1 < len(blocks):
                            emit_S(i + 1)
                        emit_PV(i)
                    pk += len(blocks)
                    rd_ = rd[qb % 2]
                    I("dve", lambda e: e.tensor_scalar(out=rd_.t[:, :Wq], in0=den.t[:, :Wq], scalar1=1e-30, scalar2=None, op0=ALU.add), r=[den.b], w=[rd_.b])
                    I("dve", lambda e: e.reciprocal(out=rd_.t[:, :Wq], in_=rd_.t[:, :Wq]), r=[rd_.b], w=[rd_.b])
                    I("dve", lambda e: e.tensor_tensor(out=mixT.t[:, h, b0:b0 + Wq], in0=num.t[:, :Wq], in1=rd_.t[:, :Wq], op=ALU.mult),
                      r=[num.b, rd_.b], w=[mixT.b])
            S.barrier()

        stage(4)
        with contextlib.ExitStack() as es2:
            def sb2(name, shape, dt=F32):
                return T(es2.enter_context(nc.sbuf_tensor("t_" + tag + name, list(shape), dt)))
            xrp = sb2("xrp", [128, TQ + 3])
            gt = sb2("gt", [128, TQ])
            u = sb2("u", [128, TQ])
            ub = sb2("ub", [128, TQ], BF16)
            rr = sb2("rr", [128, TQ])
            ii = sb2("ii", [128, TQ])
            aa = sb2("aa", [128, TQ])
            hh = [sb2("hh%d" % i, [128, TQ]) for i in range(2)]
            I("dve", lambda e: e.memset(xrp.t[:, 0:2], 0.0), w=[xrp.b])
            I("dve", lambda e: e.memset(xrp.t[:, TQ + 2:TQ + 3], 0.0), w=[xrp.b])
            for g in range(8):
                I("sp", lambda e: e.dma_start(out=xrp.t[:, 2:TQ + 2], in_=XRT[g]), r=[SXR], w=[xrp.b], dma=True)
                I("sp", lambda e: e.dma_start(out=gt.t[:], in_=GT[g]), r=[SGT], w=[gt.b], dma=True)
                lru_conv(xrp, u, ub, g, TQ, cast="act")
                for z in range(2):
                    lru_gates(u, ub, rr, ii, aa, z, g, TQ)
                    if is_prompt:
                        I("pool", lambda e: e.tensor_tensor(out=ii.t[:, 0:128], in0=ii.t[:, 0:128], in1=tmask.t[:, 0:128], op=ALU.mult), r=[ii.b, tmask.b], w=[ii.b])
                        I("pool", lambda e: e.tensor_tensor(out=ii.t[:, TQ - 128:TQ], in0=ii.t[:, TQ - 128:TQ], in1=tmask.t[:, 128:256], op=ALU.mult), r=[ii.b, tmask.b], w=[ii.b])
                    if z == 0:
                        lo, hi = (2, TQ) if is_prompt else (0, TQ)
                        init = HF.t[:, g:g + 1] if is_prompt else 0.0
                        if is_prompt:
                            I("dve", lambda e: e.memset(hh[0].t[:, 0:2], 0.0), w=[hh[0].b])
                        I("dve", lambda e: e.tensor_tensor_scan(out=hh[0].t[:, lo:hi], data0=aa.t[:, lo:hi], data1=ii.t[:, lo:hi], initial=init,
                                                                op0=ALU.mult, op1=ALU.add), r=[aa.b, ii.b] + ([HF.b] if is_prompt else []), w=[hh[0].b])
                    else:
                        lo, hi = (0, TQ - 2) if is_prompt else (0, TQ)
                        init = HB.t[:, g:g + 1] if is_prompt else 0.0
                        if is_prompt:
                            I("dve", lambda e: e.memset(hh[1].t[:, TQ - 2:TQ], 0.0), w=[hh[1].b])
                        I("dve", lambda e: e.tensor_tensor_scan(out=hh[1].t[:, lo:hi][:, ::-1], data0=aa.t[:, lo:hi][:, ::-1], data1=ii.t[:, lo:hi][:, ::-1],
                                                                initial=init, op0=ALU.mult, op1=ALU.add), r=[aa.b, ii.b] + ([HB.b] if is_prompt else []), w=[hh[1].b])
                I("pool", lambda e: e.tensor_tensor(out=hh[0].t[:], in0=hh[0].t[:], in1=hh[1].t[:], op=ALU.add), r=[hh[0].b, hh[1].b], w=[hh[0].b])
                I("act", lambda e: e.activation(out=rr.t[:], in_=gt.t[:], func=AF.Gelu), r=[gt.b], w=[rr.b])
                I("dve", lambda e: e.tensor_tensor(out=mixT.t[:, 8 + g, :TQ], in0=hh[0].t[:], in1=rr.t[:], op=ALU.mult), r=[hh[0].b, rr.b], w=[mixT.b])
            S.barrier()

        stage(5)
        with contextlib.ExitStack() as es2:
            def sb2(name, shape, dt=F32):
                return T(es2.enter_context(nc.sbuf_tensor("t_" + tag + name, list(shape), dt)))
            alloc_front(es2, full=False, hw=128)
            hsb, junk, hT = A["hsb"], A["junk"], A["hT"]
            sq = [sb2("sq%d" % i, [128, 256], BF16) for i in range(2)]
            rs = [sb2("rs%d" % i, [128, 256]) for i in range(2)]
            mm = sb2("mm", [128, 16, 256], BF16)
            wo = [sb2("wo%d" % i, [128, 16, 512], BF16) for i in range(2)]
            x1b = [sb2("x1b%d" % i, [128, D]) for i in range(2)]
            nbank = (TQ + 255) // 256
            xk = 0
            for tb in range(nbank):
                b0 = tb * 256
                W = min(256, TQ - b0)
                for part in range(2):
                    ssp = PS[part]
                    for c in range(8):
                        s_ = sq[c % 2]
                        I("pool", lambda e: e.tensor_tensor(out=s_.t[:, :W], in0=mixT.t[:, part * 8 + c, b0:b0 + W], in1=mixT.t[:, part * 8 + c, b0:b0 + W], op=ALU.mult),
                          r=[mixT.b], w=[s_.b])
                        I("pe", lambda e: e.matmul(ssp.t[:, :W], lhsT=ones.t[:, :], rhs=s_.t[:, :W], start=(c == 0), stop=(c == 7)), r=[ones.b, s_.b], w=[ssp.b])
                    r_ = rs[part]
                    I("dve", lambda e: e.tensor_scalar(out=r_.t[:, :W], in0=ssp.t[:, :W], scalar1=1.0 / 1024, scalar2=EPS, op0=ALU.mult, op1=ALU.add), r=[ssp.b], w=[r_.b])
                    I("act", lambda e: e.activation(out=r_.t[:, :W], in_=r_.t[:, :W], func=AF.Sqrt), r=[r_.b], w=[r_.b])
                    I("dve", lambda e: e.reciprocal(out=r_.t[:, :W], in_=r_.t[:, :W]), r=[r_.b], w=[r_.b])
                    for c in range(8):
                        I("dve", lambda e: e.tensor_tensor(out=mm.t[:, part * 8 + c, :W], in0=mixT.t[:, part * 8 + c, b0:b0 + W], in1=r_.t[:, :W], op=ALU.mult),
                          r=[mixT.b, r_.b], w=[mm.b])
                nt = W // 128
                for cg in range(4):
                    w_ = wo[cg % 2]
                    I("sp", lambda e: e.dma_start(out=w_.t[:], in_=WOUT[:, :, cg * 512:(cg + 1) * 512].rearrange("k p c -> p k c")), r=[WBo], w=[w_.b], dma=True)
                    for ti in range(nt):
                        ps = PS[2 + (ti % 2)]
                        if cg == 0:
                            tok = Q0 + b0 + ti * 128
                            I("sp", lambda e: e.dma_start(out=x1b[ti].t[:], in_=x_d[tok:tok + 128, :]), w=[x1b[ti].b], dma=True)
                        for kc in range(16):
                            I("pe", lambda e: e.matmul(ps.t[:, :], lhsT=mm.t[:, kc, ti * 128:(ti + 1) * 128], rhs=w_.t[:, kc, :], start=(kc == 0), stop=(kc == 15)),
                              r=[mm.b, w_.b], w=[ps.b], inc=(kc == 15))
                        I("dve", lambda e: e.tensor_tensor(out=x1b[ti].t[:, cg * 512:(cg + 1) * 512], in0=ps.t[:, :], in1=x1b[ti].t[:, cg * 512:(cg + 1) * 512], op=ALU.add),
                          r=[ps.b, x1b[ti].b], w=[x1b[ti].b])
                for ti in range(nt):
                    tl = b0 + ti * 128
                    x1 = x1b[ti]
                    I("pool", lambda e: e.dma_start(out=X1[tl:tl + 128, :], in_=x1.t[:]), r=[x1.b], w=[SX1], dma=True)
                    k = xk % 2
                    xk += 1
                    hs, s4 = hsb[k], st4[k]
                    I("act", lambda e: e.activation(out=junk.t[:], in_=x1.t[:], func=AF.Square, accum_out=s4.t[:, 0:1]), r=[x1.b], w=[junk.b, s4.b])
                    I("dve", lambda e: e.tensor_scalar(out=s4.t[:, 1:2], in0=s4.t[:, 0:1], scalar1=1.0 / D, scalar2=EPS, op0=ALU.mult, op1=ALU.add), r=[s4.b], w=[s4.b])
                    I("act", lambda e: e.activation(out=s4.t[:, 2:3], in_=s4.t[:, 1:2], func=AF.Sqrt), r=[s4.b], w=[s4.b])
                    I("dve", lambda e: e.reciprocal(out=s4.t[:, 3:4], in_=s4.t[:, 2:3]), r=[s4.b], w=[s4.b])
                    I("act", lambda e: e.activation(out=hs.t[:], in_=x1.t[:], func=AF.Identity, scale=s4.t[:, 3:4]), r=[x1.b, s4.b], w=[hs.b])
                    hTt = hT[k]
                    transpose16(hs, hTt, 0)
                    I("pool", lambda e: e.dma_start(out=H2[:, :, tl:tl + 128], in_=hTt.t[:, :, 0:128]), r=[hTt.b], w=[SH2], dma=True)
                    for tok_, idx_ in bnd.items():
                        if tl <= tok_ < tl + 128:
                            cc_ = tok_ - tl
                            edge_ = is_prompt and idx_ in (0, 7)
                            if edge_:
                                I("dve", lambda e: e.tensor_scalar(out=hbd.t[:, :, idx_:idx_ + 1], in0=hTt.t[:, :, cc_:cc_ + 1], scalar1=vmask.t[:, (0 if idx_ == 0 else 1):(1 if idx_ == 0 else 2)], scalar2=None, op0=ALU.mult),
                                  r=[hTt.b, vmask.b], w=[hbd.b])
                            else:
                                I("dve", lambda e: e.tensor_copy(out=hbd.t[:, :, idx_:idx_ + 1], in_=hTt.t[:, :, cc_:cc_ + 1]), r=[hTt.b], w=[hbd.b])
            S.barrier()
        esq.close()

        stage(6)
        with contextlib.ExitStack() as es2:
            def sb2(name, shape, dt=F32):
                return T(es2.enter_context(nc.sbuf_tensor("t_" + tag + name, list(shape), dt)))
            h2 = [sb2("h2_%d" % i, [128, 16, 514], BF16) for i in range(2)]
            actT = sb2("actT", [128, 48, 512], BF16)
            wd = [sb2("wd%d" % i, [128, 12, 512], BF16) for i in range(2)]
            A["wch"] = [sb2("wch%d" % i, [128, 16, 128], BF16) for i in range(4)]
            wch = A["wch"]
            U = [sb2("U%d" % i, [128, 514]) for i in range(2)]
            cv = [sb2("cv%d" % i, [128, 512]) for i in range(2)]
            gl = sb2("gl", [128, 512])
            x1p = [sb2("x1p%d" % i, [128, 512]) for i in range(2)]
            x2p = [sb2("x2p%d" % i, [128, 512]) for i in range(2)]
            UB = sb2("UB", [128, 96, 8])
            ssq = sb2("ssq", [128, 4, 4])
            s3 = sb2("s3", [128, 4])
            x2r = [sb2("x2r%d" % i, [128, D]) for i in range(1)]
            yt = [sb2("yt%d" % i, [128, D]) for i in range(1)]
            uk = 0
            dk = 0
            for gi in range(4):
                t0 = M0 + gi * 512
                h2t = h2[gi % 2]
                lo = t0 - 1
                hi = t0 + 513
                left_ok = lo >= 0
                right_ok = hi <= TQ
                c_lo = 0 if left_ok else 1
                c_hi = 514 if right_ok else 513
                I("sp", lambda e: e.dma_start(out=h2t.t[:, :, c_lo:c_hi], in_=H2[:, :, lo + c_lo:lo + c_hi]), r=[SH2], w=[h2t.b], dma=True)
                for jj in range(48):
                    for half in range(2):
                        j = jj + 48 * half
                        ps = PS[uk % 2]
                        pbd = PS[2]
                        U_ = U[uk % 2]
                        c_ = cv[half]
                        uk += 1
                        wt = wch[wk[0] % 4]
                        wk[0] += 1
                        I("sp", lambda e: e.dma_start(out=wt.t[:], in_=WUP[j]), r=[WBu], w=[wt.b], dma=True)
                        for kc in range(16):
                            I("pe", lambda e: e.matmul(ps.t[:, :], lhsT=wt.t[:, kc, :], rhs=h2t.t[:, kc, 1:513], start=(kc == 0), stop=(kc == 15)),
                              r=[wt.b, h2t.b], w=[ps.b], inc=(kc == 15))
                        I("act", lambda e: e.activation(out=U_.t[:, 1:513], in_=ps.t[:, :], func=AF.Identity), r=[ps.b], w=[U_.b])
                        if gi == 0:
                            for kc in range(16):
                                I("pe", lambda e: e.matmul(pbd.t[:, 0:8], lhsT=wt.t[:, kc, :], rhs=hbd.t[:, kc, :], start=(kc == 0), stop=(kc == 15)),
                                  r=[wt.b, hbd.b], w=[pbd.b], inc=(kc == 15))
                            I("dve", lambda e: e.tensor_copy(out=UB.t[:, j, :], in_=pbd.t[:, 0:8]), r=[pbd.b], w=[UB.b])
                        I("pool", lambda e: e.tensor_copy(out=U_.t[:, 0:1], in_=UB.t[:, j, 2 * gi:2 * gi + 1]), r=[UB.b], w=[U_.b])
                        I("pool", lambda e: e.tensor_copy(out=U_.t[:, 513:514], in_=UB.t[:, j, 2 * gi + 1:2 * gi + 2]), r=[UB.b], w=[U_.b])
                        I("dve", lambda e: e.tensor_scalar(out=c_.t[:], in0=U_.t[:, 0:512], scalar1=cfw.t[:, j, 0:1], scalar2=cfb.t[:, j:j + 1], op0=ALU.mult, op1=ALU.add),
                          r=[U_.b, cfw.b, cfb.b], w=[c_.b])
                        I("dve", lambda e: e.scalar_tensor_tensor(out=c_.t[:], in0=U_.t[:, 1:513], scalar=cfw.t[:, j, 1:2], in1=c_.t[:], op0=ALU.mult, op1=ALU.add),
                          r=[U_.b, cfw.b, c_.b], w=[c_.b])
                        I("dve", lambda e: e.scalar_tensor_tensor(out=c_.t[:], in0=U_.t[:, 2:514], scalar=cfw.t[:, j, 2:3], in1=c_.t[:], op0=ALU.mult, op1=ALU.add),
                          r=[U_.b, cfw.b, c_.b], w=[c_.b])
                    I("act", lambda e: e.activation(out=gl.t[:], in_=cv[0].t[:], func=AF.Gelu), r=[cv[0].b], w=[gl.b])
                    I("pool", lambda e: e.tensor_tensor(out=actT.t[:, jj, :], in0=gl.t[:], in1=cv[1].t[:], op=ALU.mult), r=[gl.b, cv[1].b], w=[actT.b])
                for cg in range(4):
                    for jb in range(4):
                        w_ = wd[dk % 2]
                        dk += 1
                        I("sp", lambda e: e.dma_start(out=w_.t[:], in_=WDN[jb * 12:(jb + 1) * 12, :, cg * 512:(cg + 1) * 512].rearrange("k p c -> p k c")), r=[WBd], w=[w_.b], dma=True)
                        for ti in range(4):
                            ps = PS[2 + ti]
                            for j2 in range(12):
                                jx = jb * 12 + j2
                                I("pe", lambda e: e.matmul(ps.t[:, :], lhsT=actT.t[:, jx, ti * 128:(ti + 1) * 128], rhs=w_.t[:, j2, :], start=(jx == 0), stop=(jx == 47)),
                                  r=[actT.b, w_.b], w=[ps.b], inc=(j2 == 11))
                    for ti in range(4):
                        ps = PS[2 + ti]
                        tl = t0 + ti * 128
                        to = gi * 512 + ti * 128
                        a_, b_ = x1p[ti % 2], x2p[ti % 2]
                        I("sp", lambda e: e.dma_start(out=a_.t[:], in_=X1[tl:tl + 128, cg * 512:(cg + 1) * 512]), r=[SX1], w=[a_.b], dma=True)
                        I("dve", lambda e: e.tensor_tensor(out=b_.t[:], in0=ps.t[:, :], in1=a_.t[:], op=ALU.add), r=[ps.b, a_.b], w=[b_.b])
                        I("act", lambda e: e.activation(out=a_.t[:], in_=b_.t[:], func=AF.Square, accum_out=ssq.t[:, ti, cg:cg + 1]), r=[b_.b], w=[a_.b, ssq.b])
                        I("pool", lambda e: e.dma_start(out=X2[to:to + 128, cg * 512:(cg + 1) * 512], in_=b_.t[:]), r=[b_.b], w=[SX2], dma=True)
                for ti in range(4):
                    to = gi * 512 + ti * 128
                    xr2, y_ = x2r[0], yt[0]
                    I("dve", lambda e: e.reduce_sum(out=s3.t[:, 0:1], in_=ssq.t[:, ti, :], axis=AX.X), r=[ssq.b], w=[s3.b])
                    I("dve", lambda e: e.tensor_scalar(out=s3.t[:, 1:2], in0=s3.t[:, 0:1], scalar1=1.0 / D, scalar2=EPS, op0=ALU.mult, op1=ALU.add), r=[s3.b], w=[s3.b])
                    I("act", lambda e: e.activation(out=s3.t[:, 2:3], in_=s3.t[:, 1:2], func=AF.Sqrt), r=[s3.b], w=[s3.b])
                    I("dve", lambda e: e.reciprocal(out=s3.t[:, 3:4], in_=s3.t[:, 2:3]), r=[s3.b], w=[s3.b])
                    I("sp", lambda e: e.dma_start(out=xr2.t[:], in_=X2[to:to + 128, :]), r=[SX2], w=[xr2.b], dma=True)
                    I("dve", lambda e: e.scalar_tensor_tensor(out=y_.t[:], in0=xr2.t[:], scalar=s3.t[:, 3:4], in1=gfin.t[:], op0=ALU.mult, op1=ALU.mult),
                      r=[xr2.b, s3.b, gfin.b], w=[y_.b])
                    I("pool", lambda e: e.dma_start(out=y_d[to:to + 128, :], in_=y_.t[:]), r=[y_.b], w=[YOUT], dma=True)
            S.barrier()

    def lru_conv(xrp, u, ub, g, n, cast="pool"):
        I("dve", lambda e: e.tensor_scalar(out=u.t[:, :n], in0=xrp.t[:, 0:n], scalar1=crw.t[:, g, 0:1], scalar2=crb.t[:, g:g + 1], op0=ALU.mult, op1=ALU.add),
          r=[xrp.b, crw.b, crb.b], w=[u.b])
        for j in range(1, 4):
            I("dve", lambda e: e.scalar_tensor_tensor(out=u.t[:, :n], in0=xrp.t[:, j:j + n], scalar=crw.t[:, g, j:j + 1], in1=u.t[:, :n], op0=ALU.mult, op1=ALU.add),
              r=[xrp.b, crw.b, u.b], w=[u.b])
        if cast == "dve":
            I("dve", lambda e: e.tensor_copy(out=ub.t[:, :n], in_=u.t[:, :n]), r=[u.b], w=[ub.b])
        elif cast == "act":
            I("act", lambda e: e.activation(out=ub.t[:, :n], in_=u.t[:, :n], func=AF.Identity), r=[u.b], w=[ub.b])
        elif cast == "pool":
            I("pool", lambda e: e.tensor_copy(out=ub.t[:, :n], in_=u.t[:, :n]), r=[u.b], w=[ub.b])

    def lru_gates_a(u, ub, rr, ii, aa, z, g, n):
        k = 0
        for gi, (dst, bias) in enumerate(((rr, hba), (ii, hbx))):
            for c0 in range(0, n, 512):
                W = min(512, n - c0)
                ps = PS[k % 2]
                k += 1
                I("pe", lambda e: e.matmul(ps.t[:, :W], lhsT=wg.t[:, gi * 16 + z * 8 + g, :], rhs=ub.t[:, c0:c0 + W], start=True, stop=True), r=[wg.b, ub.b], w=[ps.b])
                I("act", lambda e: e.activation(out=dst.t[:, c0:c0 + W], in_=ps.t[:, :W], func=AF.Tanh, scale=0.5, bias=bias.t[:, z, g:g + 1]), r=[ps.b, bias.b], w=[dst.b])

    def lru_gates_b1(u, ub, rr, ii, aa, z, g, n):
        I("act", lambda e: e.activation(out=aa.t[:, :n], in_=rr.t[:, :n], func=AF.Exp, scale=hcl.t[:, z, g:g + 1], bias=hcl.t[:, z, g:g + 1]), r=[rr.b, hcl.b], w=[aa.b])
        I("act", lambda e: e.activation(out=rr.t[:, :n], in_=rr.t[:, :n], func=AF.Exp, scale=cl.t[:, z, g:g + 1], bias=cl.t[:, z, g:g + 1]), r=[rr.b, cl.b], w=[rr.b])

    def lru_gates_b2(u, ub, rr, ii, aa, z, g, n):
        I("act", lambda e: e.activation(out=rr.t[:, :n], in_=rr.t[:, :n], func=AF.Sqrt, scale=-0.25, bias=0.25), r=[rr.b], w=[rr.b])

    def lru_gates_b3(u, ub, rr, ii, aa, z, g, n, split=False):
        I("dve", lambda e: e.scalar_tensor_tensor(out=ii.t[:, :n], in0=ii.t[:, :n], scalar=1.0, in1=u.t[:, :n], op0=ALU.add, op1=ALU.mult), r=[ii.b, u.b], w=[ii.b])
        I("pool", lambda e: e.tensor_tensor(out=ii.t[:, :n], in0=ii.t[:, :n], in1=rr.t[:, :n], op=ALU.mult), r=[ii.b, rr.b], w=[ii.b])

    def lru_gates_b(u, ub, rr, ii, aa, z, g, n, split=False):
        lru_gates_b1(u, ub, rr, ii, aa, z, g, n)
        lru_gates_b2(u, ub, rr, ii, aa, z, g, n)
        lru_gates_b3(u, ub, rr, ii, aa, z, g, n)

    def lru_gates(u, ub, rr, ii, aa, z, g, n):
        lru_gates_a(u, ub, rr, ii, aa, z, g, n)
        lru_gates_b(u, ub, rr, ii, aa, z, g, n)

    YOUT = Buf()

    def prepass():
        SXF = Buf()
        esA = contextlib.ExitStack()
        alloc_front(esA)
        zf = A["zf"]
        DVE_EVAC[0] = True
        wres = T(esA.enter_context(nc.sbuf_tensor("t_wres", [128, 8, 16, 128], BF16)))
        I("sp", lambda e: e.dma_start(out=wres.t[:], in_=WIN[24:32].rearrange("j p k c -> p j k c")), r=[WBi], w=[wres.b], dma=True)
        def front_fn(st):
            s0 = st * 512
            hTt = A["hT"][st % 2]
            return [(xf_d[s0 + ti * 128:s0 + (ti + 1) * 128, :], hTt, ti * 128) for ti in range(4)]

        def proj_fn(st):
            s0 = st * 512
            hTt = A["hT"][st % 2]

            def body(g):
                ps = PS[zk[0] % 2]
                z = zf[zk[0] % 3]
                zk[0] += 1
                for kc in range(16):
                    I("pe", lambda e: e.matmul(ps.t[:, :], lhsT=wres.t[:, g, kc, :], rhs=hTt.t[:, kc, :], start=(kc == 0), stop=(kc == 15)),
                      r=[wres.b, hTt.b], w=[ps.b], inc=(kc == 15))
                I("dve", lambda e: e.tensor_copy(out=z.t[:, :], in_=ps.t[:, :]), r=[ps.b], w=[z.b])
                I("pool", lambda e: e.dma_start(out=XRF[g, :, s0:s0 + 512], in_=z.t[:, :]), r=[z.b], w=[SXF], dma=True)
            return [(lambda g=g: body(g)) for g in range(8)]

        pipelined_steps(SEQ_P // 512, front_fn, proj_fn)
        DVE_EVAC[0] = False
        S.barrier()
        esA.close()
        with contextlib.ExitStack() as es2:
            def sb2(name, shape, dt=F32):
                return T(es2.enter_context(nc.sbuf_tensor("t_pp" + name, list(shape), dt)))
            n = 1024
            NSG = SEQ_P // n
            NB = 4
            UF = dscr("UF", [8, 128, SEQ_P], F32)
            SUF = Buf()
            xrp = [sb2("xrp%d" % i, [128, n + 3]) for i in range(NB)]
            u = [sb2("u%d" % i, [128, n]) for i in range(NB)]
            ub = [sb2("ub%d" % i, [128, n], BF16) for i in range(NB)]
            rr = [sb2("rr%d" % i, [128, n]) for i in range(NB)]
            ii = [sb2("ii%d" % i, [128, n]) for i in range(NB)]
            aa = [sb2("aa%d" % i, [128, n]) for i in range(NB)]
            hh = [sb2("hh%d" % i, [128, n]) for i in range(NB)]
            rec = sb2("rec", [128, 2, 8, 8]); tm = sb2("tm", [128, 8])
            I("dve", lambda e: e.memset(rec.t[:], 0.0), w=[rec.b])
            work = []
            for g in range(8):
                for z in range(2):
                    segs = list(range((2048 * 7 - 127) // n + 1)) if z == 0 else list(range(NSG - 1, (2048 + 126) // n - 1, -1))
                    for si, s in enumerate(segs):
                        work.append((g, z, si, s, si == len(segs) - 1))

            def stage_a(it, part):
                g, z, si, s, last_ = work[it]
                k = it % NB
                x_, u_, ub_, rr_, ii_, aa_ = xrp[k], u[k], ub[k], rr[k], ii[k], aa[k]
                if part == 2:
                    lru_gates_a(u_, ub_, rr_, ii_, aa_, z, g, n)
                    return
                if z == 0 or s > (2048 * 7 - 127) // n:
                    a0 = s * n - 2
                    c_lo = 2 if s == 0 else 0
                    c_hi = n + 2 if s == NSG - 1 else n + 3
                    if s == 0:
                        I("dve", lambda e: e.memset(x_.t[:, 0:2], 0.0), w=[x_.b])
                    if s == NSG - 1:
                        I("dve", lambda e: e.memset(x_.t[:, n + 2:n + 3], 0.0), w=[x_.b])
                    I("sp", lambda e: e.dma_start(out=x_.t[:, c_lo:c_hi], in_=XRF[g, :, a0 + c_lo:a0 + c_hi]), r=[SXF], w=[x_.b], dma=True)
                    lru_conv(x_, u_, ub_, g, n, cast="dve")
                    if z == 0:
                        I("pool", lambda e: e.dma_start(out=UF[g, :, s * n:(s + 1) * n], in_=u_.t[:]), r=[u_.b], w=[SUF], dma=True)
                else:
                    I("sp", lambda e: e.dma_start(out=u_.t[:], in_=UF[g, :, s * n:(s + 1) * n]), r=[SUF], w=[u_.b], dma=True)
                    I("dve", lambda e: e.tensor_copy(out=ub_.t[:], in_=u_.t[:]), r=[u_.b], w=[ub_.b])

            def stage_b(it, ph):
                g, z, si, s, last_ = work[it]
                k = it % NB
                u_, ub_, rr_, ii_, aa_, hh_, hp = u[k], ub[k], rr[k], ii[k], aa[k], hh[k], hh[(it - 1) % NB]
                if ph == 1:
                    lru_gates_b1(u_, ub_, rr_, ii_, aa_, z, g, n)
                    return
                if ph == 2:
                    lru_gates_b2(u_, ub_, rr_, ii_, aa_, z, g, n)
                    return
                lru_gates_b3(u_, ub_, rr_, ii_, aa_, z, g, n)
                if z == 0:
                    init = 0.0 if si == 0 else hp.t[:, n - 1:n]
                    I("dve", lambda e: e.tensor_tensor_scan(out=hh_.t[:], data0=aa_.t[:], data1=ii_.t[:], initial=init, op0=ALU.mult, op1=ALU.add),
                      r=[aa_.b, ii_.b, hp.b], w=[hh_.b])
                    for j_ in range(1, 8):
                        tk_ = 2048 * j_ - 127
                        if tk_ // n == s:
                            I("dve", lambda e: e.tensor_copy(out=rec.t[:, 0, g, j_:j_ + 1], in_=hh_.t[:, tk_ % n:tk_ % n + 1]), r=[hh_.b], w=[rec.b])
                else:
                    init = 0.0 if si == 0 else hp.t[:, 0:1]
                    I("dve", lambda e: e.tensor_tensor_scan(out=hh_.t[:, ::-1], data0=aa_.t[:, ::-1], data1=ii_.t[:, ::-1], initial=init, op0=ALU.mult, op1=ALU.add),
                      r=[aa_.b, ii_.b, hp.b], w=[hh_.b])
                    for j_ in range(0, 7):
                        tk_ = 2048 * (j_ + 1) + 126
                        if tk_ // n == s:
                            I("dve", lambda e: e.tensor_copy(out=rec.t[:, 1, g, j_:j_ + 1], in_=hh_.t[:, tk_ % n:tk_ % n + 1]), r=[hh_.b], w=[rec.b])
                if last_:
                    Hx = HF if z == 0 else HB
                    I("dve", lambda e: e.tensor_tensor(out=tm.t[:], in0=rec.t[:, z, g, :], in1=sel.t[:, z, :], op=ALU.mult), r=[rec.b, sel.b], w=[tm.b])
                    I("dve", lambda e: e.reduce_sum(out=Hx.t[:, g:g + 1], in_=tm.t[:], axis=AX.X), r=[tm.b], w=[Hx.b])

            LA = 2
            for it in range(min(LA, len(work))):
                stage_a(it, 1)
                stage_a(it, 2)
            for p in range(0, len(work), 2):
                pair = [it for it in (p, p + 1) if it < len(work)]
                nxt = [it + LA for it in pair if it + LA < len(work)]
                for it in nxt:
                    stage_a(it, 1)
                for ph in (1, 2, 3):
                    for it in pair:
                        stage_b(it, ph)
                for it in nxt:
                    stage_a(it, 2)
            S.barrier()

    try:
        run_seq("s", xs_d, CH, 0, CH, 0, cos_s_d, sin_s_d, ys_d, False, None, None)
        if prompt_on:
            if prepass_on:
                prepass()
            else:
                I("dve", lambda e: e.memset(HF.t[:], 0.0), w=[HF.b])
                I("dve", lambda e: e.memset(HB.t[:], 0.0), w=[HB.b])
            run_seq("p", xp_d, 4352, 1024, 2304, 128, cos_p_d, sin_p_d, yp_d, True, HF, HB)
    except _Stop:
        pass
    S.finish()
    return nc, es


def _rope_tables(pos):
    half = 16
    inv = (500000.0 ** (-np.arange(half, dtype=np.float32) / half)).astype(np.float32)
    ang = pos.astype(np.float32)[None, :] * inv[:, None]
    c = np.cos(ang).astype(np.float32)
    s = np.sin(ang).astype(np.float32)
    return np.concatenate([c, c], 0), np.concatenate([s, s], 0)


def _chunk16(v):
    return np.ascontiguousarray(v.reshape(-1, 128).T)


PROMPT_ON = True
PREPASS_ON = True


def kernel(x_prompt, x_sample, g_mix, w_in, w_out, g_attn_out, g_lru_out, conv_rg_w, conv_rg_b,
           rg_w_a, rg_b_a, rg_w_x, rg_b_x, rg_lam, g_mlp, w_up, conv_ff_w, conv_ff_b, w_down, g_final):
    f = np.float32
    xpf = np.ascontiguousarray(x_prompt[0], dtype=f)
    common = {
        "xf": xpf,
        "w_in": np.ascontiguousarray(w_in[0], f), "w_out": np.ascontiguousarray(w_out[0], f),
        "w_up": np.ascontiguousarray(w_up[0], f), "w_down": np.ascontiguousarray(w_down[0], f),
        "g_mix": _chunk16(g_mix[0]), "g_mlp": _chunk16(g_mlp[0]),
        "g_mo": _chunk16(np.concatenate([g_attn_out[0], g_lru_out[0]])),
        "crw": np.ascontiguousarray(conv_rg_w[0].reshape(4, 8, 128).transpose(2, 1, 0)),
        "crb": _chunk16(conv_rg_b[0]),
        "rg_w_a": np.ascontiguousarray(rg_w_a[0], f), "rg_w_x": np.ascontiguousarray(rg_w_x[0], f),
        "rg_b_a": np.ascontiguousarray(rg_b_a[0].reshape(2, 8, 128).transpose(2, 0, 1)),
        "rg_b_x": np.ascontiguousarray(rg_b_x[0].reshape(2, 8, 128).transpose(2, 0, 1)),
        "rg_lam": np.ascontiguousarray(rg_lam[0].reshape(2, 8, 128).transpose(2, 0, 1)),
        "cfw": np.ascontiguousarray(conv_ff_w[0].reshape(3, 96, 128).transpose(2, 1, 0)),
        "cfb": _chunk16(conv_ff_b[0]),
        "gfin": np.ascontiguousarray(np.broadcast_to(g_final[None, :], (128, D)), f),
    }
    cs, sn = _rope_tables(np.arange(CH))
    common["cos_s"], common["sin_s"] = cs, sn
    pmm = np.zeros((128, 128), f)
    for m in range(16):
        pmm[m + 16, m] = -1.0
        pmm[m, m + 16] = 1.0
    common["pm"] = pmm
    jj = np.arange(128)[:, None]
    cc = np.arange(512)[None, :]
    common["bmask"] = ((cc - jj >= 0) & (cc - jj <= 128)).astype(f)
    common["ident"] = np.eye(128, dtype=f)
    if not (PROMPT_ON and PREPASS_ON):
        common["xf"] = xpf[:128]
    in_maps = []
    for c in range(NCORE):
        m = dict(common)
        m["xs"] = np.ascontiguousarray(x_sample[c], f)
        a0 = c * CH - 1152
        pos = np.arange(a0, a0 + 4352)
        valid = (pos >= 0) & (pos < SEQ_P)
        xp = np.zeros((4352, D), f)
        xp[valid] = xpf[pos[valid]]
        m["xp"] = xp
        cp, sp_ = _rope_tables(np.where(valid, pos, 0))
        m["cos_p"], m["sin_p"] = cp, sp_
        kb = np.zeros((128, 3, 16, 34), f)
        for b, dil in enumerate(DILS):
            L = 4352 // dil
            for r in range(dil):
                for kbi in range((L + 127) // 128):
                    l = kbi * 128 + np.arange(128)
                    t = r + dil * l
                    ok = (l < L) & valid[np.minimum(t, 4351)]
                    kb[:, b, r, kbi] = np.where(ok, 0.0, -30000.0)
        m["kbias"] = kb
        sel = np.zeros((128, 2, 8), f)
        if c > 0:
            sel[:, 0, c] = 1.0
        if c < 7:
            sel[:, 1, c] = 1.0
        m["sel"] = sel
        vm = np.ones((128, 2), f)
        if c == 0:
            vm[:, 0] = 0.0
        if c == 7:
            vm[:, 1] = 0.0
        m["vmask"] = vm
        tm = np.ones((128, 256), f)
        tm[:, 0:128] = ((c * CH - 128 + np.arange(128)) >= 0).astype(f)[None, :]
        tm[:, 128:256] = ((c * CH + CH + np.arange(128)) < SEQ_P).astype(f)[None, :]
        m["tmask"] = tm
        in_maps.append(m)
    nc, es = build(PROMPT_ON, PREPASS_ON)
    res = run_bass_kernel_spmd(nc, in_maps, core_ids=list(range(NCORE)))
    try:
        es.close()
    except AssertionError:
        pass
    if DEBUG["scratch_out"]:
        DEBUG["res"] = res
        DEBUG["in_maps"] = in_maps
    yp = np.concatenate([np.asarray(res.results[c]["yp"], f) for c in range(NCORE)], 0)[None]
    ys = np.stack([np.asarray(res.results[c]["ys"], f) for c in range(NCORE)], 0)
    return (yp, ys)
```

```python
import contextlib
import math
import numpy as np
import concourse.bass as bass
import concourse.mybir as mybir
from concourse.bass_utils import run_bass_kernel_spmd

F32 = mybir.dt.float32
BF16 = mybir.dt.bfloat16
AF = mybir.ActivationFunctionType
ALU = mybir.AluOpType
AX = mybir.AxisListType

D = 2048
NCORE = 8
SEQ_P = 16384
CH = 2048
EPS = 1e-6
NR = 16
NRQ = {"sp": 16, "pool": 4}


class Buf:
    __slots__ = ("w", "r", "excl")

    def __init__(self, excl=False):
        self.w = None
        self.r = {}
        self.excl = excl


class Sched:
    def __init__(self, nc, es):
        self.nc = nc
        self.E = {"pe": nc.tensor, "act": nc.scalar, "dve": nc.vector, "pool": nc.gpsimd, "sp": nc.sync}
        self.cs = {k: es.enter_context(nc.semaphore("c_" + k)) for k in ("pe", "act", "dve", "pool")}
        self.cc = {k: 0 for k in self.cs}
        self.ring = {q: [es.enter_context(nc.semaphore("d_%s%d" % (q, i))) for i in range(NRQ[q])] for q in ("sp", "pool")}
        self.rt = {q: [0] * NRQ[q] for q in self.ring}
        self.rk = {q: 0 for q in self.ring}
        self.waited = {k: {} for k in self.E}
        self.pr = []
        self.pw = []

    def _wait(self, e, sem, val):
        k = id(sem)
        if self.waited[e].get(k, 0) < val:
            self.E[e].wait_ge(sem, val)
            self.waited[e][k] = val

    def I(self, e, fn, r=(), w=(), dma=False, inc=True):
        deps = []
        for b in r:
            if b.w is not None:
                deps.append(b.w)
            if b.excl:
                deps.extend(b.r.values())
        for b in w:
            if b.w is not None:
                deps.append(b.w)
            deps.extend(b.r.values())
        for sem, val in deps:
            if e == "pe" and sem is self.cs["pe"]:
                continue
            self._wait(e, sem, val)
        eng = self.E[e]
        if dma:
            i = self.rk[e] % NRQ[e]
            self.rk[e] += 1
            sem = self.ring[e][i]
            if self.rt[e][i] > 0:
                self._wait(e, sem, self.rt[e][i])
            inst = fn(eng)
            self.rt[e][i] += 16
            inst.then_inc(sem, 16)
            t = (sem, self.rt[e][i])
        else:
            inst = fn(eng)
            if not inc:
                self.pr += list(r)
                self.pw += list(w)
                return
            self.cc[e] += 1
            inst.then_inc(self.cs[e], 1)
            t = (self.cs[e], self.cc[e])
            if e == "pe":
                r = list(r) + self.pr
                w = list(w) + self.pw
                self.pr = []
                self.pw = []
        for b in r:
            b.r[id(t[0])] = t
        for b in w:
            b.w = t
            b.r = {}

    def barrier(self):
        for e in self.E:
            for k in self.cs:
                if k != e and self.cc[k] > 0:
                    self._wait(e, self.cs[k], self.cc[k])
            for q in self.ring:
                for i in range(NRQ[q]):
                    if self.rt[q][i] > 0:
                        self._wait(e, self.ring[q][i], self.rt[q][i])

    def finish(self):
        for q in self.ring:
            for i in range(NRQ[q]):
                if self.rt[q][i] > 0:
                    self._wait("sp", self.ring[q][i], self.rt[q][i])
        for k in self.cs:
            if self.cc[k] > 0:
                self._wait("sp", self.cs[k], self.cc[k])


class T:
    def __init__(self, ap, excl=False):
        self.t = ap
        self.b = Buf(excl)


DILS = (1, 4, 16)
DEBUG = {"stop": 99, "scratch_out": False, "outs": (), "maxsteps": 999, "sub": 9}


class _Stop(Exception):
    pass


def build(prompt_on=True, prepass_on=True):
    nc = bass.Bass("TRN2", target_bir_lowering=False)
    es = contextlib.ExitStack()
    S = Sched(nc, es)
    I = S.I

    def din(name, shape, dt=F32):
        return nc.dram_tensor(name, list(shape), dt, kind="ExternalInput").ap()

    def dscr(name, shape, dt):
        return nc.dram_tensor(name, list(shape), dt, kind=("ExternalOutput" if name in DEBUG["outs"] else "Internal")).ap()

    def stage(n):
        if DEBUG["stop"] <= n:
            raise _Stop()

    xs_d = din("xs", [CH, D])
    xp_d = din("xp", [4352, D])
    xf_d = din("xf", [SEQ_P if prepass_on and prompt_on else 128, D])
    w_in_d = din("w_in", [D, 5120])
    w_out_d = din("w_out", [D, D])
    w_up_d = din("w_up", [D, 12288])
    w_dn_d = din("w_down", [6144, D])
    g_mix_d = din("g_mix", [128, 16])
    g_mlp_d = din("g_mlp", [128, 16])
    g_mo_d = din("g_mo", [128, 16])
    crw_d = din("crw", [128, 8, 4])
    crb_d = din("crb", [128, 8])
    wa_d = din("rg_w_a", [2, 8, 128, 128])
    wx_d = din("rg_w_x", [2, 8, 128, 128])
    ba_d = din("rg_b_a", [128, 2, 8])
    bx_d = din("rg_b_x", [128, 2, 8])
    lam_d = din("rg_lam", [128, 2, 8])
    cfw_d = din("cfw", [128, 96, 3])
    cfb_d = din("cfb", [128, 96])
    gfin_d = din("gfin", [128, D])
    cos_s_d = din("cos_s", [32, CH])
    sin_s_d = din("sin_s", [32, CH])
    cos_p_d = din("cos_p", [32, 4352])
    sin_p_d = din("sin_p", [32, 4352])
    pm_d = din("pm", [128, 128])
    mask_d = din("bmask", [128, 512])
    ident_d = din("ident", [128, 128])
    kb_d = din("kbias", [128, 3, 16, 34])
    sel_d = din("sel", [128, 2, 8])
    vm_d = din("vmask", [128, 2])
    tm_d = din("tmask", [128, 256])
    ys_d = nc.dram_tensor("ys", [CH, D], F32, kind="ExternalOutput").ap()
    yp_d = nc.dram_tensor("yp", [CH, D], F32, kind="ExternalOutput").ap()

    WIN = dscr("WIN", [40, 128, 16, 128], BF16)
    WUP = dscr("WUP", [96, 128, 16, 128], BF16)
    WOUT = dscr("WOUT", [16, 128, D], BF16)
    WDN = dscr("WDN", [48, 128, D], BF16)
    XRF = dscr("XRF", [8, 128, SEQ_P], F32)

    def sb(name, shape, dt=F32):
        return T(es.enter_context(nc.sbuf_tensor("sb_" + name, list(shape), dt)))

    PS = [T(es.enter_context(nc.psum_tensor("ps%d" % i, [128, 512], F32)), True) for i in range(6)]
    PB = [T(es.enter_context(nc.psum_tensor("pb%d" % i, [128, 1024], BF16)), True) for i in range(2)]

    def load(dst, src, q="sp"):
        I(q, lambda e: e.dma_start(out=dst.t[:], in_=src), w=[dst.b], dma=True)

    gmix = sb("gmix", [128, 16]); load(gmix, g_mix_d[:, :])
    gmlp = sb("gmlp", [128, 16]); load(gmlp, g_mlp_d[:, :])
    gmo = sb("gmo", [128, 16]); load(gmo, g_mo_d[:, :])
    crw = sb("crw", [128, 8, 4]); load(crw, crw_d[:, :, :])
    crb = sb("crb", [128, 8]); load(crb, crb_d[:, :])
    ba = sb("ba", [128, 2, 8]); load(ba, ba_d[:, :, :])
    bx = sb("bx", [128, 2, 8]); load(bx, bx_d[:, :, :])
    lam = sb("lam", [128, 2, 8]); load(lam, lam_d[:, :, :])
    cfw = sb("cfw", [128, 96, 3]); load(cfw, cfw_d[:, :, :])
    cfb = sb("cfb", [128, 96]); load(cfb, cfb_d[:, :])
    gfin = sb("gfin", [128, D]); load(gfin, gfin_d[:, :])
    kbias = sb("kbias", [128, 3, 16, 34]); load(kbias, kb_d[:, :, :, :])
    sel = sb("sel", [128, 2, 8]); load(sel, sel_d[:, :, :])
    vmask = sb("vmask", [128, 2]); load(vmask, vm_d[:, :])
    tmask = sb("tmask", [128, 256]); load(tmask, tm_d[:, :])
    tmpc = sb("tmpc", [128, 512])
    bmask = sb("bmaskb", [128, 512], BF16)
    load(tmpc, mask_d[:, :])
    I("dve", lambda e: e.tensor_copy(out=bmask.t[:], in_=tmpc.t[:]), r=[tmpc.b], w=[bmask.b])
    ident = sb("identb", [128, 128], BF16)
    tmpi = sb("tmpi", [128, 128])
    load(tmpi, ident_d[:, :])
    I("dve", lambda e: e.tensor_copy(out=ident.t[:], in_=tmpi.t[:]), r=[tmpi.b], w=[ident.b])
    pm = sb("pmb", [128, 128], BF16)
    tmpp = sb("tmpp", [128, 128])
    load(tmpp, pm_d[:, :])
    I("dve", lambda e: e.tensor_copy(out=pm.t[:], in_=tmpp.t[:]), r=[tmpp.b], w=[pm.b])
    ones = sb("onesb", [128, 128], BF16)
    I("dve", lambda e: e.memset(ones.t[:], 1.0), w=[ones.b])
    cl = sb("cl", [128, 2, 8]); cl2 = sb("cl2", [128, 2, 8])
    I("act", lambda e: e.activation(out=cl.t[:], in_=lam.t[:], func=AF.Exp, scale=-1.0), r=[lam.b], w=[cl.b])
    I("act", lambda e: e.activation(out=cl.t[:], in_=cl.t[:], func=AF.Ln, bias=1.0), r=[cl.b], w=[cl.b])
    I("dve", lambda e: e.tensor_scalar(out=cl2.t[:], in0=cl.t[:], scalar1=-16.0, scalar2=None, op0=ALU.mult), r=[cl.b], w=[cl2.b])
    I("dve", lambda e: e.tensor_scalar(out=cl.t[:], in0=cl.t[:], scalar1=-8.0, scalar2=None, op0=ALU.mult), r=[cl.b], w=[cl.b])
    hba = sb("hba", [128, 2, 8]); hbx = sb("hbx", [128, 2, 8]); hcl = sb("hcl", [128, 2, 8])
    I("dve", lambda e: e.tensor_scalar(out=hba.t[:], in0=ba.t[:], scalar1=0.5, scalar2=None, op0=ALU.mult), r=[ba.b], w=[hba.b])
    I("dve", lambda e: e.tensor_scalar(out=hbx.t[:], in0=bx.t[:], scalar1=0.5, scalar2=None, op0=ALU.mult), r=[bx.b], w=[hbx.b])
    I("dve", lambda e: e.tensor_scalar(out=hcl.t[:], in0=cl.t[:], scalar1=0.5, scalar2=None, op0=ALU.mult), r=[cl.b], w=[hcl.b])
    wg = sb("wg", [128, 32, 128], BF16)
    esg = contextlib.ExitStack()
    wgt = T(esg.enter_context(nc.sbuf_tensor("wgt", [128, 16, 128], F32)))
    for gi, src in enumerate((wa_d, wx_d)):
        I("sp", lambda e, src=src: e.dma_start(out=wgt.t[:], in_=src.rearrange("z g i j -> i (z g) j")), w=[wgt.b], dma=True)
        I("dve", lambda e, gi=gi: e.tensor_copy(out=wg.t[:, gi * 16:(gi + 1) * 16, :], in_=wgt.t[:]), r=[wgt.b], w=[wg.b])

    S.barrier()
    esg.close()

    st4 = [sb("st4_%d" % i, [128, 4]) for i in range(3)]
    HF = sb("HF", [128, 8])
    HB = sb("HB", [128, 8])
    hbds = {"s": sb("hbd_s", [128, 16, 8], BF16), "p": sb("hbd_p", [128, 16, 8], BF16)}
    esp = contextlib.ExitStack()
    wst = [T(esp.enter_context(nc.sbuf_tensor("wst%d" % i, [128, 2048], F32))) for i in range(4)]
    wsb = [T(esp.enter_context(nc.sbuf_tensor("wsb%d" % i, [128, 2048], BF16))) for i in range(4)]
    cnt = [0]

    def prep(Wd, K, N, g, dst, tiled, WB):
        for kc in range(K // 128):
            for cb in range(0, N, 2048):
                cw = min(2048, N - cb)
                k = cnt[0] % 4
                cnt[0] += 1
                a, b = wst[k], wsb[k]
                I("sp", lambda e: e.dma_start(out=a.t[:, :cw], in_=Wd[kc * 128:(kc + 1) * 128, cb:cb + cw]), w=[a.b], dma=True)
                if g is not None:
                    I("dve", lambda e: e.tensor_scalar(out=b.t[:, :cw], in0=a.t[:, :cw], scalar1=g.t[:, kc:kc + 1], scalar2=None, op0=ALU.mult), r=[a.b, g.b], w=[b.b])
                else:
                    I("dve", lambda e: e.tensor_copy(out=b.t[:, :cw], in_=a.t[:, :cw]), r=[a.b], w=[b.b])
                if tiled:
                    j0 = cb // 128
                    I("pool", lambda e: e.dma_start(out=dst[j0:j0 + cw // 128, :, kc, :].rearrange("j p c -> p j c"),
                                                    in_=b.t[:, :cw].rearrange("p (j c) -> p j c", c=128)), r=[b.b], w=[WB], dma=True)
                else:
                    I("pool", lambda e: e.dma_start(out=dst[kc, :, cb:cb + cw], in_=b.t[:, :cw]), r=[b.b], w=[WB], dma=True)
                yield None

    WBi, WBo, WBu, WBd = Buf(), Buf(), Buf(), Buf()
    import itertools
    for _ in prep(w_in_d, D, 5120, gmix, WIN, True, WBi):
        pass
    prep_rest = itertools.chain(prep(w_out_d, D, D, gmo, WOUT, False, WBo), prep(w_up_d, D, 12288, gmlp, WUP, True, WBu),
                                prep(w_dn_d, 6144, D, None, WDN, False, WBd))

    A = {}
    uid = [0]

    def alloc_front(esx, full=True, hw=512):
        uid[0] += 1
        p = "f%d_" % uid[0]

        def sbx(name, shape, dt=F32):
            return T(esx.enter_context(nc.sbuf_tensor(p + name, list(shape), dt)))
        A["hsb"] = [sbx("hs%d" % i, [128, D], BF16) for i in range(3 if full else 2)]
        A["junk"] = sbx("junk", [128, D], BF16)
        A["hT"] = [sbx("hT%d" % i, [128, 16, hw], BF16) for i in range(2)]
        if full:
            A["xtb"] = [sbx("xt%d" % i, [128, D]) for i in range(3)]
            A["wch"] = [sbx("wch%d" % i, [128, 16, 128], BF16) for i in range(4)]
            A["zb"] = [sbx("zb%d" % i, [128, 512], BF16) for i in range(3)]
            A["zf"] = [sbx("zf%d" % i, [128, 512]) for i in range(3)]
            A["r1"] = sbx("r1", [32, 512]); A["r2"] = sbx("r2", [32, 512])
            A["cst"] = [sbx("cst%d" % i, [32, 512]) for i in range(2)]
            A["snt"] = [sbx("snt%d" % i, [32, 512]) for i in range(2)]
    wk = [0]
    tk = [0]
    DVE_EVAC = [False]

    def front_act(xsrc_rows):
        k = tk[0] % 3
        tk[0] += 1
        xt, hs, s4 = A["xtb"][k], A["hsb"][k], st4[k]
        junk = A["junk"]
        I("sp", lambda e: e.dma_start(out=xt.t[:], in_=xsrc_rows), w=[xt.b], dma=True)
        I("act", lambda e: e.activation(out=junk.t[:], in_=xt.t[:], func=AF.Square, accum_out=s4.t[:, 0:1]), r=[xt.b], w=[junk.b, s4.b])
        I("dve", lambda e: e.tensor_scalar(out=s4.t[:, 1:2], in0=s4.t[:, 0:1], scalar1=1.0 / D, scalar2=EPS, op0=ALU.mult, op1=ALU.add), r=[s4.b], w=[s4.b])
        I("act", lambda e: e.activation(out=s4.t[:, 2:3], in_=s4.t[:, 1:2], func=AF.Sqrt), r=[s4.b], w=[s4.b])
        I("dve", lambda e: e.reciprocal(out=s4.t[:, 3:4], in_=s4.t[:, 2:3]), r=[s4.b], w=[s4.b])
        I("act", lambda e: e.activation(out=hs.t[:], in_=xt.t[:], func=AF.Identity, scale=s4.t[:, 3:4]), r=[xt.b, s4.b], w=[hs.b])
        return hs

    def transpose16(hs, hTt, col0):
        for half in range(2):
            pb = PB[half]
            for kk in range(8):
                kc = half * 8 + kk
                I("pe", lambda e: e.transpose(pb.t[:, kk * 128:(kk + 1) * 128], hs.t[:, kc * 128:(kc + 1) * 128], ident.t[:]),
                  r=[hs.b, ident.b], w=[pb.b], inc=(kk == 7))
            eng = "dve" if (half == 0 or DVE_EVAC[0]) else "act"
            src = pb.t[:].rearrange("p (k c) -> p k c", c=128)
            dst = hTt.t[:, half * 8:(half + 1) * 8, col0:col0 + 128]
            if eng == "dve":
                I("dve", lambda e: e.tensor_copy(out=dst, in_=src), r=[pb.b], w=[hTt.b])
            else:
                I("act", lambda e: e.activation(out=dst, in_=src, func=AF.Identity), r=[pb.b], w=[hTt.b])

    def proj_chunk(Wt, j, hTt, W, ps, WB):
        wt = A["wch"][wk[0] % 4]
        wk[0] += 1
        I("sp", lambda e: e.dma_start(out=wt.t[:], in_=Wt[j]), r=[WB], w=[wt.b], dma=True)
        for kc in range(16):
            I("pe", lambda e: e.matmul(ps.t[:, :W], lhsT=wt.t[:, kc, :], rhs=hTt.t[:, kc, :W], start=(kc == 0), stop=(kc == 15)),
              r=[wt.b, hTt.b], w=[ps.b], inc=(kc == 15))

    zk = [0]

    def pipelined_steps(nsteps, front_fn, proj_fn):
        tiles = []
        first_of = []
        for st in range(nsteps):
            first_of.append(len(tiles))
            tiles += front_fn(st)
        first_of.append(len(tiles))
        hs_of = {}
        na = [0]
        LEAD = 2

        def emit_pe(idx):
            while na[0] <= min(idx + LEAD, len(tiles) - 1):
                hs_of[na[0]] = front_act(tiles[na[0]][0])
                na[0] += 1
            transpose16(hs_of.pop(idx), tiles[idx][1], tiles[idx][2])

        for idx in range(first_of[0], first_of[1]):
            emit_pe(idx)
        for st in range(nsteps):
            nxt = list(range(first_of[st + 1], first_of[st + 2])) if st + 1 < nsteps else []
            pj = proj_fn(st)
            per = max(1, len(pj) // max(1, len(nxt)))
            for i_, p in enumerate(pj):
                if nxt and i_ % per == 0:
                    emit_pe(nxt.pop(0))
                p()
            for idx in nxt:
                emit_pe(idx)

    def run_seq(tag, x_d, R, Q0, TQ, M0, cos_d, sin_d, y_d, is_prompt, HF, HB):
        QT = dscr(tag + "QT", [8, 128, TQ], BF16)
        KT = dscr(tag + "KT", [8, 128, R], BF16)
        VT = dscr(tag + "VT", [8, 128, R], BF16)
        XRT = dscr(tag + "XRT", [8, 128, TQ], F32)
        GT = dscr(tag + "GT", [8, 128, TQ], F32)
        X1 = dscr(tag + "X1", [TQ, D], F32)
        X2 = dscr(tag + "X2", [CH, D], F32)
        H2 = dscr(tag + "H2", [128, 16, TQ], BF16)
        SQT, SKT, SVT, SXR, SGT, SX1, SX2, SH2 = (Buf() for _ in range(8))
        hbd = hbds[tag]
        I("dve", lambda e: e.memset(hbd.t[:], 0.0), w=[hbd.b])
        bnd = {}
        for gi_ in range(4):
            for side_, tok_ in ((0, M0 + 512 * gi_ - 1), (1, M0 + 512 * gi_ + 512)):
                if 0 <= tok_ < TQ:
                    bnd[tok_] = 2 * gi_ + side_

        stage(2)
        esA = contextlib.ExitStack()
        alloc_front(esA)
        zb, zf, r1, r2, cst, snt = A["zb"], A["zf"], A["r1"], A["r2"], A["cst"], A["snt"]
        nsteps = min((R + 511) // 512, DEBUG["maxsteps"])

        def front_fn(st):
            s0 = st * 512
            W = min(512, R - s0)
            hTt = A["hT"][st % 2]
            return [(x_d[s0 + ti * 128:s0 + (ti + 1) * 128, :], hTt, ti * 128) for ti in range(W // 128)]

        def proj_fn(st):
            s0 = st * 512
            W = min(512, R - s0)
            hTt = A["hT"][st % 2]
            inq = (s0 >= Q0) and (s0 < Q0 + TQ)
            wq = min(W, Q0 + TQ - s0) if inq else 0
            ct, sn = cst[st % 2], snt[st % 2]
            chunks = list(range(8, 24)) + (list(range(0, 8)) + list(range(24, 40)) if inq else [])

            def body(j, first):
                if first:
                    I("sp", lambda e: e.dma_start(out=ct.t[:, :W], in_=cos_d[:, s0:s0 + W]), w=[ct.b], dma=True)
                    I("sp", lambda e: e.dma_start(out=sn.t[:, :W], in_=sin_d[:, s0:s0 + W]), w=[sn.b], dma=True)
                ps = PS[zk[0] % 2]
                k3 = zk[0] % 3
                zk[0] += 1
                proj_chunk(WIN, j, hTt, W, ps, WBi)
                if tag == "s":
                    next(prep_rest, None)
                if j < 16:
                    z = zb[k3]
                    I("act", lambda e: e.activation(out=z.t[:, :W], in_=ps.t[:, :W], func=AF.Identity), r=[ps.b], w=[z.b])
                    rp = PS[2]
                    I("pe", lambda e: e.matmul(rp.t[:, :W], lhsT=pm.t[:, :], rhs=z.t[:, :W], start=True, stop=True), r=[pm.b, z.b], w=[rp.b])
                    I("dve", lambda e: e.tensor_tensor(out=r1.t[:, :W], in0=ps.t[0:32, :W], in1=ct.t[:, :W], op=ALU.mult), r=[ps.b, ct.b], w=[r1.b])
                    I("dve", lambda e: e.tensor_tensor(out=r2.t[:, :W], in0=rp.t[0:32, :W], in1=sn.t[:, :W], op=ALU.mult), r=[rp.b, sn.b], w=[r2.b])
                    I("dve", lambda e: e.tensor_tensor(out=z.t[0:32, :W], in0=r1.t[:, :W], in1=r2.t[:, :W], op=ALU.add), r=[r1.b, r2.b], w=[z.b])
                    if j < 8:
                        I("pool", lambda e: e.dma_start(out=QT[j, :, s0 - Q0:s0 - Q0 + wq], in_=z.t[:, :wq]), r=[z.b], w=[SQT], dma=True)
                    else:
                        I("pool", lambda e: e.dma_start(out=KT[j - 8, :, s0:s0 + W], in_=z.t[:, :W]), r=[z.b], w=[SKT], dma=True)
                elif j < 24:
                    z = zb[k3]
                    I("act", lambda e: e.activation(out=z.t[:, :W], in_=ps.t[:, :W], func=AF.Identity), r=[ps.b], w=[z.b])
                    I("pool", lambda e: e.dma_start(out=VT[j - 16, :, s0:s0 + W], in_=z.t[:, :W]), r=[z.b], w=[SVT], dma=True)
                else:
                    z = zf[k3]
                    I("act", lambda e: e.activation(out=z.t[:, :W], in_=ps.t[:, :W], func=AF.Identity), r=[ps.b], w=[z.b])
                    if j < 32:
                        I("pool", lambda e: e.dma_start(out=XRT[j - 24, :, s0 - Q0:s0 - Q0 + wq], in_=z.t[:, :wq]), r=[z.b], w=[SXR], dma=True)
                    else:
                        I("pool", lambda e: e.dma_start(out=GT[j - 32, :, s0 - Q0:s0 - Q0 + wq], in_=z.t[:, :wq]), r=[z.b], w=[SGT], dma=True)
            return [(lambda j=j, f=(i == 0): body(j, f)) for i, j in enumerate(chunks)]

        pipelined_steps(nsteps, front_fn, proj_fn)

        if tag == "s":
            for _ in prep_rest:
                pass
        S.barrier()
        esA.close()
        if tag == "s":
            esp.close()
        esq = contextlib.ExitStack()
        mixT = T(esq.enter_context(nc.sbuf_tensor("t_" + tag + "mixT", [128, 16, TQ], BF16)))

        stage(3)
        with contextlib.ExitStack() as es2:
            def sb2(name, shape, dt=F32):
                return T(es2.enter_context(nc.sbuf_tensor("t_" + tag + name, list(shape), dt)))
            RP = R + (1792 if is_prompt else 0)
            kT = [sb2("kT%d" % i, [128, RP], BF16) for i in range(2)]
            vT = [sb2("vT%d" % i, [128, RP], BF16) for i in range(1)]
            if RP > R:
                for t_ in kT + vT:
                    I("dve", lambda e: e.memset(t_.t[:, R:RP], 0.0), w=[t_.b])
            qT = [sb2("qT%d" % i, [128, TQ], BF16) for i in range(2)]
            nblk = [((R // d) + 127) // 128 for d in DILS]
            vc = [sb2("vc%d" % b, [128, DILS[b] * nblk[b], 128], BF16) for b in range(3)]
            pt = [sb2("pt%d" % i, [128, 256], BF16) for i in range(4)]
            rd = [sb2("rd%d" % i, [128, 512]) for i in range(2)]
            pk = 0
            sc = 1.0 / math.sqrt(128.0)
            for h in range(8):
                k_, v_, q_ = kT[h % 2], vT[0], qT[h % 2]
                I("sp", lambda e: e.dma_start(out=k_.t[:, :R], in_=KT[h]), r=[SKT], w=[k_.b], dma=True)
                I("sp", lambda e: e.dma_start(out=v_.t[:, :R], in_=VT[h]), r=[SVT], w=[v_.b], dma=True)
                I("sp", lambda e: e.dma_start(out=q_.t[:], in_=QT[h]), r=[SQT], w=[q_.b], dma=True)
                for b, dil in enumerate(DILS):
                    L = R // dil
                    items = [(r, kb) for r in range(dil) for kb in range(nblk[b])]
                    for i0 in range(0, len(items), 8):
                        grp = items[i0:i0 + 8]
                        pb = PB[(i0 // 8) % 2]
                        for n_, (r, kb) in enumerate(grp):
                            k0 = kb * 128
                            nk = 128
                            a0 = r + dil * k0
                            I("pe", lambda e: e.transpose(pb.t[:nk, n_ * 128:(n_ + 1) * 128], v_.t[:, a0:a0 + dil * (nk - 1) + 1:dil], ident.t[:]),
                              r=[v_.b, ident.b], w=[pb.b], inc=(n_ == len(grp) - 1))
                        for n_, (r, kb) in enumerate(grp):
                            nk = 128
                            idx = r * nblk[b] + kb
                            eng = "act" if (i0 // 8) % 2 else "dve"
                            if eng == "dve":
                                I("dve", lambda e: e.tensor_copy(out=vc[b].t[:nk, idx, :], in_=pb.t[:nk, n_ * 128:(n_ + 1) * 128]), r=[pb.b], w=[vc[b].b])
                            else:
                                I("act", lambda e: e.activation(out=vc[b].t[:nk, idx, :], in_=pb.t[:nk, n_ * 128:(n_ + 1) * 128], func=AF.Identity), r=[pb.b], w=[vc[b].b])
                nbank = (TQ + 511) // 512
                for qb in range(nbank):
                    b0 = qb * 512
                    Wq = min(512, TQ - b0)
                    num, den = PS[2 + (qb % 2) * 2], PS[3 + (qb % 2) * 2]
                    blocks = []
                    for b, dil in enumerate(DILS):
                        L = R // dil
                        for r in range(dil):
                            i0 = (r - (Q0 + b0)) % dil
                            nq = len(range(i0, Wq, dil))
                            if nq <= 0:
                                continue
                            lq0 = (Q0 + b0 + i0 - r) // dil
                            lo = max(0, lq0 - 64)
                            hi = min(L - 1, lq0 + nq - 1 + 64)
                            for kb in range(lo // 128, hi // 128 + 1):
                                k0 = kb * 128
                                nk = 128
                                qa = max(lq0, k0 - 64)
                                qe = min(lq0 + nq, k0 + min(128, L - k0) + 64)
                                n = qe - qa
                                if n <= 0:
                                    continue
                                blocks.append((b, dil, r, kb, k0, nk, qa, n))

                    def emit_S(i):
                        b, dil, r, kb, k0, nk, qa, n = blocks[i]
                        sp_ = PS[(pk + i) % 2]
                        p_ = pt[(pk + i) % 4]
                        ka = r + dil * k0
                        qc = r + dil * qa - Q0
                        I("pe", lambda e: e.matmul(sp_.t[:nk, :n], lhsT=k_.t[:, ka:ka + dil * (nk - 1) + 1:dil],
                                                   rhs=q_.t[:, qc:qc + dil * (n - 1) + 1:dil], start=True, stop=True),
                          r=[k_.b, q_.b], w=[sp_.b])
                        if is_prompt:
                            I("act", lambda e: e.activation(out=p_.t[:nk, :n], in_=sp_.t[:nk, :n], func=AF.Exp, scale=sc,
                                                            bias=kbias.t[:nk, b, r, kb:kb + 1]), r=[sp_.b, kbias.b], w=[p_.b])
                        else:
                            I("act", lambda e: e.activation(out=p_.t[:nk, :n], in_=sp_.t[:nk, :n], func=AF.Exp, scale=sc), r=[sp_.b], w=[p_.b])
                        off = qa - k0 + 64
                        meng = "dve" if i % 3 else "pool"
                        I(meng, lambda e: e.tensor_tensor(out=p_.t[:nk, :n], in0=p_.t[:nk, :n], in1=bmask.t[:nk, off:off + n], op=ALU.mult),
                          r=[p_.b, bmask.b], w=[p_.b])

                    def emit_PV(i):
                        b, dil, r, kb, k0, nk, qa, n = blocks[i]
                        p_ = pt[(pk + i) % 4]
                        c0 = r + dil * qa - Q0 - b0
                        idx = r * nblk[b] + kb
                        I("pe", lambda e: e.matmul(num.t[:, c0:c0 + dil * (n - 1) + 1:dil], lhsT=vc[b].t[:nk, idx, :], rhs=p_.t[:nk, :n],
                                                   start=(i == 0), stop=False, skip_group_check=True), r=[vc[b].b, p_.b], w=[num.b], inc=False)
                        I("pe", lambda e: e.matmul(den.t[:, c0:c0 + dil * (n - 1) + 1:dil], lhsT=ones.t[:nk, :], rhs=p_.t[:nk, :n],
                                                   start=(i == 0), stop=False, skip_group_check=True), r=[ones.b, p_.b], w=[den.b])

                    emit_S(0)
                    for i in range(len(blocks)):
                        if i + 1 < len(blocks):
                            emit_S(i + 1)
                        emit_PV(i)
                    pk += len(blocks)
                    rd_ = rd[qb % 2]
                    I("dve", lambda e: e.tensor_scalar(out=rd_.t[:, :Wq], in0=den.t[:, :Wq], scalar1=1e-30, scalar2=None, op0=ALU.add), r=[den.b], w=[rd_.b])
                    I("dve", lambda e: e.reciprocal(out=rd_.t[:, :Wq], in_=rd_.t[:, :Wq]), r=[rd_.b], w=[rd_.b])
                    I("dve", lambda e: e.tensor_tensor(out=mixT.t[:, h, b0:b0 + Wq], in0=num.t[:, :Wq], in1=rd_.t[:, :Wq], op=ALU.mult),
                      r=[num.b, rd_.b], w=[mixT.b])
            S.barrier()

        stage(4)
        with contextlib.ExitStack() as es2:
            def sb2(name, shape, dt=F32):
                return T(es2.enter_context(nc.sbuf_tensor("t_" + tag + name, list(shape), dt)))
            xrp = sb2("xrp", [128, TQ + 3])
            gt = sb2("gt", [128, TQ])
            u = sb2("u", [128, TQ])
            ub = sb2("ub", [128, TQ], BF16)
            rr = sb2("rr", [128, TQ])
            ii = sb2("ii", [128, TQ])
            aa = sb2("aa", [128, TQ])
            hh = [sb2("hh%d" % i, [128, TQ]) for i in range(2)]
            I("dve", lambda e: e.memset(xrp.t[:, 0:2], 0.0), w=[xrp.b])
            I("dve", lambda e: e.memset(xrp.t[:, TQ + 2:TQ + 3], 0.0), w=[xrp.b])
            for g in range(8):
                I("sp", lambda e: e.dma_start(out=xrp.t[:, 2:TQ + 2], in_=XRT[g]), r=[SXR], w=[xrp.b], dma=True)
                I("sp", lambda e: e.dma_start(out=gt.t[:], in_=GT[g]), r=[SGT], w=[gt.b], dma=True)
                lru_conv(xrp, u, ub, g, TQ, cast="act")
                for z in range(2):
                    lru_gates(u, ub, rr, ii, aa, z, g, TQ)
                    if is_prompt:
                        I("pool", lambda e: e.tensor_tensor(out=ii.t[:, 0:128], in0=ii.t[:, 0:128], in1=tmask.t[:, 0:128], op=ALU.mult), r=[ii.b, tmask.b], w=[ii.b])
                        I("pool", lambda e: e.tensor_tensor(out=ii.t[:, TQ - 128:TQ], in0=ii.t[:, TQ - 128:TQ], in1=tmask.t[:, 128:256], op=ALU.mult), r=[ii.b, tmask.b], w=[ii.b])
                    if z == 0:
                        lo, hi = (2, TQ) if is_prompt else (0, TQ)
                        init = HF.t[:, g:g + 1] if is_prompt else 0.0
                        if is_prompt:
                            I("dve", lambda e: e.memset(hh[0].t[:, 0:2], 0.0), w=[hh[0].b])
                        I("dve", lambda e: e.tensor_tensor_scan(out=hh[0].t[:, lo:hi], data0=aa.t[:, lo:hi], data1=ii.t[:, lo:hi], initial=init,
                                                                op0=ALU.mult, op1=ALU.add), r=[aa.b, ii.b] + ([HF.b] if is_prompt else []), w=[hh[0].b])
                    else:
                        lo, hi = (0, TQ - 2) if is_prompt else (0, TQ)
                        init = HB.t[:, g:g + 1] if is_prompt else 0.0
                        if is_prompt:
                            I("dve", lambda e: e.memset(hh[1].t[:, TQ - 2:TQ], 0.0), w=[hh[1].b])
                        I("dve", lambda e: e.tensor_tensor_scan(out=hh[1].t[:, lo:hi][:, ::-1], data0=aa.t[:, lo:hi][:, ::-1], data1=ii.t[:, lo:hi][:, ::-1],
                                                                initial=init, op0=ALU.mult, op1=ALU.add), r=[aa.b, ii.b] + ([HB.b] if is_prompt else []), w=[hh[1].b])
                I("pool", lambda e: e.tensor_tensor(out=hh[0].t[:], in0=hh[0].t[:], in1=hh[1].t[:], op=ALU.add), r=[hh[0].b, hh[1].b], w=[hh[0].b])
                I("act", lambda e: e.activation(out=rr.t[:], in_=gt.t[:], func=AF.Gelu), r=[gt.b], w=[rr.b])
                I("dve", lambda e: e.tensor_tensor(out=mixT.t[:, 8 + g, :TQ], in0=hh[0].t[:], in1=rr.t[:], op=ALU.mult), r=[hh[0].b, rr.b], w=[mixT.b])
            S.barrier()

        stage(5)
        with contextlib.ExitStack() as es2:
            def sb2(name, shape, dt=F32):
                return T(es2.enter_context(nc.sbuf_tensor("t_" + tag + name, list(shape), dt)))
            alloc_front(es2, full=False, hw=128)
            hsb, junk, hT = A["hsb"], A["junk"], A["hT"]
            sq = [sb2("sq%d" % i, [128, 256], BF16) for i in range(2)]
            rs = [sb2("rs%d" % i, [128, 256]) for i in range(2)]
            mm = sb2("mm", [128, 16, 256], BF16)
            wo = [sb2("wo%d" % i, [128, 16, 512], BF16) for i in range(2)]
            x1b = [sb2("x1b%d" % i, [128, D]) for i in range(2)]
            nbank = (TQ + 255) // 256
            xk = 0
            for tb in range(nbank):
                b0 = tb * 256
                W = min(256, TQ - b0)
                for part in range(2):
                    ssp = PS[part]
                    for c in range(8):
                        s_ = sq[c % 2]
                        I("dve", lambda e: e.tensor_tensor(out=s_.t[:, :W], in0=mixT.t[:, part * 8 + c, b0:b0 + W], in1=mixT.t[:, part * 8 + c, b0:b0 + W], op=ALU.mult),
                          r=[mixT.b], w=[s_.b])
                        I("pe", lambda e: e.matmul(ssp.t[:, :W], lhsT=ones.t[:, :], rhs=s_.t[:, :W], start=(c == 0), stop=(c == 7)), r=[ones.b, s_.b], w=[ssp.b])
                    r_ = rs[part]
                    I("dve", lambda e: e.tensor_scalar(out=r_.t[:, :W], in0=ssp.t[:, :W], scalar1=1.0 / 1024, scalar2=EPS, op0=ALU.mult, op1=ALU.add), r=[ssp.b], w=[r_.b])
                    I("act", lambda e: e.activation(out=r_.t[:, :W], in_=r_.t[:, :W], func=AF.Sqrt), r=[r_.b], w=[r_.b])
                    I("dve", lambda e: e.reciprocal(out=r_.t[:, :W], in_=r_.t[:, :W]), r=[r_.b], w=[r_.b])
                    for c in range(8):
                        I("dve", lambda e: e.tensor_tensor(out=mm.t[:, part * 8 + c, :W], in0=mixT.t[:, part * 8 + c, b0:b0 + W], in1=r_.t[:, :W], op=ALU.mult),
                          r=[mixT.b, r_.b], w=[mm.b])
                nt = W // 128
                for cg in range(4):
                    w_ = wo[cg % 2]
                    I("sp", lambda e: e.dma_start(out=w_.t[:], in_=WOUT[:, :, cg * 512:(cg + 1) * 512].rearrange("k p c -> p k c")), r=[WBo], w=[w_.b], dma=True)
                    for ti in range(nt):
                        ps = PS[2 + (ti % 2)]
                        if cg == 0:
                            tok = Q0 + b0 + ti * 128
                            I("sp", lambda e: e.dma_start(out=x1b[ti].t[:], in_=x_d[tok:tok + 128, :]), w=[x1b[ti].b], dma=True)
                        for kc in range(16):
                            I("pe", lambda e: e.matmul(ps.t[:, :], lhsT=mm.t[:, kc, ti * 128:(ti + 1) * 128], rhs=w_.t[:, kc, :], start=(kc == 0), stop=(kc == 15)),
                              r=[mm.b, w_.b], w=[ps.b], inc=(kc == 15))
                        I("dve", lambda e: e.tensor_tensor(out=x1b[ti].t[:, cg * 512:(cg + 1) * 512], in0=ps.t[:, :], in1=x1b[ti].t[:, cg * 512:(cg + 1) * 512], op=ALU.add),
                          r=[ps.b, x1b[ti].b], w=[x1b[ti].b])
                for ti in range(nt):
                    tl = b0 + ti * 128
                    x1 = x1b[ti]
                    I("pool", lambda e: e.dma_start(out=X1[tl:tl + 128, :], in_=x1.t[:]), r=[x1.b], w=[SX1], dma=True)
                    k = xk % 2
                    xk += 1
                    hs, s4 = hsb[k], st4[k]
                    I("act", lambda e: e.activation(out=junk.t[:], in_=x1.t[:], func=AF.Square, accum_out=s4.t[:, 0:1]), r=[x1.b], w=[junk.b, s4.b])
                    I("dve", lambda e: e.tensor_scalar(out=s4.t[:, 1:2], in0=s4.t[:, 0:1], scalar1=1.0 / D, scalar2=EPS, op0=ALU.mult, op1=ALU.add), r=[s4.b], w=[s4.b])
                    I("act", lambda e: e.activation(out=s4.t[:, 2:3], in_=s4.t[:, 1:2], func=AF.Sqrt), r=[s4.b], w=[s4.b])
                    I("dve", lambda e: e.reciprocal(out=s4.t[:, 3:4], in_=s4.t[:, 2:3]), r=[s4.b], w=[s4.b])
                    I("act", lambda e: e.activation(out=hs.t[:], in_=x1.t[:], func=AF.Identity, scale=s4.t[:, 3:4]), r=[x1.b, s4.b], w=[hs.b])
                    hTt = hT[k]
                    transpose16(hs, hTt, 0)
                    I("pool", lambda e: e.dma_start(out=H2[:, :, tl:tl + 128], in_=hTt.t[:, :, 0:128]), r=[hTt.b], w=[SH2], dma=True)
                    for tok_, idx_ in bnd.items():
                        if tl <= tok_ < tl + 128:
                            cc_ = tok_ - tl
                            edge_ = is_prompt and idx_ in (0, 7)
                            if edge_:
                                I("dve", lambda e: e.tensor_scalar(out=hbd.t[:, :, idx_:idx_ + 1], in0=hTt.t[:, :, cc_:cc_ + 1], scalar1=vmask.t[:, (0 if idx_ == 0 else 1):(1 if idx_ == 0 else 2)], scalar2=None, op0=ALU.mult),
                                  r=[hTt.b, vmask.b], w=[hbd.b])
                            else:
                                I("dve", lambda e: e.tensor_copy(out=hbd.t[:, :, idx_:idx_ + 1], in_=hTt.t[:, :, cc_:cc_ + 1]), r=[hTt.b], w=[hbd.b])
            S.barrier()
        esq.close()

        stage(6)
        with contextlib.ExitStack() as es2:
            def sb2(name, shape, dt=F32):
                return T(es2.enter_context(nc.sbuf_tensor("t_" + tag + name, list(shape), dt)))
            h2 = [sb2("h2_%d" % i, [128, 16, 514], BF16) for i in range(2)]
            actT = sb2("actT", [128, 48, 512], BF16)
            wd = [sb2("wd%d" % i, [128, 12, 512], BF16) for i in range(2)]
            A["wch"] = [sb2("wch%d" % i, [128, 16, 128], BF16) for i in range(4)]
            wch = A["wch"]
            U = [sb2("U%d" % i, [128, 514]) for i in range(2)]
            cv = [sb2("cv%d" % i, [128, 512]) for i in range(2)]
            gl = sb2("gl", [128, 512])
            x1p = [sb2("x1p%d" % i, [128, 512]) for i in range(2)]
            x2p = [sb2("x2p%d" % i, [128, 512]) for i in range(2)]
            UB = sb2("UB", [128, 96, 8])
            ssq = sb2("ssq", [128, 4, 4])
            s3 = sb2("s3", [128, 4])
            x2r = [sb2("x2r%d" % i, [128, D]) for i in range(1)]
            yt = [sb2("yt%d" % i, [128, D]) for i in range(1)]
            uk = 0
            dk = 0
            for gi in range(4):
                t0 = M0 + gi * 512
                h2t = h2[gi % 2]
                lo = t0 - 1
                hi = t0 + 513
                left_ok = lo >= 0
                right_ok = hi <= TQ
                c_lo = 0 if left_ok else 1
                c_hi = 514 if right_ok else 513
                I("sp", lambda e: e.dma_start(out=h2t.t[:, :, c_lo:c_hi], in_=H2[:, :, lo + c_lo:lo + c_hi]), r=[SH2], w=[h2t.b], dma=True)
                for jj in range(48):
                    for half in range(2):
                        j = jj + 48 * half
                        ps = PS[uk % 2]
                        pbd = PS[2]
                        U_ = U[uk % 2]
                        c_ = cv[half]
                        uk += 1
                        wt = wch[wk[0] % 4]
                        wk[0] += 1
                        I("sp", lambda e: e.dma_start(out=wt.t[:], in_=WUP[j]), r=[WBu], w=[wt.b], dma=True)
                        for kc in range(16):
                            I("pe", lambda e: e.matmul(ps.t[:, :], lhsT=wt.t[:, kc, :], rhs=h2t.t[:, kc, 1:513], start=(kc == 0), stop=(kc == 15)),
                              r=[wt.b, h2t.b], w=[ps.b], inc=(kc == 15))
                        I("act", lambda e: e.activation(out=U_.t[:, 1:513], in_=ps.t[:, :], func=AF.Identity), r=[ps.b], w=[U_.b])
                        if gi == 0:
                            for kc in range(16):
                                I("pe", lambda e: e.matmul(pbd.t[:, 0:8], lhsT=wt.t[:, kc, :], rhs=hbd.t[:, kc, :], start=(kc == 0), stop=(kc == 15)),
                                  r=[wt.b, hbd.b], w=[pbd.b], inc=(kc == 15))
                            I("dve", lambda e: e.tensor_copy(out=UB.t[:, j, :], in_=pbd.t[:, 0:8]), r=[pbd.b], w=[UB.b])
                        I("pool", lambda e: e.tensor_copy(out=U_.t[:, 0:1], in_=UB.t[:, j, 2 * gi:2 * gi + 1]), r=[UB.b], w=[U_.b])
                        I("pool", lambda e: e.tensor_copy(out=U_.t[:, 513:514], in_=UB.t[:, j, 2 * gi + 1:2 * gi + 2]), r=[UB.b], w=[U_.b])
                        I("dve", lambda e: e.tensor_scalar(out=c_.t[:], in0=U_.t[:, 0:512], scalar1=cfw.t[:, j, 0:1], scalar2=cfb.t[:, j:j + 1], op0=ALU.mult, op1=ALU.add),
                          r=[U_.b, cfw.b, cfb.b], w=[c_.b])
                        I("dve", lambda e: e.scalar_tensor_tensor(out=c_.t[:], in0=U_.t[:, 1:513], scalar=cfw.t[:, j, 1:2], in1=c_.t[:], op0=ALU.mult, op1=ALU.add),
                          r=[U_.b, cfw.b, c_.b], w=[c_.b])
                        I("dve", lambda e: e.scalar_tensor_tensor(out=c_.t[:], in0=U_.t[:, 2:514], scalar=cfw.t[:, j, 2:3], in1=c_.t[:], op0=ALU.mult, op1=ALU.add),
                          r=[U_.b, cfw.b, c_.b], w=[c_.b])
                    I("act", lambda e: e.activation(out=gl.t[:], in_=cv[0].t[:], func=AF.Gelu), r=[cv[0].b], w=[gl.b])
                    I("pool", lambda e: e.tensor_tensor(out=actT.t[:, jj, :], in0=gl.t[:], in1=cv[1].t[:], op=ALU.mult), r=[gl.b, cv[1].b], w=[actT.b])
                for cg in range(4):
                    for jb in range(4):
                        w_ = wd[dk % 2]
                        dk += 1
                        I("sp", lambda e: e.dma_start(out=w_.t[:], in_=WDN[jb * 12:(jb + 1) * 12, :, cg * 512:(cg + 1) * 512].rearrange("k p c -> p k c")), r=[WBd], w=[w_.b], dma=True)
                        for ti in range(4):
                            ps = PS[2 + ti]
                            for j2 in range(12):
                                jx = jb * 12 + j2
                                I("pe", lambda e: e.matmul(ps.t[:, :], lhsT=actT.t[:, jx, ti * 128:(ti + 1) * 128], rhs=w_.t[:, j2, :], start=(jx == 0), stop=(jx == 47)),
                                  r=[actT.b, w_.b], w=[ps.b], inc=(j2 == 11))
                    for ti in range(4):
                        ps = PS[2 + ti]
                        tl = t0 + ti * 128
                        to = gi * 512 + ti * 128
                        a_, b_ = x1p[ti % 2], x2p[ti % 2]
                        I("sp", lambda e: e.dma_start(out=a_.t[:], in_=X1[tl:tl + 128, cg * 512:(cg + 1) * 512]), r=[SX1], w=[a_.b], dma=True)
                        I("dve", lambda e: e.tensor_tensor(out=b_.t[:], in0=ps.t[:, :], in1=a_.t[:], op=ALU.add), r=[ps.b, a_.b], w=[b_.b])
                        I("act", lambda e: e.activation(out=a_.t[:], in_=b_.t[:], func=AF.Square, accum_out=ssq.t[:, ti, cg:cg + 1]), r=[b_.b], w=[a_.b, ssq.b])
                        I("pool", lambda e: e.dma_start(out=X2[to:to + 128, cg * 512:(cg + 1) * 512], in_=b_.t[:]), r=[b_.b], w=[SX2], dma=True)
                for ti in range(4):
                    to = gi * 512 + ti * 128
                    xr2, y_ = x2r[0], yt[0]
                    I("dve", lambda e: e.reduce_sum(out=s3.t[:, 0:1], in_=ssq.t[:, ti, :], axis=AX.X), r=[ssq.b], w=[s3.b])
                    I("dve", lambda e: e.tensor_scalar(out=s3.t[:, 1:2], in0=s3.t[:, 0:1], scalar1=1.0 / D, scalar2=EPS, op0=ALU.mult, op1=ALU.add), r=[s3.b], w=[s3.b])
                    I("act", lambda e: e.activation(out=s3.t[:, 2:3], in_=s3.t[:, 1:2], func=AF.Sqrt), r=[s3.b], w=[s3.b])
                    I("dve", lambda e: e.reciprocal(out=s3.t[:, 3:4], in_=s3.t[:, 2:3]), r=[s3.b], w=[s3.b])
                    I("sp", lambda e: e.dma_start(out=xr2.t[:], in_=X2[to:to + 128, :]), r=[SX2], w=[xr2.b], dma=True)
                    I("dve", lambda e: e.scalar_tensor_tensor(out=y_.t[:], in0=xr2.t[:], scalar=s3.t[:, 3:4], in1=gfin.t[:], op0=ALU.mult, op1=ALU.mult),
                      r=[xr2.b, s3.b, gfin.b], w=[y_.b])
                    I("pool", lambda e: e.dma_start(out=y_d[to:to + 128, :], in_=y_.t[:]), r=[y_.b], w=[YOUT], dma=True)
            S.barrier()

    def lru_conv(xrp, u, ub, g, n, cast="pool"):
        I("dve", lambda e: e.tensor_scalar(out=u.t[:, :n], in0=xrp.t[:, 0:n], scalar1=crw.t[:, g, 0:1], scalar2=crb.t[:, g:g + 1], op0=ALU.mult, op1=ALU.add),
          r=[xrp.b, crw.b, crb.b], w=[u.b])
        for j in range(1, 4):
            I("dve", lambda e: e.scalar_tensor_tensor(out=u.t[:, :n], in0=xrp.t[:, j:j + n], scalar=crw.t[:, g, j:j + 1], in1=u.t[:, :n], op0=ALU.mult, op1=ALU.add),
              r=[xrp.b, crw.b, u.b], w=[u.b])
        if cast == "dve":
            I("dve", lambda e: e.tensor_copy(out=ub.t[:, :n], in_=u.t[:, :n]), r=[u.b], w=[ub.b])
        elif cast == "act":
            I("act", lambda e: e.activation(out=ub.t[:, :n], in_=u.t[:, :n], func=AF.Identity), r=[u.b], w=[ub.b])
        elif cast == "pool":
            I("pool", lambda e: e.tensor_copy(out=ub.t[:, :n], in_=u.t[:, :n]), r=[u.b], w=[ub.b])

    def lru_gates_a(u, ub, rr, ii, aa, z, g, n):
        k = 0
        for gi, (dst, bias) in enumerate(((rr, hba), (ii, hbx))):
            for c0 in range(0, n, 512):
                W = min(512, n - c0)
                ps = PS[k % 2]
                k += 1
                I("pe", lambda e: e.matmul(ps.t[:, :W], lhsT=wg.t[:, gi * 16 + z * 8 + g, :], rhs=ub.t[:, c0:c0 + W], start=True, stop=True), r=[wg.b, ub.b], w=[ps.b])
                I("act", lambda e: e.activation(out=dst.t[:, c0:c0 + W], in_=ps.t[:, :W], func=AF.Tanh, scale=0.5, bias=bias.t[:, z, g:g + 1]), r=[ps.b, bias.b], w=[dst.b])

    def lru_gates_b1(u, ub, rr, ii, aa, z, g, n):
        I("act", lambda e: e.activation(out=aa.t[:, :n], in_=rr.t[:, :n], func=AF.Exp, scale=hcl.t[:, z, g:g + 1], bias=hcl.t[:, z, g:g + 1]), r=[rr.b, hcl.b], w=[aa.b])
        I("act", lambda e: e.activation(out=rr.t[:, :n], in_=rr.t[:, :n], func=AF.Exp, scale=cl.t[:, z, g:g + 1], bias=cl.t[:, z, g:g + 1]), r=[rr.b, cl.b], w=[rr.b])

    def lru_gates_b2(u, ub, rr, ii, aa, z, g, n):
        I("act", lambda e: e.activation(out=rr.t[:, :n], in_=rr.t[:, :n], func=AF.Sqrt, scale=-0.25, bias=0.25), r=[rr.b], w=[rr.b])

    def lru_gates_b3(u, ub, rr, ii, aa, z, g, n, split=False):
        I("dve", lambda e: e.scalar_tensor_tensor(out=ii.t[:, :n], in0=ii.t[:, :n], scalar=1.0, in1=u.t[:, :n], op0=ALU.add, op1=ALU.mult), r=[ii.b, u.b], w=[ii.b])
        I("pool", lambda e: e.tensor_tensor(out=ii.t[:, :n], in0=ii.t[:, :n], in1=rr.t[:, :n], op=ALU.mult), r=[ii.b, rr.b], w=[ii.b])

    def lru_gates_b(u, ub, rr, ii, aa, z, g, n, split=False):
        lru_gates_b1(u, ub, rr, ii, aa, z, g, n)
        lru_gates_b2(u, ub, rr, ii, aa, z, g, n)
        lru_gates_b3(u, ub, rr, ii, aa, z, g, n)

    def lru_gates(u, ub, rr, ii, aa, z, g, n):
        lru_gates_a(u, ub, rr, ii, aa, z, g, n)
        lru_gates_b(u, ub, rr, ii, aa, z, g, n)

    YOUT = Buf()

    def prepass():
        SXF = Buf()
        esA = contextlib.ExitStack()
        alloc_front(esA)
        zf = A["zf"]
        DVE_EVAC[0] = True
        wres = T(esA.enter_context(nc.sbuf_tensor("t_wres", [128, 8, 16, 128], BF16)))
        I("sp", lambda e: e.dma_start(out=wres.t[:], in_=WIN[24:32].rearrange("j p k c -> p j k c")), r=[WBi], w=[wres.b], dma=True)
        def front_fn(st):
            s0 = st * 512
            hTt = A["hT"][st % 2]
            return [(xf_d[s0 + ti * 128:s0 + (ti + 1) * 128, :], hTt, ti * 128) for ti in range(4)]

        def proj_fn(st):
            s0 = st * 512
            hTt = A["hT"][st % 2]

            def body(g):
                ps = PS[zk[0] % 2]
                z = zf[zk[0] % 3]
                zk[0] += 1
                for kc in range(16):
                    I("pe", lambda e: e.matmul(ps.t[:, :], lhsT=wres.t[:, g, kc, :], rhs=hTt.t[:, kc, :], start=(kc == 0), stop=(kc == 15)),
                      r=[wres.b, hTt.b], w=[ps.b], inc=(kc == 15))
                I("dve", lambda e: e.tensor_copy(out=z.t[:, :], in_=ps.t[:, :]), r=[ps.b], w=[z.b])
                I("pool", lambda e: e.dma_start(out=XRF[g, :, s0:s0 + 512], in_=z.t[:, :]), r=[z.b], w=[SXF], dma=True)
            return [(lambda g=g: body(g)) for g in range(8)]

        pipelined_steps(SEQ_P // 512, front_fn, proj_fn)
        DVE_EVAC[0] = False
        S.barrier()
        esA.close()
        with contextlib.ExitStack() as es2:
            def sb2(name, shape, dt=F32):
                return T(es2.enter_context(nc.sbuf_tensor("t_pp" + name, list(shape), dt)))
            n = 1024
            NSG = SEQ_P // n
            NB = 4
            UF = dscr("UF", [8, 128, SEQ_P], F32)
            SUF = Buf()
            xrp = [sb2("xrp%d" % i, [128, n + 3]) for i in range(NB)]
            u = [sb2("u%d" % i, [128, n]) for i in range(NB)]
            ub = [sb2("ub%d" % i, [128, n], BF16) for i in range(NB)]
            rr = [sb2("rr%d" % i, [128, n]) for i in range(NB)]
            ii = [sb2("ii%d" % i, [128, n]) for i in range(NB)]
            aa = [sb2("aa%d" % i, [128, n]) for i in range(NB)]
            hh = [sb2("hh%d" % i, [128, n]) for i in range(NB)]
            rec = sb2("rec", [128, 2, 8, 8]); tm = sb2("tm", [128, 8])
            I("dve", lambda e: e.memset(rec.t[:], 0.0), w=[rec.b])
            work = []
            for g in range(8):
                for z in range(2):
                    segs = list(range((2048 * 7 - 127) // n + 1)) if z == 0 else list(range(NSG - 1, (2048 + 126) // n - 1, -1))
                    for si, s in enumerate(segs):
                        work.append((g, z, si, s, si == len(segs) - 1))

            def stage_a(it, part):
                g, z, si, s, last_ = work[it]
                k = it % NB
                x_, u_, ub_, rr_, ii_, aa_ = xrp[k], u[k], ub[k], rr[k], ii[k], aa[k]
                if part == 2:
                    lru_gates_a(u_, ub_, rr_, ii_, aa_, z, g, n)
                    return
                if z == 0 or s > (2048 * 7 - 127) // n:
                    a0 = s * n - 2
                    c_lo = 2 if s == 0 else 0
                    c_hi = n + 2 if s == NSG - 1 else n + 3
                    if s == 0:
                        I("dve", lambda e: e.memset(x_.t[:, 0:2], 0.0), w=[x_.b])
                    if s == NSG - 1:
                        I("dve", lambda e: e.memset(x_.t[:, n + 2:n + 3], 0.0), w=[x_.b])
                    I("sp", lambda e: e.dma_start(out=x_.t[:, c_lo:c_hi], in_=XRF[g, :, a0 + c_lo:a0 + c_hi]), r=[SXF], w=[x_.b], dma=True)
                    lru_conv(x_, u_, ub_, g, n, cast="dve")
                    if z == 0:
                        I("pool", lambda e: e.dma_start(out=UF[g, :, s * n:(s + 1) * n], in_=u_.t[:]), r=[u_.b], w=[SUF], dma=True)
                else:
                    I("sp", lambda e: e.dma_start(out=u_.t[:], in_=UF[g, :, s * n:(s + 1) * n]), r=[SUF], w=[u_.b], dma=True)
                    I("dve", lambda e: e.tensor_copy(out=ub_.t[:], in_=u_.t[:]), r=[u_.b], w=[ub_.b])

            def stage_b(it, ph):
                g, z, si, s, last_ = work[it]
                k = it % NB
                u_, ub_, rr_, ii_, aa_, hh_, hp = u[k], ub[k], rr[k], ii[k], aa[k], hh[k], hh[(it - 1) % NB]
                if ph == 1:
                    lru_gates_b1(u_, ub_, rr_, ii_, aa_, z, g, n)
                    return
                if ph == 2:
                    lru_gates_b2(u_, ub_, rr_, ii_, aa_, z, g, n)
                    return
                lru_gates_b3(u_, ub_, rr_, ii_, aa_, z, g, n)
                if z == 0:
                    init = 0.0 if si == 0 else hp.t[:, n - 1:n]
                    I("dve", lambda e: e.tensor_tensor_scan(out=hh_.t[:], data0=aa_.t[:], data1=ii_.t[:], initial=init, op0=ALU.mult, op1=ALU.add),
                      r=[aa_.b, ii_.b, hp.b], w=[hh_.b])
                    for j_ in range(1, 8):
                        tk_ = 2048 * j_ - 127
                        if tk_ // n == s:
                            I("dve", lambda e: e.tensor_copy(out=rec.t[:, 0, g, j_:j_ + 1], in_=hh_.t[:, tk_ % n:tk_ % n + 1]), r=[hh_.b], w=[rec.b])
                else:
                    init = 0.0 if si == 0 else hp.t[:, 0:1]
                    I("dve", lambda e: e.tensor_tensor_scan(out=hh_.t[:, ::-1], data0=aa_.t[:, ::-1], data1=ii_.t[:, ::-1], initial=init, op0=ALU.mult, op1=ALU.add),
                      r=[aa_.b, ii_.b, hp.b], w=[hh_.b])
                    for j_ in range(0, 7):
                        tk_ = 2048 * (j_ + 1) + 126
                        if tk_ // n == s:
                            I("dve", lambda e: e.tensor_copy(out=rec.t[:, 1, g, j_:j_ + 1], in_=hh_.t[:, tk_ % n:tk_ % n + 1]), r=[hh_.b], w=[rec.b])
                if last_:
                    Hx = HF if z == 0 else HB
                    I("dve", lambda e: e.tensor_tensor(out=tm.t[:], in0=rec.t[:, z, g, :], in1=sel.t[:, z, :], op=ALU.mult), r=[rec.b, sel.b], w=[tm.b])
                    I("dve", lambda e: e.reduce_sum(out=Hx.t[:, g:g + 1], in_=tm.t[:], axis=AX.X), r=[tm.b], w=[Hx.b])

            LA = 2
            for it in range(min(LA, len(work))):
                stage_a(it, 1)
                stage_a(it, 2)
            for p in range(0, len(work), 2):
                pair = [it for it in (p, p + 1) if it < len(work)]
                nxt = [it + LA for it in pair if it + LA < len(work)]
                for it in nxt:
                    stage_a(it, 1)
                for ph in (1, 2, 3):
                    for it in pair:
                        stage_b(it, ph)
                for it in nxt:
                    stage_a(it, 2)
            S.barrier()

    try:
        run_seq("s", xs_d, CH, 0, CH, 0, cos_s_d, sin_s_d, ys_d, False, None, None)
        if prompt_on:
            if prepass_on:
                prepass()
            else:
                I("dve", lambda e: e.memset(HF.t[:], 0.0), w=[HF.b])
                I("dve", lambda e: e.memset(HB.t[:], 0.0), w=[HB.b])
            run_seq("p", xp_d, 4352, 1024, 2304, 128, cos_p_d, sin_p_d, yp_d, True, HF, HB)
    except _Stop:
        pass
    S.finish()
    return nc, es


def _rope_tables(pos):
    half = 16
    inv = (500000.0 ** (-np.arange(half, dtype=np.float32) / half)).astype(np.float32)
    ang = pos.astype(np.float32)[None, :] * inv[:, None]
    c = np.cos(ang).astype(np.float32)
    s = np.sin(ang).astype(np.float32)
    return np.concatenate([c, c], 0), np.concatenate([s, s], 0)


def _chunk16(v):
    return np.ascontiguousarray(v.reshape(-1, 128).T)


PROMPT_ON = True
PREPASS_ON = True


def kernel(x_prompt, x_sample, g_mix, w_in, w_out, g_attn_out, g_lru_out, conv_rg_w, conv_rg_b,
           rg_w_a, rg_b_a, rg_w_x, rg_b_x, rg_lam, g_mlp, w_up, conv_ff_w, conv_ff_b, w_down, g_final):
    f = np.float32
    xpf = np.ascontiguousarray(x_prompt[0], dtype=f)
    common = {
        "xf": xpf,
        "w_in": np.ascontiguousarray(w_in[0], f), "w_out": np.ascontiguousarray(w_out[0], f),
        "w_up": np.ascontiguousarray(w_up[0], f), "w_down": np.ascontiguousarray(w_down[0], f),
        "g_mix": _chunk16(g_mix[0]), "g_mlp": _chunk16(g_mlp[0]),
        "g_mo": _chunk16(np.concatenate([g_attn_out[0], g_lru_out[0]])),
        "crw": np.ascontiguousarray(conv_rg_w[0].reshape(4, 8, 128).transpose(2, 1, 0)),
        "crb": _chunk16(conv_rg_b[0]),
        "rg_w_a": np.ascontiguousarray(rg_w_a[0], f), "rg_w_x": np.ascontiguousarray(rg_w_x[0], f),
        "rg_b_a": np.ascontiguousarray(rg_b_a[0].reshape(2, 8, 128).transpose(2, 0, 1)),
        "rg_b_x": np.ascontiguousarray(rg_b_x[0].reshape(2, 8, 128).transpose(2, 0, 1)),
        "rg_lam": np.ascontiguousarray(rg_lam[0].reshape(2, 8, 128).transpose(2, 0, 1)),
        "cfw": np.ascontiguousarray(conv_ff_w[0].reshape(3, 96, 128).transpose(2, 1, 0)),
        "cfb": _chunk16(conv_ff_b[0]),
        "gfin": np.ascontiguousarray(np.broadcast_to(g_final[None, :], (128, D)), f),
    }
    cs, sn = _rope_tables(np.arange(CH))
    common["cos_s"], common["sin_s"] = cs, sn
    pmm = np.zeros((128, 128), f)
    for m in range(16):
        pmm[m + 16, m] = -1.0
        pmm[m, m + 16] = 1.0
    common["pm"] = pmm
    jj = np.arange(128)[:, None]
    cc = np.arange(512)[None, :]
    common["bmask"] = ((cc - jj >= 0) & (cc - jj <= 128)).astype(f)
    common["ident"] = np.eye(128, dtype=f)
    if not (PROMPT_ON and PREPASS_ON):
        common["xf"] = xpf[:128]
    in_maps = []
    for c in range(NCORE):
        m = dict(common)
        m["xs"] = np.ascontiguousarray(x_sample[c], f)
        a0 = c * CH - 1152
        pos = np.arange(a0, a0 + 4352)
        valid = (pos >= 0) & (pos < SEQ_P)
        xp = np.zeros((4352, D), f)
        xp[valid] = xpf[pos[valid]]
        m["xp"] = xp
        cp, sp_ = _rope_tables(np.where(valid, pos, 0))
        m["cos_p"], m["sin_p"] = cp, sp_
        kb = np.zeros((128, 3, 16, 34), f)
        for b, dil in enumerate(DILS):
            L = 4352 // dil
            for r in range(dil):
                for kbi in range((L + 127) // 128):
                    l = kbi * 128 + np.arange(128)
                    t = r + dil * l
                    ok = (l < L) & valid[np.minimum(t, 4351)]
                    kb[:, b, r, kbi] = np.where(ok, 0.0, -30000.0)
        m["kbias"] = kb
        sel = np.zeros((128, 2, 8), f)
        if c > 0:
            sel[:, 0, c] = 1.0
        if c < 7:
            sel[:, 1, c] = 1.0
        m["sel"] = sel
        vm = np.ones((128, 2), f)
        if c == 0:
            vm[:, 0] = 0.0
        if c == 7:
            vm[:, 1] = 0.0
        m["vmask"] = vm
        tm = np.ones((128, 256), f)
        tm[:, 0:128] = ((c * CH - 128 + np.arange(128)) >= 0).astype(f)[None, :]
        tm[:, 128:256] = ((c * CH + CH + np.arange(128)) < SEQ_P).astype(f)[None, :]
        m["tmask"] = tm
        in_maps.append(m)
    nc, es = build(PROMPT_ON, PREPASS_ON)
    res = run_bass_kernel_spmd(nc, in_maps, core_ids=list(range(NCORE)))
    try:
        es.close()
    except AssertionError:
        pass
    if DEBUG["scratch_out"]:
        DEBUG["res"] = res
        DEBUG["in_maps"] = in_maps
    yp = np.concatenate([np.asarray(res.results[c]["yp"], f) for c in range(NCORE)], 0)[None]
    ys = np.stack([np.asarray(res.results[c]["ys"], f) for c in range(NCORE)], 0)
    return (yp, ys)
```
